# Optimizing a Trainium2 kernel written in Bass

```python
import math
import jax, jax.numpy as jnp
from jax import lax
import numpy as np

D_MODEL = 1024
BATCH = 8
SEQ = 2048
DEPTH = 4
DEC_BATCH = 128
DEC_SEQ = 1
PAST_LEN = 16384
PAGE_SIZE = 128

N_CONV_LAYERS = (DEPTH + 1) // 2
N_SSM_LAYERS = DEPTH // 2
CONV_WIDTH = 31
EXPAND = 2
D_INNER = EXPAND * D_MODEL
HEAD_DIM_SSM = 64
N_HEADS_SSM = D_INNER // HEAD_DIM_SSM
N_GROUPS_SSM = 8
HEADS_PER_GROUP = N_HEADS_SSM // N_GROUPS_SSM
D_STATE = 128
SSM_CONV_WIDTH = 4
SSM_CONV_DIM = D_INNER + 2 * N_GROUPS_SSM * D_STATE
SSM_IN_DIM = D_INNER + SSM_CONV_DIM + N_HEADS_SSM
CHUNK = 128
D_FF = ((8 * D_MODEL // 3 + 255) // 256) * 256
PLE_DIM = 256
EPS = 1e-6

kernel_name = "hybrid_conformer_mamba2_decode_step"


def rmsnorm(x, g):
    xf = x.astype(jnp.float32)
    y = xf * lax.rsqrt(jnp.mean(xf * xf, axis=-1, keepdims=True) + EPS)
    return (y * g.astype(jnp.float32)).astype(x.dtype)


def layernorm(x, g, b):
    xf = x.astype(jnp.float32)
    mu = jnp.mean(xf, axis=-1, keepdims=True)
    var = jnp.mean(jnp.square(xf - mu), axis=-1, keepdims=True)
    y = (xf - mu) * lax.rsqrt(var + EPS) * g.astype(jnp.float32) + b.astype(jnp.float32)
    return y.astype(x.dtype)


def causal_dwconv(x, prefix, w, b):
    xp = jnp.concatenate([prefix.astype(x.dtype), x], axis=1)
    y = lax.conv_general_dilated(
        xp, w[:, None, :].astype(x.dtype), window_strides=(1,), padding='VALID',
        dimension_numbers=('NWC', 'WIO', 'NWC'), feature_group_count=x.shape[-1])
    return y + b, xp[:, -(w.shape[0] - 1):]


def conformer_mixer(u, buf, w_pw1, b_pw1, w_dw, b_dw, ln_g, ln_b, w_pw2, b_pw2):
    a = u @ w_pw1 + b_pw1
    v = jax.nn.glu(a, axis=-1)
    c, new_buf = causal_dwconv(v, buf, w_dw, b_dw)
    c = jax.nn.silu(layernorm(c, ln_g, ln_b))
    return c @ w_pw2 + b_pw2, new_buf


def ssd(x, dt, A, B, C, s0):
    b, l, g, r, p = x.shape
    n = B.shape[-1]
    q = CHUNK if l % CHUNK == 0 else l
    nc = l // q
    xdt = (x.astype(jnp.float32) * dt[..., None]).reshape(b, nc, q, g, r, p)
    Bc = B.astype(jnp.float32).reshape(b, nc, q, g, n)
    Cc = C.astype(jnp.float32).reshape(b, nc, q, g, n)
    a = jnp.moveaxis((dt * A).reshape(b, nc, q, g, r), 2, -1)
    a_cum = jnp.cumsum(a, axis=-1)
    seg = a_cum[..., :, None] - a_cum[..., None, :]
    causal = jnp.tril(jnp.ones((q, q), dtype=bool))
    Lmat = jnp.exp(jnp.where(causal, seg, -jnp.inf))
    CB = jnp.einsum('bcign,bcjgn->bcgij', Cc, Bc)
    y_diag = jnp.einsum('bcgrij,bcjgrp->bcigrp', CB[:, :, :, None] * Lmat, xdt)
    decay = jnp.exp(a_cum[..., -1:] - a_cum)
    states = jnp.einsum('bcjgn,bcgrj,bcjgrp->bcgrpn', Bc, decay, xdt)
    chunk_decay = jnp.exp(a_cum[..., -1])

    def step(s, inp):
        st, dec = inp
        return dec[..., None, None] * s + st, s

    final, prev = lax.scan(step, s0.astype(jnp.float32),
                           (jnp.moveaxis(states, 1, 0), jnp.moveaxis(chunk_decay, 1, 0)))
    prev = jnp.moveaxis(prev, 0, 1)
    y_off = jnp.einsum('bcign,bcgrpn,bcgri->bcigrp', Cc, prev, jnp.exp(a_cum))
    return (y_diag + y_off).reshape(b, l, g, r, p), final


def mamba2_mixer(u, ssm_state, conv_buf, w_in, w_conv, b_conv, dt_bias, a_log, d_skip, norm_g, w_out):
    b, l, _ = u.shape
    zxbcdt = u @ w_in
    z = zxbcdt[..., :D_INNER]
    xbc = zxbcdt[..., D_INNER:D_INNER + SSM_CONV_DIM]
    dt = zxbcdt[..., D_INNER + SSM_CONV_DIM:]
    xbc, new_conv = causal_dwconv(xbc, conv_buf, w_conv, b_conv)
    xbc = jax.nn.silu(xbc)
    gn = N_GROUPS_SSM * D_STATE
    xs = xbc[..., :D_INNER].reshape(b, l, N_GROUPS_SSM, HEADS_PER_GROUP, HEAD_DIM_SSM)
    Bm = xbc[..., D_INNER:D_INNER + gn].reshape(b, l, N_GROUPS_SSM, D_STATE)
    Cm = xbc[..., D_INNER + gn:].reshape(b, l, N_GROUPS_SSM, D_STATE)
    dt = jax.nn.softplus(dt.astype(jnp.float32) + dt_bias.astype(jnp.float32))
    dt = dt.reshape(b, l, N_GROUPS_SSM, HEADS_PER_GROUP)
    A = -jnp.exp(a_log.astype(jnp.float32)).reshape(N_GROUPS_SSM, HEADS_PER_GROUP)
    s0 = ssm_state.reshape(b, N_GROUPS_SSM, HEADS_PER_GROUP, HEAD_DIM_SSM, D_STATE)
    y, final = ssd(xs, dt, A, Bm, Cm, s0)
    y = y + d_skip.astype(jnp.float32).reshape(N_GROUPS_SSM, HEADS_PER_GROUP)[:, :, None] * xs.astype(jnp.float32)
    yg = y.reshape(b, l, D_INNER) * jax.nn.silu(z.astype(jnp.float32))
    yg = yg.reshape(b, l, N_GROUPS_SSM, D_INNER // N_GROUPS_SSM)
    yg = yg * lax.rsqrt(jnp.mean(yg * yg, axis=-1, keepdims=True) + EPS)
    yg = yg.reshape(b, l, D_INNER) * norm_g.astype(jnp.float32)
    out = yg.astype(u.dtype) @ w_out
    new_state = final.reshape(b, N_HEADS_SSM, HEAD_DIM_SSM, D_STATE)
    return out, new_state, new_conv


def swiglu(u, w_gate, w_up, w_down):
    return (jax.nn.silu(u @ w_gate) * (u @ w_up)) @ w_down


def trunk(x, p, conv_bufs, ssm_states, ssm_bufs,
          g_mix, g_ffn, g_ple, g_final,
          cv_w_pw1, cv_b_pw1, cv_w_dw, cv_b_dw, cv_ln_g, cv_ln_b, cv_w_pw2, cv_b_pw2,
          ssm_w_in, ssm_w_conv, ssm_b_conv, ssm_dt_bias, ssm_a_log, ssm_d, ssm_norm_g, ssm_w_out,
          ffn_w_gate, ffn_w_up, ffn_w_down, ple_w_proj, ple_w_gate):
    h = x
    new_conv, new_ssm, new_ssm_conv = [], [], []
    for i in range(DEPTH):
        u = rmsnorm(h, g_mix[i])
        j = i // 2
        if i % 2 == 0:
            o, nb = conformer_mixer(u, conv_bufs[j], cv_w_pw1[j], cv_b_pw1[j], cv_w_dw[j], cv_b_dw[j],
                                    cv_ln_g[j], cv_ln_b[j], cv_w_pw2[j], cv_b_pw2[j])
            new_conv.append(nb)
        else:
            o, ns, nb = mamba2_mixer(u, ssm_states[j], ssm_bufs[j], ssm_w_in[j], ssm_w_conv[j], ssm_b_conv[j],
                                     ssm_dt_bias[j], ssm_a_log[j], ssm_d[j], ssm_norm_g[j], ssm_w_out[j])
            new_ssm.append(ns)
            new_ssm_conv.append(nb)
        h = h + o
        h = h + swiglu(rmsnorm(h, g_ffn[i]), ffn_w_gate[i], ffn_w_up[i], ffn_w_down[i])
        gate = jax.nn.sigmoid(rmsnorm(h, g_ple[i]) @ ple_w_gate[i])
        h = h + (p[i] @ ple_w_proj[i]) * gate
    return rmsnorm(h, g_final), jnp.stack(new_conv), jnp.stack(new_ssm), jnp.stack(new_ssm_conv)


def setup_inputs(seed: int = 0) -> dict:
    key = jax.random.key(seed)
    ks = iter(jax.random.split(key, 64))
    f32 = jnp.float32

    def nrm(shape, scale):
        return jax.random.normal(next(ks), shape, f32) * scale

    def gain(shape):
        return 1.0 + 0.05 * jax.random.normal(next(ks), shape, f32)

    NC, NS = N_CONV_LAYERS, N_SSM_LAYERS
    out_scale = 1.0 / math.sqrt(2 * DEPTH)
    inp = {}
    inp['x_prompt'] = nrm((BATCH, SEQ, D_MODEL), 1.0)
    inp['x_sample'] = nrm((DEC_BATCH, DEC_SEQ, D_MODEL), 1.0)
    inp['p_prompt'] = nrm((DEPTH, BATCH, SEQ, PLE_DIM), 1.0)
    inp['p_sample'] = nrm((DEPTH, DEC_BATCH, DEC_SEQ, PLE_DIM), 1.0)
    inp['state_conv_mixer'] = nrm((NC, DEC_BATCH, CONV_WIDTH - 1, D_MODEL), 0.5)
    inp['state_ssm'] = nrm((NS, DEC_BATCH, N_HEADS_SSM, HEAD_DIM_SSM, D_STATE), 0.1)
    inp['state_ssm_conv'] = nrm((NS, DEC_BATCH, SSM_CONV_WIDTH - 1, SSM_CONV_DIM), 0.5)
    inp['g_mix'] = gain((DEPTH, D_MODEL))
    inp['g_ffn'] = gain((DEPTH, D_MODEL))
    inp['g_ple'] = gain((DEPTH, D_MODEL))
    inp['g_final'] = gain((D_MODEL,))
    inp['cv_w_pw1'] = nrm((NC, D_MODEL, 2 * D_MODEL), D_MODEL ** -0.5)
    inp['cv_b_pw1'] = nrm((NC, 2 * D_MODEL), 0.02)
    inp['cv_w_dw'] = nrm((NC, CONV_WIDTH, D_MODEL), CONV_WIDTH ** -0.5)
    inp['cv_b_dw'] = nrm((NC, D_MODEL), 0.02)
    inp['cv_ln_g'] = gain((NC, D_MODEL))
    inp['cv_ln_b'] = nrm((NC, D_MODEL), 0.02)
    inp['cv_w_pw2'] = nrm((NC, D_MODEL, D_MODEL), D_MODEL ** -0.5 * out_scale)
    inp['cv_b_pw2'] = nrm((NC, D_MODEL), 0.02)
    inp['ssm_w_in'] = nrm((NS, D_MODEL, SSM_IN_DIM), D_MODEL ** -0.5)
    inp['ssm_w_conv'] = nrm((NS, SSM_CONV_WIDTH, SSM_CONV_DIM), SSM_CONV_WIDTH ** -0.5)
    inp['ssm_b_conv'] = nrm((NS, SSM_CONV_DIM), 0.02)
    dt0 = jnp.exp(jax.random.uniform(next(ks), (NS, N_HEADS_SSM), f32, math.log(1e-3), math.log(1e-1)))
    inp['ssm_dt_bias'] = dt0 + jnp.log(-jnp.expm1(-dt0))
    inp['ssm_a_log'] = jnp.log(jax.random.uniform(next(ks), (NS, N_HEADS_SSM), f32, 1.0, 16.0))
    inp['ssm_d'] = gain((NS, N_HEADS_SSM))
    inp['ssm_norm_g'] = gain((NS, D_INNER))
    inp['ssm_w_out'] = nrm((NS, D_INNER, D_MODEL), D_INNER ** -0.5 * out_scale)
    inp['ffn_w_gate'] = nrm((DEPTH, D_MODEL, D_FF), D_MODEL ** -0.5)
    inp['ffn_w_up'] = nrm((DEPTH, D_MODEL, D_FF), D_MODEL ** -0.5)
    inp['ffn_w_down'] = nrm((DEPTH, D_FF, D_MODEL), D_FF ** -0.5 * out_scale)
    inp['ple_w_proj'] = nrm((DEPTH, PLE_DIM, D_MODEL), PLE_DIM ** -0.5 * out_scale)
    inp['ple_w_gate'] = nrm((DEPTH, D_MODEL, D_MODEL), D_MODEL ** -0.5)
    return inp


def reference(x_prompt, x_sample, p_prompt, p_sample, state_conv_mixer, state_ssm, state_ssm_conv,
              g_mix, g_ffn, g_ple, g_final,
              cv_w_pw1, cv_b_pw1, cv_w_dw, cv_b_dw, cv_ln_g, cv_ln_b, cv_w_pw2, cv_b_pw2,
              ssm_w_in, ssm_w_conv, ssm_b_conv, ssm_dt_bias, ssm_a_log, ssm_d, ssm_norm_g, ssm_w_out,
              ffn_w_gate, ffn_w_up, ffn_w_down, ple_w_proj, ple_w_gate):
    weights = (g_mix, g_ffn, g_ple, g_final,
               cv_w_pw1, cv_b_pw1, cv_w_dw, cv_b_dw, cv_ln_g, cv_ln_b, cv_w_pw2, cv_b_pw2,
               ssm_w_in, ssm_w_conv, ssm_b_conv, ssm_dt_bias, ssm_a_log, ssm_d, ssm_norm_g, ssm_w_out,
               ffn_w_gate, ffn_w_up, ffn_w_down, ple_w_proj, ple_w_gate)
    bp = x_prompt.shape[0]
    conv0 = jnp.zeros((N_CONV_LAYERS, bp, CONV_WIDTH - 1, D_MODEL), x_prompt.dtype)
    ssm0 = jnp.zeros((N_SSM_LAYERS, bp, N_HEADS_SSM, HEAD_DIM_SSM, D_STATE), jnp.float32)
    ssmc0 = jnp.zeros((N_SSM_LAYERS, bp, SSM_CONV_WIDTH - 1, SSM_CONV_DIM), x_prompt.dtype)
    y_prompt, nc_p, ns_p, nsc_p = trunk(x_prompt, p_prompt, conv0, ssm0, ssmc0, *weights)
    y_sample, nc_s, ns_s, nsc_s = trunk(x_sample, p_sample, state_conv_mixer, state_ssm, state_ssm_conv, *weights)
    return (y_prompt, y_sample, nc_p, ns_p, nsc_p, nc_s, ns_s, nsc_s)
```

```python
import os
import numpy as np
from contextlib import ExitStack
import concourse.bass as bass
import concourse.mybir as mybir
from concourse.ap import AP as _AP
from concourse.bass_utils import run_bass_kernel_spmd

F32 = mybir.dt.float32
BF16 = mybir.dt.bfloat16
ALU = mybir.AluOpType
AF = mybir.ActivationFunctionType
ISZ = {F32: 4, BF16: 2, mybir.dt.float32r: 4}

NCORES = 8
D = 1024
SEQ = 2048
DEPTH = 4
NBLK = 2
TP = 1024
NS = 16
PLE = 256
DFF = 2816
CW = 31
DIN = 2048
NH = 32
HD = 64
NG = 8
DST = 128
SCW = 4
SCD = 4096
SIN = DIN + SCD + NH
EPS = 1e-6
COMPUTE = ("pe", "act", "dve", "pool")
SAME_ENGINE_SYNC = True


class _Op:
    __slots__ = ("meth", "kw", "waits", "sig", "sigidx", "dmakey", "dmaidx")


class Sched:
    def __init__(self):
        self.ops = {e: [] for e in ("pe", "act", "dve", "pool", "sp")}
        self.recs = {"SB": [], "PSUM": []}
        self.known = {e: {} for e in self.ops}
        self.dmacnt = {}
        self.base = {}

    def _box(self, ap):
        isz = ISZ[ap.dtype]
        pat = ap.ap
        pstride = pat[0][0] * isz
        off = ap.offset * isz
        p0 = off // pstride
        f0 = off % pstride + self.base[ap.name]
        ext = 0
        for st, cnt in pat[1:]:
            ext += abs(st) * (cnt - 1)
        p1 = p0 + pat[0][1]
        f1 = f0 + (ext + 1) * isz
        if str(ap.space) == "PSUM":
            p0 = p0 // 32 * 32
            p1 = (p1 + 31) // 32 * 32
            f0 = f0 // 2048 * 2048
            f1 = f0 + 2048
        return (p0, p1, f0, f1)

    dry = False

    region = False
    rcount = 0
    rlimit = 10 ** 9

    def emit(self, eng, meth, dmakey=None, **kw):
        if self.dry:
            return None
        if self.region:
            self.rcount += 1
            if self.rcount > self.rlimit:
                return None
        op = _Op()
        op.meth = meth
        op.kw = kw
        op.sig = False
        op.dmakey = dmakey
        lst = self.ops[eng]
        if dmakey is not None:
            idx = self.dmacnt.get(dmakey, 0)
            self.dmacnt[dmakey] = idx + 1
            op.dmaidx = idx
            prod, pseq = "dma:" + dmakey, idx
        else:
            prod, pseq = eng, len(lst)
        deps = {}
        acc = []
        for key, v in kw.items():
            if not isinstance(v, _AP):
                continue
            sp = str(v.space)
            if sp not in ("SB", "PSUM"):
                continue
            isw = key in ("out", "accum_out", "ap")
            box = self._box(v)
            acc.append((sp, box, isw))
            for r in self.recs[sp]:
                if r[0] < box[1] and box[0] < r[1] and r[2] < box[3] and box[2] < r[3]:
                    if isw or r[4] or (sp == "PSUM" and r[5] != prod):
                        rp, rs = r[5], r[6]
                        if rp == prod:
                            if not rp.startswith("dma:"):
                                if eng == "pe" or not SAME_ENGINE_SYNC:
                                    continue
                                if not r[4]:
                                    continue
                            if rs == pseq:
                                continue
                        if rp.startswith("dma:"):
                            rs = self.dmacnt[rp[4:]] - 1 - (1 if rp == prod else 0)
                        if deps.get(rp, -1) < rs:
                            deps[rp] = rs
        kn = self.known[eng]
        waits = []
        for rp, rs in deps.items():
            if kn.get(rp, -1) >= rs:
                continue
            kn[rp] = rs
            waits.append((rp, rs))
            if not rp.startswith("dma:"):
                self.ops[rp][rs].sig = True
        op.waits = waits
        for sp, box, isw in acc:
            l2 = self.recs[sp]
            if isw:
                l2[:] = [r for r in l2 if not (box[0] <= r[0] and r[1] <= box[1] and box[2] <= r[2] and r[3] <= box[3])]
            else:
                l2[:] = [r for r in l2 if not (r[5] == prod and not r[4] and box[0] <= r[0] and r[1] <= box[1]
                                               and box[2] <= r[2] and r[3] <= box[3])]
            l2.append((box[0], box[1], box[2], box[3], isw, prod, pseq))
        lst.append(op)
        return op

    def finalize(self, nc, stack):
        CH = 16000
        sems = {}
        for e in COMPUTE:
            n = 0
            for op in self.ops[e]:
                if op.sig:
                    op.sigidx = n
                    n += 1
            sems[e] = [stack.enter_context(nc.semaphore(f"s_{e}_{i}")) for i in range(n // CH + 1)]
        for key in self.dmacnt:
            sems["dma:" + key] = stack.enter_context(nc.semaphore("d_" + key))
        block = stack.enter_context(nc.Block())
        ops = self.ops
        dmacnt = self.dmacnt

        def body(ename, final=False):
            def f(eng):
                for op in ops[ename]:
                    for rp, rs in op.waits:
                        if rp.startswith("dma:"):
                            eng.wait_ge(sems[rp], 16 * (rs + 1))
                        else:
                            si = ops[rp][rs].sigidx
                            eng.wait_ge(sems[rp][si // CH], si % CH + 1)
                    ins = getattr(eng, op.meth)(**op.kw)
                    if op.dmakey is not None:
                        ins.then_inc(sems["dma:" + op.dmakey], 16)
                    elif op.sig:
                        ins.then_inc(sems[ename][op.sigidx // CH], 1)
                if final:
                    for key, cnt in dmacnt.items():
                        eng.wait_ge(sems["dma:" + key], 16 * cnt)
            return f

        block.tensor(body("pe"))
        block.scalar(body("act"))
        block.vector(body("dve"))
        block.gpsimd(body("pool"))
        block.sync(body("sp", final=True))


class Eng:
    def __init__(self, sched, name):
        self.s, self.n = sched, name

    def __getattr__(self, meth):
        def f(**kw):
            return self.s.emit(self.n, meth, **kw)
        return f


class Builder:
    def __init__(self, opts):
        self.opts = opts
        self.nc = bass.Bass("TRN2", target_bir_lowering=False)
        self.stack = ExitStack()
        self.S = Sched()
        self.PE, self.ACT, self.DVE, self.POOL = (Eng(self.S, e) for e in COMPUTE)
        self.ins = {}
        self.in_shapes = {}
        self.outs = {}
        self.arena_off = 0

    def din(self, name, shape):
        need = self.opts.get("need")
        if need is not None and name not in need:
            shape = [1] * len(shape)
        self.in_shapes[name] = tuple(shape)
        t = self.nc.dram_tensor(name, list(shape), F32, kind="ExternalInput").ap()
        self.ins[name] = t
        return t

    def dout(self, name, shape):
        t = self.nc.dram_tensor(name, list(shape), F32, kind="ExternalOutput").ap()
        self.outs[name] = t
        return t

    def dma(self, key, out, in_):
        return self.S.emit("sp", "dma_start", dmakey=key, out=out, in_=in_)

    def setup_mem(self):
        nc = self.nc
        self.ARENA_BYTES = 212480
        self.arena = self.stack.enter_context(nc.sbuf_tensor("arena", [128, self.ARENA_BYTES // 4], F32))
        self.S.base["arena"] = 0
        self.banks = []
        for i in range(8):
            b = self.stack.enter_context(nc.psum_tensor(f"bank{i}", [128, 512], F32))
            self.S.base[f"bank{i}"] = i * 2048
            self.banks.append(b)
        self.bank_rr = {}
        self.POOLS_DENSE = {"mm": (0, 1, 2, 3, 4), "aux": (5, 6, 7)}
        self.POOLS_SSD = {"mm": (0, 1, 2, 6), "aux": (5,), "ssdC": (3, 4), "ssdE": (7,), "ssd": (7, 6)}
        self.pools = self.POOLS_DENSE

    def alloc(self, shape, dtype):
        n = 1
        for s in shape:
            n *= s
        nbytes = (n * ISZ[dtype] + 31) // 32 * 32
        off = self.arena_off
        assert off + nbytes <= self.ARENA_BYTES, ("SBUF arena overflow", off, nbytes)
        self.arena_off = off + nbytes
        v = self.arena[:, off // 4:(off + nbytes) // 4]
        if dtype != F32:
            v = v.bitcast(dtype)
        v = v[:, 0:n]
        if len(shape) == 2:
            v = v.rearrange("p (a b) -> p a b", a=shape[0])
        elif len(shape) == 3:
            v = v.rearrange("p (a b c) -> p a b c", a=shape[0], b=shape[1])
        elif len(shape) == 4:
            v = v.rearrange("p (a b c d) -> p a b c d", a=shape[0], b=shape[1], c=shape[2])
        return v

    def mark(self):
        return self.arena_off

    def release(self, m):
        self.arena_off = m

    def bank(self, pool):
        pools = self.pools
        ids = pools[pool]
        i = self.bank_rr.get(pool, 0)
        self.bank_rr[pool] = i + 1
        return self.banks[ids[i % len(ids)]]


class WStream:
    NST, NWB, CAP = 2, 3, 2112

    def __init__(self, B):
        self.B = B
        self.st = [B.alloc([self.CAP], F32) for _ in range(self.NST)]
        self.wb = [B.alloc([self.CAP], BF16) for _ in range(self.NWB)]
        self.plan = []
        self.planning = True
        self.nl = 0
        self.nu = 0

    def _views(self, buf, pieces):
        out = []
        off = 0
        for src, kc in pieces:
            n = src.shape[1]
            out.append(buf[:, off:off + kc * n].rearrange("p (k n) -> p k n", k=kc))
            off += kc * n
        assert off <= self.CAP, off
        return out

    def _load(self, i):
        name, pieces = self.plan[i]
        sv = self._views(self.st[i % self.NST], pieces)
        for (src, kc), v in zip(pieces, sv):
            self.B.dma(f"w{i % self.NST}", out=v, in_=src.rearrange("(k p) n -> p k n", p=128))
        tot = sum(kc * src.shape[1] for src, kc in pieces)
        self.B.POOL.tensor_copy(out=self.wb[i % self.NWB][:, 0:tot], in_=self.st[i % self.NST][:, 0:tot])

    def get(self, name, pieces):
        if self.planning:
            self.plan.append((name, pieces))
            return self._views(self.wb[0], pieces)
        i = self.nu
        assert self.plan[i][0] == name, (self.plan[i][0], name)
        while self.nl < min(len(self.plan), i + 3):
            self._load(self.nl)
            self.nl += 1
        self.nu += 1
        return self._views(self.wb[i % self.NWB], pieces)


def tiles_of(blk):
    t = [(0, 512), (512, 512)]
    if blk == NBLK - 1:
        t.append((1024, NS))
    return t


def build(opts):
    B = Builder(opts)
    nc = B.nc
    PE, ACT, DVE, POOL = B.PE, B.ACT, B.DVE, B.POOL
    nlayers = opts.get("nlayers", DEPTH)
    x_p = B.din("x_p", [SEQ, D])
    x_s = B.din("x_s", [NS, D])
    consts = B.din("consts", [128, 128 + 512 + 4096 + 1024 + 128])
    NCOLROWS = opts["ncolrows"]
    smallp = B.din("smallp", [NCOLROWS, 128])
    g_cols = opts["colidx"]
    y_p = B.dout("y_p", [SEQ, D])
    y_s = B.dout("y_s", [NS, D])
    p_p = B.din("p_p", [DEPTH, SEQ, PLE])
    p_s = B.din("p_s", [DEPTH, NS, PLE])
    wg = B.din("ffn_w_gate", [DEPTH, D, DFF])
    wu = B.din("ffn_w_up", [DEPTH, D, DFF])
    wd = B.din("ffn_w_down", [DEPTH, DFF, D])
    wpp = B.din("ple_w_proj", [DEPTH, PLE, D])
    wpg = B.din("ple_w_gate", [DEPTH, D, D])
    w1 = B.din("cv_w_pw1", [2, D, 2 * D])
    w2 = B.din("cv_w_pw2", [2, D, D])
    conv_s = B.din("conv_s", [2, NS, CW - 1, D])
    win = B.din("ssm_w_in", [2, D, SIN])
    wout = B.din("ssm_w_out", [2, DIN, D])
    ssm_s = B.din("ssm_s", [2, NS, NH * HD, DST])
    ssmc_s = B.din("ssmc_s", [2, NS, SCW - 1, SCD])
    ns_p = B.dout("ns_p", [2, NH * HD, DST])
    nsc_p = B.dout("nsc_p", [2, SCW - 1, SCD])
    ns_s = B.dout("ns_s", [2, NS, NH * HD, DST])
    nsc_s = B.dout("nsc_s", [2, NS, SCW - 1, SCD])
    nc_p = B.dout("nc_p", [2, CW - 1, D])
    nc_s = B.dout("nc_s", [2, NS, CW - 1, D])

    B.setup_mem()
    T = TP + NS
    h = B.alloc([8, T], F32)
    u = B.alloc([8, T], BF16)
    ident = B.alloc([128], F32)
    onesb = B.alloc([128], BF16)
    ncolt = (NCOLROWS + 127) // 128
    cols = B.alloc([ncolt * 128], F32)
    identb = B.alloc([128], BF16)
    vprefix = [B.alloc([8, CW - 1], BF16) for _ in range(2)]
    xin = [B.alloc([D], F32) for _ in range(2)]
    rstd = B.alloc([512], F32)
    sqb = B.alloc([8, 512], BF16)
    ws = WStream(B)
    scr4 = B.alloc([4, 512], F32)
    sgb = [scr4[:, 0, :], scr4[:, 1, :]]
    tmpb = [scr4[:, 2, :], scr4[:, 3, :]]
    scr_bf = B.arena[:, (B.arena_off - 4 * 512 * 4) // 4:B.arena_off // 4].bitcast(BF16).rearrange("p (a b) -> p a b", a=8)
    STs = [B.alloc([NG, 256], F32) for _ in range(2)]
    xprefix = [B.alloc([32, SCW], BF16) for _ in range(2)]
    cnt = [0]

    B.dma("c0", out=ident, in_=consts[:, 0:128])
    DVE.memset(ap=onesb, constant=1.0)
    DVE.tensor_copy(out=identb, in_=ident)
    for t in range(ncolt):
        r0 = t * 128
        nr = min(128, NCOLROWS - r0)
        st = xin[t % 2]
        B.dma(f"xin{t % 2}", out=st[0:nr, 0:128], in_=smallp[r0:r0 + nr, :])
        bk = B.bank("aux")
        PE.transpose(out=bk[:, 0:nr], in_=st[0:nr, 0:128], identity=ident[0:nr, 0:nr])
        DVE.tensor_copy(out=cols[:, r0:r0 + nr], in_=bk[:, 0:nr])

    def col(name, i=0):
        c = g_cols[name] + i
        return cols[:, c:c + 1]

    def load_x(blk):
        for tc in range(8):
            st = xin[tc % 2]
            B.dma(f"xin{tc % 2}", out=st, in_=x_p[blk * TP + tc * 128: blk * TP + (tc + 1) * 128, :])
            for half in range(2):
                bk = B.bank("aux")
                for q in range(4):
                    m = half * 4 + q
                    PE.transpose(out=bk[:, q * 128:(q + 1) * 128], in_=st[:, m * 128:(m + 1) * 128], identity=ident)
                ACT.copy(out=h[:, half * 4:half * 4 + 4, tc * 128:(tc + 1) * 128],
                         in_=bk[:, :].rearrange("p (q t) -> p q t", q=4))
        if blk == NBLK - 1:
            st = xin[0]
            B.dma("xin0", out=st[0:NS, :], in_=x_s)
            for half in range(2):
                bk = B.bank("aux")
                for q in range(4):
                    m = half * 4 + q
                    PE.transpose(out=bk[:, q * NS:(q + 1) * NS], in_=st[0:NS, m * 128:(m + 1) * 128],
                                 identity=ident[0:NS, 0:NS])
                ACT.copy(out=h[:, half * 4:half * 4 + 4, TP:TP + NS],
                         in_=bk[:, 0:4 * NS].rearrange("p (q t) -> p q t", q=4))

    def rms_stats(src, nch, c0, n, dim):
        ACT.activation(out=sqb[:, 0:nch, 0:n], in_=src[:, 0:nch, c0:c0 + n], func=AF.Square)
        bk = B.bank("aux")
        for m in range(nch):
            PE.matmul(out=bk[:, 0:n], lhsT=onesb, rhs=sqb[:, m, 0:n], start=(m == 0), stop=(m == nch - 1))
        ACT.activation(out=rstd[:, 0:n], in_=bk[:, 0:n], func=AF.Sqrt, scale=1.0 / dim, bias=epsc)
        DVE.reciprocal(out=rstd[:, 0:n], in_=rstd[:, 0:n])

    epsc = B.alloc([1], F32)
    DVE.memset(ap=epsc, constant=EPS)

    def rmsnorm_to(dst, gname, gi, blk):
        for (c0, n) in tiles_of(blk):
            rms_stats(h, 8, c0, n, D)
            for m in range(8):
                DVE.scalar_tensor_tensor(out=dst[:, m, c0:c0 + n], in0=h[:, m, c0:c0 + n], scalar=col(gname, gi * 8 + m),
                                         in1=rstd[:, 0:n], op0=ALU.mult, op1=ALU.mult)

    def final_out(blk):
        mk = B.mark()
        yv = B.alloc([8, 512], F32)
        ost = [B.alloc([D], F32) for _ in range(2)]
        cnt = 0
        for (c0, n) in tiles_of(blk):
            rms_stats(h, 8, c0, n, D)
            for m in range(8):
                DVE.scalar_tensor_tensor(out=yv[:, m, 0:n], in0=h[:, m, c0:c0 + n], scalar=col("g_final", m),
                                         in1=rstd[:, 0:n], op0=ALU.mult, op1=ALU.mult)
            ntc = (n + 127) // 128
            for tc in range(ntc):
                w = min(128, n - tc * 128)
                o = ost[cnt % 2]
                for half in range(2):
                    bk = B.bank("aux")
                    for q in range(4):
                        m = half * 4 + q
                        PE.transpose(out=bk[0:w, q * 128:(q + 1) * 128], in_=yv[:, m, tc * 128:tc * 128 + w],
                                     identity=ident)
                    ACT.copy(out=o[0:w, half * 512:(half + 1) * 512], in_=bk[0:w, :])
                if c0 < TP:
                    r0 = blk * TP + c0 + tc * 128
                    B.dma(f"ost{cnt % 2}", out=y_p[r0:r0 + w, :], in_=o[0:w, :])
                else:
                    B.dma(f"ost{cnt % 2}", out=y_s[:, :], in_=o[0:w, :])
                cnt += 1
        B.release(mk)

    def ffn(i, blk):
        tiles = tiles_of(blk)
        rmsnorm_to(u, "g_ffn", i, blk)
        mk = B.mark()
        hid = B.alloc([11, T], BF16)
        for half in range(2):
            f0 = half * 11
            for fa in range(0, 11, 1):
                nf = 1
                cs = slice((f0 + fa) * 128, (f0 + fa + nf) * 128)
                wgv, wuv = ws.get(f"ffn_gu{i}_{half}_{fa}", [(wg[i][:, cs], 8), (wu[i][:, cs], 8)])
                for q in range(nf):
                    f = fa + q
                    for (c0, n) in tiles:
                        bg = B.bank("mm")
                        bu = B.bank("mm")
                        for k in range(8):
                            PE.matmul(out=bg[:, 0:n], lhsT=wgv[:, k, q * 128:(q + 1) * 128], rhs=u[:, k, c0:c0 + n],
                                      start=(k == 0), stop=(k == 7))
                        for k in range(8):
                            PE.matmul(out=bu[:, 0:n], lhsT=wuv[:, k, q * 128:(q + 1) * 128], rhs=u[:, k, c0:c0 + n],
                                      start=(k == 0), stop=(k == 7))
                        sg = sgb[cnt[0] % 2]
                        cnt[0] += 1
                        ACT.activation(out=sg[:, 0:n], in_=bg[:, 0:n], func=AF.Silu)
                        DVE.tensor_tensor(out=hid[:, f, c0:c0 + n], in0=bu[:, 0:n], in1=sg[:, 0:n], op=ALU.mult)
            for mo2 in range(8):
                (wdv,) = ws.get(f"ffn_d{i}_{half}_{mo2}",
                                [(wd[i][f0 * 128:(f0 + 11) * 128, mo2 * 128:(mo2 + 1) * 128], 11)])
                for q in range(1):
                    mo = mo2 + q
                    for (c0, n) in tiles:
                        bk = B.bank("mm")
                        for k in range(11):
                            PE.matmul(out=bk[:, 0:n], lhsT=wdv[:, k, q * 128:(q + 1) * 128], rhs=hid[:, k, c0:c0 + n],
                                      start=(k == 0), stop=(k == 10))
                        DVE.tensor_tensor(out=h[:, mo, c0:c0 + n], in0=bk[:, 0:n], in1=h[:, mo, c0:c0 + n], op=ALU.add)
        B.release(mk)

    def ple(i, blk):
        tiles = tiles_of(blk)
        mk = B.mark()
        pT = B.alloc([2, T], BF16)
        for tc in range(8):
            st = xin[tc % 2]
            r0 = blk * TP + tc * 128
            B.dma(f"xin{tc % 2}", out=st[:, 0:PLE], in_=p_p[i, r0:r0 + 128, :])
            bk = B.bank("aux")
            for q in range(2):
                PE.transpose(out=bk[:, q * 128:(q + 1) * 128], in_=st[:, q * 128:(q + 1) * 128], identity=ident)
            ACT.copy(out=pT[:, 0:2, tc * 128:(tc + 1) * 128], in_=bk[:, 0:256].rearrange("p (q t) -> p q t", q=2))
        if blk == NBLK - 1:
            st = xin[0]
            B.dma("xin0", out=st[0:NS, 0:PLE], in_=p_s[i])
            bk = B.bank("aux")
            for q in range(2):
                PE.transpose(out=bk[:, q * NS:(q + 1) * NS], in_=st[0:NS, q * 128:(q + 1) * 128],
                             identity=ident[0:NS, 0:NS])
            ACT.copy(out=pT[:, 0:2, TP:TP + NS], in_=bk[:, 0:2 * NS].rearrange("p (q t) -> p q t", q=2))
        rmsnorm_to(u, "g_ple", i, blk)
        for mo2 in range(8):
            cs = slice(mo2 * 128, (mo2 + 1) * 128)
            wgv, wpv = ws.get(f"ple{i}_{mo2}", [(wpg[i][:, cs], 8), (wpp[i][:, cs], 2)])
            for q in range(1):
                mo = mo2 + q
                for (c0, n) in tiles:
                    bg = B.bank("mm")
                    bp = B.bank("mm")
                    for k in range(8):
                        PE.matmul(out=bg[:, 0:n], lhsT=wgv[:, k, q * 128:(q + 1) * 128], rhs=u[:, k, c0:c0 + n],
                                  start=(k == 0), stop=(k == 7))
                    for k in range(2):
                        PE.matmul(out=bp[:, 0:n], lhsT=wpv[:, k, q * 128:(q + 1) * 128], rhs=pT[:, k, c0:c0 + n],
                                  start=(k == 0), stop=(k == 1))
                    sg = sgb[cnt[0] % 2]
                    tm = tmpb[cnt[0] % 2]
                    cnt[0] += 1
                    ACT.activation(out=sg[:, 0:n], in_=bg[:, 0:n], func=AF.Sigmoid)
                    DVE.tensor_tensor(out=tm[:, 0:n], in0=bp[:, 0:n], in1=sg[:, 0:n], op=ALU.mult)
                    DVE.tensor_tensor(out=h[:, mo, c0:c0 + n], in0=tm[:, 0:n], in1=h[:, mo, c0:c0 + n], op=ALU.add)
        B.release(mk)

    def bc8(ap2, n):
        return ap2[:, 0:n].unsqueeze(1).to_broadcast([128, 8, n])

    def conformer(i, blk):
        j = i // 2
        tiles = tiles_of(blk)
        last = blk == NBLK - 1
        rmsnorm_to(u, "g_mix", i, blk)
        mk = B.mark()
        v = B.alloc([8, CW - 1 + T], BF16)
        c32 = B.alloc([8, T], F32)
        diag = [B.alloc([CW, 128], BF16)]
        cb = scr_bf
        mean = B.alloc([512], F32)
        msq = B.alloc([512], F32)
        v32 = B.alloc([8, CW - 1], F32)
        vs32 = B.alloc([8, NS], F32)
        if blk == 0:
            DVE.memset(ap=v[:, :, 0:CW - 1], constant=0.0)
        else:
            DVE.tensor_copy(out=v[:, :, 0:CW - 1], in_=vprefix[j])
        for mo in range(8):
            wa, wgt = ws.get(f"cv1_{i}_{mo}", [(w1[j][:, mo * 128:(mo + 1) * 128], 8),
                                              (w1[j][:, D + mo * 128:D + (mo + 1) * 128], 8)])
            for (c0, n) in tiles:
                ba = B.bank("mm")
                bg = B.bank("mm")
                for k in range(8):
                    PE.matmul(out=ba[:, 0:n], lhsT=wa[:, k, :], rhs=u[:, k, c0:c0 + n], start=(k == 0), stop=(k == 7))
                for k in range(8):
                    PE.matmul(out=bg[:, 0:n], lhsT=wgt[:, k, :], rhs=u[:, k, c0:c0 + n], start=(k == 0), stop=(k == 7))
                sg = sgb[cnt[0] % 2]
                cnt[0] += 1
                ACT.activation(out=sg[:, 0:n], in_=bg[:, 0:n], func=AF.Sigmoid, bias=col("cv_b_pw1", j * 16 + 8 + mo))
                ba_col = col("cv_b_pw1", j * 16 + mo)
                DVE.scalar_tensor_tensor(out=v[:, mo, CW - 1 + c0:CW - 1 + c0 + n], in0=ba[:, 0:n], scalar=ba_col,
                                         in1=sg[:, 0:n], op0=ALU.add, op1=ALU.mult)
                if last and c0 == 512:
                    DVE.scalar_tensor_tensor(out=v32[:, mo, :], in0=ba[:, 512 - (CW - 1):512], scalar=ba_col,
                                             in1=sg[:, 512 - (CW - 1):512], op0=ALU.add, op1=ALU.mult)
                if last and c0 == TP:
                    DVE.scalar_tensor_tensor(out=vs32[:, mo, :], in0=ba[:, 0:NS], scalar=ba_col,
                                             in1=sg[:, 0:NS], op0=ALU.add, op1=ALU.mult)
        if not last:
            DVE.tensor_copy(out=vprefix[j], in_=v[:, :, TP:TP + CW - 1])
        stt = None
        if last:
            stt = B.alloc([8, NS, CW - 1], BF16)
            for s4 in range(4):
                st = xin[s4 % 2]
                B.dma(f"xin{s4 % 2}", out=st[0:120, :], in_=conv_s[j, s4 * 4:(s4 + 1) * 4].rearrange("s k c -> (s k) c"))
                for half in range(2):
                    bk = B.bank("aux")
                    for q in range(4):
                        PE.transpose(out=bk[:, q * 120:(q + 1) * 120], in_=st[0:120, (half * 4 + q) * 128:(half * 4 + q + 1) * 128],
                                     identity=ident[0:120, 0:120])
                    ACT.copy(out=stt[:, half * 4:half * 4 + 4, s4 * 4:(s4 + 1) * 4, :].rearrange("p m s k -> p m (s k)"),
                             in_=bk[:, 0:480].rearrange("p (q t) -> p q t", q=4))
            B.dma("ncs", out=nc_s[j, :, 0:CW - 2, :], in_=conv_s[j, :, 1:CW - 1, :])
            for (src, wdt, dst) in ((v32, CW - 1, nc_p[j]), (vs32, NS, nc_s[j, :, CW - 2, :])):
                o = xin[0] if wdt == NS else xin[1]
                for half in range(2):
                    bk = B.bank("aux")
                    for q in range(4):
                        PE.transpose(out=bk[0:wdt, q * 128:(q + 1) * 128], in_=src[:, half * 4 + q, :], identity=ident)
                    ACT.copy(out=o[0:wdt, half * 512:(half + 1) * 512], in_=bk[0:wdt, :])
                B.dma("xin0" if wdt == NS else "xin1", out=dst, in_=o[0:wdt, :])
        for mo in range(8):
            dg = diag[0]
            for k in range(CW):
                DVE.tensor_scalar(out=dg[:, k, :], in0=identb, scalar1=col("cv_w_dw", (j * CW + k) * 8 + mo), scalar2=None,
                                  op0=ALU.mult)
            for (c0, n) in tiles:
                bk = B.bank("mm")
                if c0 < TP:
                    for k in range(CW):
                        PE.matmul(out=bk[:, 0:n], lhsT=dg[:, k, :], rhs=v[:, mo, c0 + k:c0 + k + n],
                                  start=(k == 0), stop=(k == CW - 1))
                else:
                    for k in range(CW - 1):
                        PE.matmul(out=bk[:, 0:NS], lhsT=dg[:, k, :], rhs=stt[:, mo, :, k], start=(k == 0), stop=False)
                    PE.matmul(out=bk[:, 0:NS], lhsT=dg[:, CW - 1, :], rhs=v[:, mo, CW - 1 + TP:CW - 1 + TP + NS],
                              start=False, stop=True)
                ACT.activation(out=c32[:, mo, c0:c0 + n], in_=bk[:, 0:n], func=AF.Identity, bias=col("cv_b_dw", j * 8 + mo))
        for (c0, n) in tiles:
            ACT.activation(out=sqb[:, :, 0:n], in_=c32[:, :, c0:c0 + n], func=AF.Square)
            DVE.tensor_copy(out=cb[:, :, 0:n], in_=c32[:, :, c0:c0 + n])
            bs = B.bank("aux")
            bq = B.bank("aux")
            for m in range(8):
                PE.matmul(out=bs[:, 0:n], lhsT=onesb, rhs=cb[:, m, 0:n], start=(m == 0), stop=(m == 7))
            for m in range(8):
                PE.matmul(out=bq[:, 0:n], lhsT=onesb, rhs=sqb[:, m, 0:n], start=(m == 0), stop=(m == 7))
            DVE.tensor_scalar(out=mean[:, 0:n], in0=bs[:, 0:n], scalar1=1.0 / D, scalar2=None, op0=ALU.mult)
            DVE.tensor_tensor(out=msq[:, 0:n], in0=mean[:, 0:n], in1=mean[:, 0:n], op=ALU.mult)
            DVE.scalar_tensor_tensor(out=msq[:, 0:n], in0=bq[:, 0:n], scalar=1.0 / D, in1=msq[:, 0:n],
                                     op0=ALU.mult, op1=ALU.subtract)
            ACT.activation(out=rstd[:, 0:n], in_=msq[:, 0:n], func=AF.Sqrt, bias=epsc)
            DVE.reciprocal(out=rstd[:, 0:n], in_=rstd[:, 0:n])
            DVE.tensor_tensor(out=c32[:, :, c0:c0 + n], in0=c32[:, :, c0:c0 + n], in1=bc8(mean, n), op=ALU.subtract)
            DVE.tensor_tensor(out=c32[:, :, c0:c0 + n], in0=c32[:, :, c0:c0 + n], in1=bc8(rstd, n), op=ALU.mult)
            for m in range(8):
                ACT.activation(out=u[:, m, c0:c0 + n], in_=c32[:, m, c0:c0 + n], func=AF.Silu,
                               scale=col("cv_ln_g", j * 8 + m), bias=col("cv_ln_b", j * 8 + m))
        for mo in range(8):
            (w2v,) = ws.get(f"cv2_{i}_{mo}", [(w2[j][:, mo * 128:(mo + 1) * 128], 8)])
            for (c0, n) in tiles:
                bk = B.bank("mm")
                for k in range(8):
                    PE.matmul(out=bk[:, 0:n], lhsT=w2v[:, k, :], rhs=u[:, k, c0:c0 + n], start=(k == 0), stop=(k == 7))
                DVE.scalar_tensor_tensor(out=h[:, mo, c0:c0 + n], in0=bk[:, 0:n], scalar=col("cv_b_pw2", j * 8 + mo),
                                         in1=h[:, mo, c0:c0 + n], op0=ALU.add, op1=ALU.add)
        B.release(mk)

    def bcast(ap2, shape, axis):
        return ap2.unsqueeze(axis).to_broadcast(shape)

    def mamba(i, blk):
        j = i // 2
        tiles = tiles_of(blk)
        last = blk == NBLK - 1
        Tb = TP + (NS if last else 0)
        rmsnorm_to(u, "g_mix", i, blk)
        mk = B.mark()
        B.S.region = True
        B.S.rlimit = opts.get("rlimit", 10 ** 9)
        B.pools = B.POOLS_SSD
        mask01 = B.alloc([128], F32)
        selhp = B.alloc([256], F32)
        ones4 = B.alloc([128], F32)
        dtT = B.alloc([T], F32)
        acT = B.alloc([T], F32)
        tmpT = B.alloc([T], F32)
        sz = B.alloc([2, T], F32)
        xpre = B.alloc([4, SCW + T], BF16)
        xs = B.alloc([2, 512], F32)
        y32 = B.alloc([2, 512], F32)
        B32 = B.alloc([512], F32)
        BT = B.alloc([512], BF16)
        CT = B.alloc([512], BF16)
        C32s = B.alloc([NS], F32)
        diag4 = B.alloc([4, SCW, 128], BF16)
        nac = B.alloc([8, 4], F32)
        dtm = B.alloc([8, 4], F32)
        dd = B.alloc([8, 4], F32)
        dl = B.alloc([4], F32)
        aneg = B.alloc([1], F32)
        ctmp = []
        for _ in range(2):
            ctmp.append((B.alloc([256], BF16), B.alloc([256], BF16), B.alloc([128], BF16), B.alloc([4, 128], F32),
                         B.alloc([4, 128], F32), B.alloc([4, 128], BF16), B.alloc([4, 128], BF16), B.alloc([4, 128], F32)))
        ccnt = [0]
        prevT = B.alloc([256], BF16)
        ygb = B.alloc([2, 512], BF16)
        xl32 = B.alloc([32, SCW - 1], F32)
        xs32s = B.alloc([32, NS], F32)
        stg = ctmp[1][7][:, 2:4, :]
        t1 = scr4[:, 0:2, :]
        if last:
            sst = B.alloc([4, NS, SCW - 1], BF16)
            ea = B.alloc([2, NS], F32)
            xdths = B.alloc([2, NS], F32)
            ysm = B.alloc([2, NS], F32)
            BCs = B.alloc([256], F32)
            S0 = [B.alloc([4, 2, 128], F32) for _ in range(2)]
        B.dma("xin0", out=xin[0][:, 0:128], in_=consts[:, 128:256])
        DVE.tensor_scalar(out=mask01, in0=xin[0][:, 0:128], scalar1=0.0, scalar2=None, op0=ALU.is_equal)
        B.dma("c1", out=selhp[0:4, :], in_=consts[0:4, 640:896])
        DVE.memset(ap=ones4, constant=1.0)
        if blk == 0:
            DVE.memset(ap=STs[j], constant=0.0)
        skip = opts.get("skip", ())
        if last and "nscs" not in skip:
            B.dma("nscs", out=nsc_s[j, :, 0:SCW - 2, :], in_=ssmc_s[j, :, 1:SCW - 1, :])
        s0cnt = 0
        for g in range(NG):
            colz = g * 256
            colx = DIN + g * 256
            colB = 2 * DIN + g * 128
            colC = 2 * DIN + 1024 + g * 128
            coldt = DIN + SCD + 4 * g
            chunk_ids = [2 * g, 2 * g + 1, 16 + g, 24 + g]
            STg = STs[j][:, g, :].rearrange("p (a b) -> p a b", a=4)
            if blk == 0:
                DVE.memset(ap=xpre[:, :, 0:SCW], constant=0.0)
            else:
                DVE.tensor_copy(out=xpre[:, :, 0:SCW], in_=xprefix[j][:, g * 4:(g + 1) * 4, :])
            (wz,) = ws.get(f"mz{i}_{g}", [(win[j][:, colz:colz + 256], 8)])
            for q in range(2):
                for (c0, n) in tiles:
                    bk = B.bank("mm")
                    for k in range(8):
                        PE.matmul(out=bk[:, 0:n], lhsT=wz[:, k, q * 128:(q + 1) * 128], rhs=u[:, k, c0:c0 + n],
                                  start=(k == 0), stop=(k == 7))
                    ACT.activation(out=sz[:, q, c0:c0 + n], in_=bk[:, 0:n], func=AF.Silu)

            def evac_pre(bk, qi, c0, n):
                ACT.copy(out=xpre[:, qi, SCW + c0:SCW + c0 + n], in_=bk[:, 0:n])
                if last and c0 == 512:
                    DVE.tensor_copy(out=xl32[:, chunk_ids[qi], :], in_=bk[:, 512 - (SCW - 1):512])
                if last and c0 == TP:
                    DVE.tensor_copy(out=xs32s[:, chunk_ids[qi], :], in_=bk[:, 0:NS])

            (wx,) = ws.get(f"mx{i}_{g}", [(win[j][:, colx:colx + 256], 8)])
            for q in range(2):
                for (c0, n) in tiles:
                    bk = B.bank("mm")
                    for k in range(8):
                        PE.matmul(out=bk[:, 0:n], lhsT=wx[:, k, q * 128:(q + 1) * 128], rhs=u[:, k, c0:c0 + n],
                                  start=(k == 0), stop=(k == 7))
                    evac_pre(bk, q, c0, n)
            wB, wC, wdt = ws.get(f"mbc{i}_{g}", [(win[j][:, colB:colB + 128], 8), (win[j][:, colC:colC + 128], 8),
                                                 (win[j][:, coldt:coldt + 4], 8)])
            for qi, wv in ((2, wB), (3, wC)):
                for (c0, n) in tiles:
                    bk = B.bank("mm")
                    for k in range(8):
                        PE.matmul(out=bk[:, 0:n], lhsT=wv[:, k, :], rhs=u[:, k, c0:c0 + n], start=(k == 0), stop=(k == 7))
                    evac_pre(bk, qi, c0, n)
            dtb = col("ssm_dt_bias_g", j * 8 + g)[0:4, :]
            for (c0, n) in (tiles if "dt" not in skip else ()):
                bk = B.bank("mm")
                for k in range(8):
                    PE.matmul(out=bk[0:4, 0:n], lhsT=wdt[:, k, :], rhs=u[:, k, c0:c0 + n], start=(k == 0), stop=(k == 7))
                ACT.activation(out=tmpT[0:4, c0:c0 + n], in_=bk[0:4, 0:n], func=AF.Exp, bias=dtb)
                ACT.activation(out=dtT[0:4, c0:c0 + n], in_=tmpT[0:4, c0:c0 + n], func=AF.Ln, bias=ones4[0:4, 0:1])
            if not last:
                DVE.tensor_copy(out=xprefix[j][:, g * 4:(g + 1) * 4, :], in_=xpre[:, :, TP:TP + SCW])
            mcut = opts.get("mcut", 9)
            if mcut <= -3:
                continue
            ACT.activation(out=aneg[0:4, :], in_=col("ssm_a_log_g", j * 8 + g)[0:4, :], func=AF.Exp)
            DVE.tensor_scalar(out=aneg[0:4, :], in0=aneg[0:4, :], scalar1=-1.0, scalar2=None, op0=ALU.mult)
            DVE.tensor_scalar(out=tmpT[0:4, 0:Tb], in0=dtT[0:4, 0:Tb], scalar1=aneg[0:4, 0:1], scalar2=None, op0=ALU.mult)
            for c in range(8):
                cs = slice(c * 128, (c + 1) * 128)
                DVE.tensor_tensor_scan(out=acT[0:4, cs], data0=ones4[0:4, 0:128], data1=tmpT[0:4, cs], initial=0.0,
                                       op0=ALU.mult, op1=ALU.add)
            for c in range(8):
                cs = slice(c * 128, (c + 1) * 128)
                bk = B.bank("aux")
                PE.transpose(out=bk[:, 0:4], in_=acT[0:4, cs], identity=ident[0:4, 0:4])
                PE.transpose(out=bk[:, 4:8], in_=dtT[0:4, cs], identity=ident[0:4, 0:4])
                DVE.tensor_scalar(out=dl[0:4, 0:4], in0=ident[0:4, 0:4], scalar1=acT[0:4, c * 128 + 127:c * 128 + 128],
                                  scalar2=None, op0=ALU.mult)
                PE.matmul(out=bk[:, 8:12], lhsT=ones4[0:4, 0:128], rhs=dl[0:4, 0:4], start=True, stop=True)
                DVE.tensor_scalar(out=nac[:, c, :], in0=bk[:, 0:4], scalar1=-1.0, scalar2=None, op0=ALU.mult)
                DVE.tensor_tensor(out=dd[:, c, :], in0=bk[:, 8:12], in1=nac[:, c, :], op=ALU.add)
                ACT.activation(out=dd[:, c, :], in_=dd[:, c, :], func=AF.Exp)
                DVE.tensor_copy(out=dtm[:, c, :], in_=bk[:, 4:8])
                DVE.tensor_tensor(out=dd[:, c, :], in0=dd[:, c, :], in1=dtm[:, c, :], op=ALU.mult)
            for qi, q32 in enumerate(chunk_ids):
                for k in range(SCW):
                    DVE.tensor_scalar(out=diag4[:, qi, k, :], in0=identb, scalar1=col("ssm_w_conv", (j * SCW + k) * 32 + q32),
                                      scalar2=None, op0=ALU.mult)
            if last:
                st = xin[1]
                for qi, q32 in enumerate(chunk_ids):
                    B.dma("xin1", out=st[0:NS * (SCW - 1), qi * 128:(qi + 1) * 128],
                          in_=ssmc_s[j, :, :, q32 * 128:(q32 + 1) * 128].rearrange("s k c -> (s k) c"))
                bk = B.bank("aux")
                nr = NS * (SCW - 1)
                for qi in range(4):
                    PE.transpose(out=bk[:, qi * nr:(qi + 1) * nr], in_=st[0:nr, qi * 128:(qi + 1) * 128],
                                 identity=ident[0:nr, 0:nr])
                ACT.copy(out=sst.rearrange("p q s k -> p q (s k)"), in_=bk[:, 0:4 * nr].rearrange("p (q t) -> p q t", q=4))
            (wo,) = ws.get(f"mo{i}_{g}", [(wout[j][g * 256:(g + 1) * 256, :], 2)])
            ACT.copy(out=prevT, in_=STs[j][:, g, :])
            for (c0, n) in tiles:
                for qi, q32 in enumerate(chunk_ids):
                    bias = col("ssm_b_conv", j * 32 + q32)
                    bk = B.bank("mm")
                    if c0 < TP:
                        for k in range(SCW):
                            PE.matmul(out=bk[:, 0:n], lhsT=diag4[:, qi, k, :], rhs=xpre[:, qi, c0 + k + 1:c0 + k + 1 + n],
                                      start=(k == 0), stop=(k == SCW - 1))
                    else:
                        for k in range(SCW - 1):
                            PE.matmul(out=bk[:, 0:NS], lhsT=diag4[:, qi, k, :], rhs=sst[:, qi, :, k], start=(k == 0), stop=False)
                        PE.matmul(out=bk[:, 0:NS], lhsT=diag4[:, qi, SCW - 1, :],
                                  rhs=xpre[:, qi, SCW + TP:SCW + TP + NS], start=False, stop=True)
                    if qi < 2:
                        ACT.activation(out=xs[:, qi, 0:n], in_=bk[:, 0:n], func=AF.Silu, bias=bias)
                    elif qi == 2:
                        ACT.activation(out=B32[:, 0:n], in_=bk[:, 0:n], func=AF.Silu, bias=bias)
                        DVE.tensor_copy(out=BT[:, 0:n], in_=B32[:, 0:n])
                    else:
                        ACT.activation(out=CT[:, 0:n], in_=bk[:, 0:n], func=AF.Silu, bias=bias)
                        if c0 == TP:
                            ACT.activation(out=C32s[:, 0:NS], in_=bk[:, 0:NS], func=AF.Silu, bias=bias)
                def stage_a(cl):
                    c = c0 // 128 + cl
                    cs = slice(c * 128, (c + 1) * 128)
                    ls = slice(cl * 128, (cl + 1) * 128)
                    pp = ccnt[0] % 2
                    ccnt[0] += 1
                    xdt, dxdt, Btm, Ebuf, LT, Mb, Cs, R4 = ctmp[pp]
                    DVE.tensor_tensor(out=R4[0:4], in0=bcast(acT[0:4, cs], [4, 4, 128], 1),
                                      in1=bcast(ident[0:4, 0:4], [4, 4, 128], 2), op=ALU.mult)
                    bkT = B.bank("aux")
                    for q in range(2):
                        PE.transpose(out=bkT[:, q * 128:(q + 1) * 128], in_=xs[:, q, ls], identity=ident)
                    PE.transpose(out=bkT[:, 256:384], in_=B32[:, ls], identity=ident)
                    xT4 = bkT[:, 0:256].rearrange("p (a b) -> p a b", a=4)
                    DVE.tensor_tensor(out=xdt.rearrange("p (a b) -> p a b", a=4), in0=xT4,
                                      in1=bcast(dtm[:, c, :], [128, 4, 64], 2), op=ALU.mult)
                    DVE.tensor_tensor(out=dxdt.rearrange("p (a b) -> p a b", a=4), in0=xT4,
                                      in1=bcast(dd[:, c, :], [128, 4, 64], 2), op=ALU.mult)
                    ACT.copy(out=Btm, in_=bkT[:, 256:384])
                    bkC = B.bank("ssdC")
                    PE.matmul(out=bkC[:, 0:128], lhsT=BT[:, ls], rhs=CT[:, ls], start=True, stop=True)
                    bkE = B.bank("ssdE")
                    PE.matmul(out=bkE[:, 0:512], lhsT=ones4[0:4, 0:128], rhs=R4[0:4].rearrange("p a b -> p (a b)"),
                              start=True, stop=True)
                    ACT.activation(out=Ebuf.rearrange("p a b -> p (a b)"), in_=bkE[:, 0:512], func=AF.Exp)
                    for hh in range(4):
                        DVE.tensor_scalar(out=LT[:, hh, :], in0=bkE[:, hh * 128:(hh + 1) * 128], scalar1=nac[:, c, hh:hh + 1],
                                          scalar2=0.0, op0=ALU.add, op1=ALU.min)
                    ACT.activation(out=LT.rearrange("p a b -> p (a b)"), in_=LT.rearrange("p a b -> p (a b)"), func=AF.Exp)
                    return (ls, ctmp[pp], bkC)

                def stage_b(info):
                    ls, (xdt, dxdt, Btm, Ebuf, LT, Mb, Cs, R4), bkC = info
                    DVE.tensor_tensor(out=bkC[:, 0:128], in0=bkC[:, 0:128], in1=mask01, op=ALU.mult)
                    DVE.tensor_tensor(out=Mb, in0=LT, in1=bcast(bkC[:, 0:128], [128, 4, 128], 1), op=ALU.mult)
                    POOL.tensor_tensor(out=Cs, in0=Ebuf, in1=bcast(CT[:, ls], [128, 4, 128], 1), op=ALU.mult)
                    for m2 in range(2):
                        bkY = B.bank("mm")
                        for h2 in range(2):
                            hh = m2 * 2 + h2
                            PE.matmul(out=bkY[h2 * 64:(h2 + 1) * 64, 0:128], lhsT=xdt[:, hh * 64:(hh + 1) * 64], rhs=Mb[:, hh, :],
                                      start=True, stop=False)
                            PE.matmul(out=bkY[h2 * 64:(h2 + 1) * 64, 0:128], lhsT=prevT[:, hh * 64:(hh + 1) * 64], rhs=Cs[:, hh, :],
                                      start=False, stop=True)
                        DVE.scalar_tensor_tensor(out=y32[:, m2, ls], in0=xs[:, m2, ls], scalar=col("ssm_d_rep", j * 16 + 2 * g + m2),
                                                 in1=bkY[:, 0:128], op0=ALU.mult, op1=ALU.add)
                    bkS = B.bank("mm")
                    PE.matmul(out=bkS[:, 0:256], lhsT=Btm, rhs=dxdt, start=True, stop=True)
                    DVE.tensor_tensor(out=STg, in0=STg, in1=Ebuf[:, :, 127:128].to_broadcast([128, 4, 64]), op=ALU.mult)
                    DVE.tensor_tensor(out=STg, in0=STg, in1=bkS[:, 0:256].rearrange("p (a b) -> p a b", a=4), op=ALU.add)
                    ACT.copy(out=prevT, in_=STs[j][:, g, :])

                nch = n // 128 if c0 < TP else 0
                pend = stage_a(0) if nch else None
                for cl in range(nch):
                    nxt = stage_a(cl + 1) if cl + 1 < nch else None
                    stage_b(pend)
                    pend = nxt
                if last and c0 == 512:
                    bk = B.bank("aux")
                    for m2 in range(2):
                        PE.transpose(out=bk[:, m2 * 128:(m2 + 1) * 128], in_=STs[j][:, g, m2 * 128:(m2 + 1) * 128], identity=ident)
                    ACT.copy(out=stg, in_=bk[:, 0:256].rearrange("p (a b) -> p a b", a=2))
                    B.dma("stg", out=ns_p[j, g * 256:(g + 1) * 256, :].rearrange("(m q) n -> q m n", m=2), in_=stg)
                if c0 == TP:
                    sc = slice(TP, TP + NS)
                    R4s = ctmp[0][7]
                    RB = ctmp[0][5]
                    RC = ctmp[0][6]
                    tBs = [R4s[:, 0, :], R4s[:, 2, :]]
                    junk = R4s[:, 1, :]
                    bk = B.bank("aux")
                    for m2 in range(2):
                        PE.matmul(out=bk[:, m2 * 32:m2 * 32 + NS], lhsT=selhp[0:4, m2 * 128:(m2 + 1) * 128], rhs=tmpT[0:4, sc],
                                  start=True, stop=True)
                        PE.matmul(out=bk[:, 64 + m2 * 32:64 + m2 * 32 + NS], lhsT=selhp[0:4, m2 * 128:(m2 + 1) * 128],
                                  rhs=dtT[0:4, sc], start=True, stop=True)
                    for m2 in range(2):
                        ACT.activation(out=ea[:, m2, :], in_=bk[:, m2 * 32:m2 * 32 + NS], func=AF.Exp)
                        DVE.tensor_tensor(out=xdths[:, m2, :], in0=bk[:, 64 + m2 * 32:64 + m2 * 32 + NS], in1=xs[:, m2, 0:NS], op=ALU.mult)
                    bk2 = B.bank("aux")
                    PE.transpose(out=bk2[0:NS, 0:128], in_=B32[:, 0:NS], identity=ident)
                    PE.transpose(out=bk2[0:NS, 128:256], in_=C32s[:, 0:NS], identity=ident)
                    ACT.copy(out=BCs[0:NS, :], in_=bk2[0:NS, 0:256])
                    def s_load(s4):
                        S = S0[s4 % 2]
                        for sl in range(4):
                            B.dma(f"s0_{s4 % 2}", out=S[:, sl],
                                  in_=ssm_s[j, s4 * 4 + sl, g * 256:(g + 1) * 256, :].rearrange("(m q) n -> q m n", m=2))

                    s_load(0)
                    s_load(1)
                    for s4 in range(4):
                        idn = bcast(ident[0:NS, s4 * 4:(s4 + 1) * 4], [NS, 4, 128], 2)
                        DVE.tensor_tensor(out=RB[0:NS], in0=bcast(BCs[0:NS, 0:128], [NS, 4, 128], 1), in1=idn, op=ALU.mult)
                        DVE.tensor_tensor(out=RC[0:NS], in0=bcast(BCs[0:NS, 128:256], [NS, 4, 128], 1), in1=idn, op=ALU.mult)
                        bkB = B.bank("ssd")
                        bkC2 = B.bank("ssd")
                        PE.matmul(out=bkB[:, 0:512], lhsT=onesb[0:NS, 0:128], rhs=RB[0:NS].rearrange("p a b -> p (a b)"),
                                  start=True, stop=True)
                        PE.matmul(out=bkC2[:, 0:512], lhsT=onesb[0:NS, 0:128], rhs=RC[0:NS].rearrange("p a b -> p (a b)"),
                                  start=True, stop=True)
                        S = S0[s4 % 2]
                        key = f"s0_{s4 % 2}"
                        for sl in range(4):
                            sidx = s4 * 4 + sl
                            for m2 in range(2):
                                tB = tBs[m2]
                                DVE.tensor_scalar(out=tB, in0=bkB[:, sl * 128:(sl + 1) * 128], scalar1=xdths[:, m2, sidx:sidx + 1],
                                                  scalar2=None, op0=ALU.mult)
                                DVE.scalar_tensor_tensor(out=S[:, sl, m2, :], in0=S[:, sl, m2, :], scalar=ea[:, m2, sidx:sidx + 1],
                                                         in1=tB, op0=ALU.mult, op1=ALU.add)
                                DVE.scalar_tensor_tensor(out=junk, in0=S[:, sl, m2, :], scalar=1.0, in1=bkC2[:, sl * 128:(sl + 1) * 128],
                                                         op0=ALU.mult, op1=ALU.mult, accum_out=ysm[:, m2, sidx:sidx + 1])
                        for sl in range(4):
                            B.dma(key, out=ns_s[j, s4 * 4 + sl, g * 256:(g + 1) * 256, :].rearrange("(m q) n -> q m n", m=2), in_=S[:, sl])
                        if s4 + 2 < 4:
                            s_load(s4 + 2)
                    for m2 in range(2):
                        DVE.scalar_tensor_tensor(out=y32[:, m2, 0:NS], in0=xs[:, m2, 0:NS], scalar=col("ssm_d_rep", j * 16 + 2 * g + m2),
                                                 in1=ysm[:, m2, :], op0=ALU.mult, op1=ALU.add)
                DVE.tensor_tensor(out=t1[:, :, 0:n], in0=y32[:, :, 0:n], in1=sz[:, :, c0:c0 + n], op=ALU.mult)
                ACT.activation(out=sqb[:, 0:2, 0:n], in_=t1[:, :, 0:n], func=AF.Square)
                bk = B.bank("aux")
                for m in range(2):
                    PE.matmul(out=bk[:, 0:n], lhsT=onesb, rhs=sqb[:, m, 0:n], start=(m == 0), stop=(m == 1))
                ACT.activation(out=rstd[:, 0:n], in_=bk[:, 0:n], func=AF.Sqrt, scale=1.0 / 256.0, bias=epsc)
                DVE.reciprocal(out=rstd[:, 0:n], in_=rstd[:, 0:n])
                for q in range(2):
                    DVE.scalar_tensor_tensor(out=ygb[:, q, 0:n], in0=t1[:, q, 0:n], scalar=col("ssm_norm_g", j * 16 + 2 * g + q),
                                             in1=rstd[:, 0:n], op0=ALU.mult, op1=ALU.mult)
                for mo in range(8):
                    bk = B.bank("mm")
                    for k in range(2):
                        PE.matmul(out=bk[:, 0:n], lhsT=wo[:, k, mo * 128:(mo + 1) * 128], rhs=ygb[:, k, 0:n],
                                  start=(k == 0), stop=(k == 1))
                    DVE.tensor_tensor(out=h[:, mo, c0:c0 + n], in0=bk[:, 0:n], in1=h[:, mo, c0:c0 + n], op=ALU.add)
        if last and opts.get("mcut", 9) >= 1:
            for (src, wdt, dst) in ((xl32, SCW - 1, nsc_p[j]), (xs32s, NS, nsc_s[j, :, SCW - 2, :])):
                for pc in range(4):
                    o = xin[pc % 2]
                    for half in range(2):
                        bk = B.bank("aux")
                        for q in range(4):
                            PE.transpose(out=bk[0:wdt, q * 128:(q + 1) * 128], in_=src[:, pc * 8 + half * 4 + q, :], identity=ident)
                        ACT.copy(out=o[0:wdt, half * 512:(half + 1) * 512], in_=bk[0:wdt, :])
                    B.dma(f"xin{pc % 2}", out=dst[:, pc * 1024:(pc + 1) * 1024], in_=o[0:wdt, :])
        B.S.region = False
        B.pools = B.POOLS_DENSE
        B.release(mk)

    stage = opts.get("stage", "full")

    def program():
        for blk in range(NBLK):
            load_x(blk)
            for i in range(nlayers):
                if stage in ("s3", "full") and i % 2 == 0:
                    conformer(i, blk)
                if stage in ("s4", "full") and i % 2 == 1:
                    mamba(i, blk)
                if stage in ("s2", "s3", "full"):
                    ffn(i, blk)
                if stage in ("s2", "s3", "full"):
                    ple(i, blk)
            final_out(blk)

    B.S.dry = True
    ws.planning = True
    program()
    B.S.dry = False
    ws.planning = False
    B.bank_rr = {}
    cnt[0] = 0
    program()

    B.S.finalize(nc, B.stack)
    B.stack.close()
    return B


def _small_params(inp):
    rows = []
    idx = {}

    def add(name, arr):
        a = np.ascontiguousarray(arr, dtype=np.float32).reshape(-1)
        pad = (-a.size) % 128
        if pad:
            a = np.concatenate([a, np.zeros(pad, np.float32)])
        idx[name] = sum(r.shape[0] for r in rows)
        rows.append(a.reshape(-1, 128))

    for nm in ("g_mix", "g_ffn", "g_ple", "g_final", "cv_b_pw1", "cv_b_dw", "cv_ln_g", "cv_ln_b", "cv_b_pw2", "cv_w_dw"):
        add(nm, inp[nm])
    for nm in ("ssm_b_conv", "ssm_w_conv", "ssm_norm_g"):
        add(nm, inp[nm])
    add("ssm_d_rep", np.repeat(np.asarray(inp["ssm_d"]), HD, axis=1))
    for nm in ("ssm_dt_bias", "ssm_a_log"):
        a = np.zeros((2, NG, 128), np.float32)
        a[:, :, 0:4] = np.asarray(inp[nm]).reshape(2, NG, 4)
        add(nm + "_g", a)
    return np.concatenate(rows, 0), idx


def _consts():
    c = np.zeros((128, 128 + 512 + 4096 + 1024 + 128), np.float32)
    c[:, 0:128] = np.eye(128, dtype=np.float32)
    j = np.arange(128)[:, None]
    i = np.arange(128)[None, :]
    neg = np.where(i < j, -30000.0, 0.0).astype(np.float32)
    c[:, 128:128 + 512] = np.tile(neg, (1, 4))
    sel = np.zeros((128, 2, 128), np.float32)
    for m2 in range(2):
        sel[2 * m2, m2, 0:64] = 1.0
        sel[2 * m2 + 1, m2, 64:128] = 1.0
    c[:, 640:640 + 256] = sel.reshape(128, 256)
    rm = np.ones((128, 1024), np.float32)
    rm[:, ::128] = 0.0
    c[:, 4736:4736 + 1024] = rm
    c[:, 5760:5888] = 1.0
    return c


_CACHE = {}


_NEED = {
    "s1": {"x_p", "x_s", "consts", "smallp"},
    "s2": {"x_p", "x_s", "consts", "smallp", "p_p", "p_s", "ffn_w_gate", "ffn_w_up", "ffn_w_down", "ple_w_proj", "ple_w_gate"},
    "s3": {"x_p", "x_s", "consts", "smallp", "p_p", "p_s", "ffn_w_gate", "ffn_w_up", "ffn_w_down", "ple_w_proj", "ple_w_gate",
           "cv_w_pw1", "cv_w_pw2", "conv_s"},
    "s4": {"x_p", "x_s", "consts", "smallp", "ssm_w_in", "ssm_w_out", "ssm_s", "ssmc_s"},
}


def kernel(**inp):
    opts = dict(inp.pop("_opts", {}) or {})
    if opts.get("stage", "full") in _NEED:
        opts["need"] = _NEED[opts.get("need_as", opts["stage"])]
    smallp, colidx = _small_params(inp)
    opts["ncolrows"] = smallp.shape[0]
    opts["colidx"] = colidx
    B = build(opts)
    global _LASTB
    _LASTB = B
    cst = _consts()
    in_maps = []
    for c in range(NCORES):
        m = {
            "x_p": np.ascontiguousarray(inp["x_prompt"][c]),
            "x_s": np.ascontiguousarray(inp["x_sample"][c * NS:(c + 1) * NS, 0, :]),
            "consts": cst,
            "smallp": smallp,
            "p_p": np.ascontiguousarray(inp["p_prompt"][:, c]),
            "p_s": np.ascontiguousarray(inp["p_sample"][:, c * NS:(c + 1) * NS, 0, :]),
        }
        m["conv_s"] = np.ascontiguousarray(inp["state_conv_mixer"][:, c * NS:(c + 1) * NS])
        m["ssm_s"] = np.ascontiguousarray(inp["state_ssm"][:, c * NS:(c + 1) * NS]).reshape(2, NS, NH * HD, DST)
        m["ssmc_s"] = np.ascontiguousarray(inp["state_ssm_conv"][:, c * NS:(c + 1) * NS])
        for wn in ("ffn_w_gate", "ffn_w_up", "ffn_w_down", "ple_w_proj", "ple_w_gate", "cv_w_pw1", "cv_w_pw2",
                   "ssm_w_in", "ssm_w_out"):
            m[wn] = inp[wn]
        in_maps.append({k: (v if tuple(v.shape) == B.in_shapes[k] else np.zeros(B.in_shapes[k], np.float32))
                        for k, v in m.items() if k in B.ins})
    ncr = opts.get("ncores", NCORES)
    res = run_bass_kernel_spmd(B.nc, in_maps[:ncr], core_ids=list(range(ncr)))
    R = list(res.results) + [res.results[0]] * (NCORES - ncr)
    y_prompt = np.stack([R[c]["y_p"] for c in range(NCORES)], 0)
    y_sample = np.concatenate([R[c]["y_s"] for c in range(NCORES)], 0)[:, None, :]
    nc_p = np.stack([R[c]["nc_p"] for c in range(NCORES)], 1)
    nc_s = np.concatenate([R[c]["nc_s"] for c in range(NCORES)], 1)
    ns_p = np.stack([R[c]["ns_p"] for c in range(NCORES)], 1).reshape(2, NCORES, NH, HD, DST)
    nsc_p = np.stack([R[c]["nsc_p"] for c in range(NCORES)], 1)
    ns_s = np.concatenate([R[c]["ns_s"] for c in range(NCORES)], 1).reshape(2, NCORES * NS, NH, HD, DST)
    nsc_s = np.concatenate([R[c]["nsc_s"] for c in range(NCORES)], 1)
    return (y_prompt, y_sample, nc_p, ns_p, nsc_p, nc_s, ns_s, nsc_s)
```

```python
import os
import numpy as np
from contextlib import ExitStack
import concourse.bass as bass
import concourse.mybir as mybir
from concourse.ap import AP as _AP
from concourse.bass_utils import run_bass_kernel_spmd

F32 = mybir.dt.float32
BF16 = mybir.dt.bfloat16
ALU = mybir.AluOpType
AF = mybir.ActivationFunctionType
ISZ = {F32: 4, BF16: 2, mybir.dt.float32r: 4}

NCORES = 8
D = 1024
SEQ = 2048
DEPTH = 4
NBLK = 2
TP = 1024
NS = 16
PLE = 256
DFF = 2816
CW = 31
DIN = 2048
NH = 32
HD = 64
NG = 8
DST = 128
SCW = 4
SCD = 4096
SIN = DIN + SCD + NH
EPS = 1e-6
COMPUTE = ("pe", "act", "dve", "pool")
SAME_ENGINE_SYNC = True


class _Op:
    __slots__ = ("meth", "kw", "waits", "sig", "sigidx", "dmakey", "dmaidx")


class Sched:
    def __init__(self):
        self.ops = {e: [] for e in ("pe", "act", "dve", "pool", "sp")}
        self.recs = {"SB": [], "PSUM": []}
        self.known = {e: {} for e in self.ops}
        self.dmacnt = {}
        self.base = {}

    def _box(self, ap):
        isz = ISZ[ap.dtype]
        pat = ap.ap
        pstride = pat[0][0] * isz
        off = ap.offset * isz
        p0 = off // pstride
        f0 = off % pstride + self.base[ap.name]
        ext = 0
        for st, cnt in pat[1:]:
            ext += abs(st) * (cnt - 1)
        p1 = p0 + pat[0][1]
        f1 = f0 + (ext + 1) * isz
        if str(ap.space) == "PSUM":
            p0 = p0 // 32 * 32
            p1 = (p1 + 31) // 32 * 32
            f0 = f0 // 2048 * 2048
            f1 = f0 + 2048
        return (p0, p1, f0, f1)

    dry = False

    region = False
    rcount = 0
    rlimit = 10 ** 9

    def emit(self, eng, meth, dmakey=None, **kw):
        if self.dry:
            return None
        if self.region:
            self.rcount += 1
            if self.rcount > self.rlimit:
                return None
        op = _Op()
        op.meth = meth
        op.kw = kw
        op.sig = False
        op.dmakey = dmakey
        lst = self.ops[eng]
        if dmakey is not None:
            idx = self.dmacnt.get(dmakey, 0)
            self.dmacnt[dmakey] = idx + 1
            op.dmaidx = idx
            prod, pseq = "dma:" + dmakey, idx
        else:
            prod, pseq = eng, len(lst)
        deps = {}
        acc = []
        for key, v in kw.items():
            if not isinstance(v, _AP):
                continue
            sp = str(v.space)
            if sp not in ("SB", "PSUM"):
                continue
            isw = key in ("out", "accum_out", "ap")
            box = self._box(v)
            acc.append((sp, box, isw))
            for r in self.recs[sp]:
                if r[0] < box[1] and box[0] < r[1] and r[2] < box[3] and box[2] < r[3]:
                    if isw or r[4] or (sp == "PSUM" and r[5] != prod):
                        rp, rs = r[5], r[6]
                        if rp == prod:
                            if not rp.startswith("dma:"):
                                if eng == "pe" or not SAME_ENGINE_SYNC:
                                    continue
                                if not r[4]:
                                    continue
                            if rs == pseq:
                                continue
                        if rp.startswith("dma:"):
                            rs = self.dmacnt[rp[4:]] - 1 - (1 if rp == prod else 0)
                        if deps.get(rp, -1) < rs:
                            deps[rp] = rs
        kn = self.known[eng]
        waits = []
        for rp, rs in deps.items():
            if kn.get(rp, -1) >= rs:
                continue
            kn[rp] = rs
            waits.append((rp, rs))
            if not rp.startswith("dma:"):
                self.ops[rp][rs].sig = True
        op.waits = waits
        for sp, box, isw in acc:
            l2 = self.recs[sp]
            if isw:
                l2[:] = [r for r in l2 if not (box[0] <= r[0] and r[1] <= box[1] and box[2] <= r[2] and r[3] <= box[3])]
            else:
                l2[:] = [r for r in l2 if not (r[5] == prod and not r[4] and box[0] <= r[0] and r[1] <= box[1]
                                               and box[2] <= r[2] and r[3] <= box[3])]
            l2.append((box[0], box[1], box[2], box[3], isw, prod, pseq))
        lst.append(op)
        return op

    def finalize(self, nc, stack):
        CH = 16000
        sems = {}
        for e in COMPUTE:
            n = 0
            for op in self.ops[e]:
                if op.sig:
                    op.sigidx = n
                    n += 1
            sems[e] = [stack.enter_context(nc.semaphore(f"s_{e}_{i}")) for i in range(n // CH + 1)]
        for key in self.dmacnt:
            sems["dma:" + key] = stack.enter_context(nc.semaphore("d_" + key))
        block = stack.enter_context(nc.Block())
        ops = self.ops
        dmacnt = self.dmacnt

        def body(ename, final=False):
            def f(eng):
                for op in ops[ename]:
                    for rp, rs in op.waits:
                        if rp.startswith("dma:"):
                            eng.wait_ge(sems[rp], 16 * (rs + 1))
                        else:
                            si = ops[rp][rs].sigidx
                            eng.wait_ge(sems[rp][si // CH], si % CH + 1)
                    ins = getattr(eng, op.meth)(**op.kw)
                    if op.dmakey is not None:
                        ins.then_inc(sems["dma:" + op.dmakey], 16)
                    elif op.sig:
                        ins.then_inc(sems[ename][op.sigidx // CH], 1)
                if final:
                    for key, cnt in dmacnt.items():
                        eng.wait_ge(sems["dma:" + key], 16 * cnt)
            return f

        block.tensor(body("pe"))
        block.scalar(body("act"))
        block.vector(body("dve"))
        block.gpsimd(body("pool"))
        block.sync(body("sp", final=True))


class Eng:
    def __init__(self, sched, name):
        self.s, self.n = sched, name

    def __getattr__(self, meth):
        def f(**kw):
            return self.s.emit(self.n, meth, **kw)
        return f


class Builder:
    def __init__(self, opts):
        self.opts = opts
        self.nc = bass.Bass("TRN2", target_bir_lowering=False)
        self.stack = ExitStack()
        self.S = Sched()
        self.PE, self.ACT, self.DVE, self.POOL = (Eng(self.S, e) for e in COMPUTE)
        self.ins = {}
        self.in_shapes = {}
        self.outs = {}
        self.arena_off = 0

    def din(self, name, shape):
        need = self.opts.get("need")
        if need is not None and name not in need:
            shape = [1] * len(shape)
        self.in_shapes[name] = tuple(shape)
        t = self.nc.dram_tensor(name, list(shape), F32, kind="ExternalInput").ap()
        self.ins[name] = t
        return t

    def dout(self, name, shape):
        t = self.nc.dram_tensor(name, list(shape), F32, kind="ExternalOutput").ap()
        self.outs[name] = t
        return t

    def dma(self, key, out, in_):
        return self.S.emit("sp", "dma_start", dmakey=key, out=out, in_=in_)

    def setup_mem(self):
        nc = self.nc
        self.ARENA_BYTES = 212480
        self.arena = self.stack.enter_context(nc.sbuf_tensor("arena", [128, self.ARENA_BYTES // 4], F32))
        self.S.base["arena"] = 0
        self.banks = []
        for i in range(8):
            b = self.stack.enter_context(nc.psum_tensor(f"bank{i}", [128, 512], F32))
            self.S.base[f"bank{i}"] = i * 2048
            self.banks.append(b)
        self.bank_rr = {}
        self.POOLS_DENSE = {"mm": (0, 1, 2, 3, 4), "aux": (5, 6, 7)}
        self.POOLS_SSD = {"mm": (0, 1, 2, 6), "aux": (5,), "ssdC": (3, 4), "ssdE": (7,), "ssd": (7, 6)}
        self.pools = self.POOLS_DENSE

    def alloc(self, shape, dtype):
        n = 1
        for s in shape:
            n *= s
        nbytes = (n * ISZ[dtype] + 31) // 32 * 32
        off = self.arena_off
        assert off + nbytes <= self.ARENA_BYTES, ("SBUF arena overflow", off, nbytes)
        self.arena_off = off + nbytes
        v = self.arena[:, off // 4:(off + nbytes) // 4]
        if dtype != F32:
            v = v.bitcast(dtype)
        v = v[:, 0:n]
        if len(shape) == 2:
            v = v.rearrange("p (a b) -> p a b", a=shape[0])
        elif len(shape) == 3:
            v = v.rearrange("p (a b c) -> p a b c", a=shape[0], b=shape[1])
        elif len(shape) == 4:
            v = v.rearrange("p (a b c d) -> p a b c d", a=shape[0], b=shape[1], c=shape[2])
        return v

    def mark(self):
        return self.arena_off

    def release(self, m):
        self.arena_off = m

    def bank(self, pool):
        pools = self.pools
        ids = pools[pool]
        i = self.bank_rr.get(pool, 0)
        self.bank_rr[pool] = i + 1
        return self.banks[ids[i % len(ids)]]


class WStream:
    NST, NWB, CAP = 2, 3, 2112

    def __init__(self, B):
        self.B = B
        self.st = [B.alloc([self.CAP], F32) for _ in range(self.NST)]
        self.wb = [B.alloc([self.CAP], BF16) for _ in range(self.NWB)]
        self.plan = []
        self.planning = True
        self.nl = 0
        self.nu = 0

    def _views(self, buf, pieces):
        out = []
        off = 0
        for src, kc in pieces:
            n = src.shape[1]
            out.append(buf[:, off:off + kc * n].rearrange("p (k n) -> p k n", k=kc))
            off += kc * n
        assert off <= self.CAP, off
        return out

    def _load(self, i):
        name, pieces = self.plan[i]
        sv = self._views(self.st[i % self.NST], pieces)
        for (src, kc), v in zip(pieces, sv):
            self.B.dma(f"w{i % self.NST}", out=v, in_=src.rearrange("(k p) n -> p k n", p=128))
        tot = sum(kc * src.shape[1] for src, kc in pieces)
        self.B.POOL.tensor_copy(out=self.wb[i % self.NWB][:, 0:tot], in_=self.st[i % self.NST][:, 0:tot])

    def get(self, name, pieces):
        if self.planning:
            self.plan.append((name, pieces))
            return self._views(self.wb[0], pieces)
        i = self.nu
        assert self.plan[i][0] == name, (self.plan[i][0], name)
        while self.nl < min(len(self.plan), i + 3):
            self._load(self.nl)
            self.nl += 1
        self.nu += 1
        return self._views(self.wb[i % self.NWB], pieces)


def tiles_of(blk):
    t = [(0, 512), (512, 512)]
    if blk == NBLK - 1:
        t.append((1024, NS))
    return t


def build(opts):
    B = Builder(opts)
    nc = B.nc
    PE, ACT, DVE, POOL = B.PE, B.ACT, B.DVE, B.POOL
    nlayers = opts.get("nlayers", DEPTH)
    x_p = B.din("x_p", [SEQ, D])
    x_s = B.din("x_s", [NS, D])
    consts = B.din("consts", [128, 128 + 512 + 4096 + 1024 + 128])
    NCOLROWS = opts["ncolrows"]
    smallp = B.din("smallp", [NCOLROWS, 128])
    g_cols = opts["colidx"]
    y_p = B.dout("y_p", [SEQ, D])
    y_s = B.dout("y_s", [NS, D])
    p_p = B.din("p_p", [DEPTH, SEQ, PLE])
    p_s = B.din("p_s", [DEPTH, NS, PLE])
    wg = B.din("ffn_w_gate", [DEPTH, D, DFF])
    wu = B.din("ffn_w_up", [DEPTH, D, DFF])
    wd = B.din("ffn_w_down", [DEPTH, DFF, D])
    wpp = B.din("ple_w_proj", [DEPTH, PLE, D])
    wpg = B.din("ple_w_gate", [DEPTH, D, D])
    w1 = B.din("cv_w_pw1", [2, D, 2 * D])
    w2 = B.din("cv_w_pw2", [2, D, D])
    conv_s = B.din("conv_s", [2, NS, CW - 1, D])
    win = B.din("ssm_w_in", [2, D, SIN])
    wout = B.din("ssm_w_out", [2, DIN, D])
    ssm_s = B.din("ssm_s", [2, NS, NH * HD, DST])
    ssmc_s = B.din("ssmc_s", [2, NS, SCW - 1, SCD])
    ns_p = B.dout("ns_p", [2, NH * HD, DST])
    nsc_p = B.dout("nsc_p", [2, SCW - 1, SCD])
    ns_s = B.dout("ns_s", [2, NS, NH * HD, DST])
    nsc_s = B.dout("nsc_s", [2, NS, SCW - 1, SCD])
    nc_p = B.dout("nc_p", [2, CW - 1, D])
    nc_s = B.dout("nc_s", [2, NS, CW - 1, D])

    B.setup_mem()
    T = TP + NS
    h = B.alloc([8, T], F32)
    u = B.alloc([8, T], BF16)
    ident = B.alloc([128], F32)
    onesb = B.alloc([128], BF16)
    ncolt = (NCOLROWS + 127) // 128
    cols = B.alloc([ncolt * 128], F32)
    identb = B.alloc([128], BF16)
    vprefix = [B.alloc([8, CW - 1], BF16) for _ in range(2)]
    xin = [B.alloc([D], F32) for _ in range(2)]
    rstd = B.alloc([512], F32)
    sqb = B.alloc([8, 512], BF16)
    ws = WStream(B)
    scr4 = B.alloc([4, 512], F32)
    sgb = [scr4[:, 0, :], scr4[:, 1, :]]
    tmpb = [scr4[:, 2, :], scr4[:, 3, :]]
    scr_bf = B.arena[:, (B.arena_off - 4 * 512 * 4) // 4:B.arena_off // 4].bitcast(BF16).rearrange("p (a b) -> p a b", a=8)
    STs = [B.alloc([NG, 256], F32) for _ in range(2)]
    xprefix = [B.alloc([32, SCW], BF16) for _ in range(2)]
    cnt = [0]

    B.dma("c0", out=ident, in_=consts[:, 0:128])
    DVE.memset(ap=onesb, constant=1.0)
    DVE.tensor_copy(out=identb, in_=ident)
    for t in range(ncolt):
        r0 = t * 128
        nr = min(128, NCOLROWS - r0)
        st = xin[t % 2]
        B.dma(f"xin{t % 2}", out=st[0:nr, 0:128], in_=smallp[r0:r0 + nr, :])
        bk = B.bank("aux")
        PE.transpose(out=bk[:, 0:nr], in_=st[0:nr, 0:128], identity=ident[0:nr, 0:nr])
        DVE.tensor_copy(out=cols[:, r0:r0 + nr], in_=bk[:, 0:nr])

    def col(name, i=0):
        c = g_cols[name] + i
        return cols[:, c:c + 1]

    def load_x(blk):
        for tc in range(8):
            st = xin[tc % 2]
            B.dma(f"xin{tc % 2}", out=st, in_=x_p[blk * TP + tc * 128: blk * TP + (tc + 1) * 128, :])
            for half in range(2):
                bk = B.bank("aux")
                for q in range(4):
                    m = half * 4 + q
                    PE.transpose(out=bk[:, q * 128:(q + 1) * 128], in_=st[:, m * 128:(m + 1) * 128], identity=ident)
                ACT.copy(out=h[:, half * 4:half * 4 + 4, tc * 128:(tc + 1) * 128],
                         in_=bk[:, :].rearrange("p (q t) -> p q t", q=4))
        if blk == NBLK - 1:
            st = xin[0]
            B.dma("xin0", out=st[0:NS, :], in_=x_s)
            for half in range(2):
                bk = B.bank("aux")
                for q in range(4):
                    m = half * 4 + q
                    PE.transpose(out=bk[:, q * NS:(q + 1) * NS], in_=st[0:NS, m * 128:(m + 1) * 128],
                                 identity=ident[0:NS, 0:NS])
                ACT.copy(out=h[:, half * 4:half * 4 + 4, TP:TP + NS],
                         in_=bk[:, 0:4 * NS].rearrange("p (q t) -> p q t", q=4))

    def rms_stats(src, nch, c0, n, dim):
        ACT.activation(out=sqb[:, 0:nch, 0:n], in_=src[:, 0:nch, c0:c0 + n], func=AF.Square)
        bk = B.bank("aux")
        for m in range(nch):
            PE.matmul(out=bk[:, 0:n], lhsT=onesb, rhs=sqb[:, m, 0:n], start=(m == 0), stop=(m == nch - 1))
        ACT.activation(out=rstd[:, 0:n], in_=bk[:, 0:n], func=AF.Sqrt, scale=1.0 / dim, bias=epsc)
        DVE.reciprocal(out=rstd[:, 0:n], in_=rstd[:, 0:n])

    epsc = B.alloc([1], F32)
    DVE.memset(ap=epsc, constant=EPS)

    def rmsnorm_to(dst, gname, gi, blk):
        for (c0, n) in tiles_of(blk):
            rms_stats(h, 8, c0, n, D)
            for m in range(8):
                DVE.scalar_tensor_tensor(out=dst[:, m, c0:c0 + n], in0=h[:, m, c0:c0 + n], scalar=col(gname, gi * 8 + m),
                                         in1=rstd[:, 0:n], op0=ALU.mult, op1=ALU.mult)

    def final_out(blk):
        mk = B.mark()
        yv = B.alloc([8, 512], F32)
        ost = [B.alloc([D], F32) for _ in range(2)]
        cnt = 0
        for (c0, n) in tiles_of(blk):
            rms_stats(h, 8, c0, n, D)
            for m in range(8):
                DVE.scalar_tensor_tensor(out=yv[:, m, 0:n], in0=h[:, m, c0:c0 + n], scalar=col("g_final", m),
                                         in1=rstd[:, 0:n], op0=ALU.mult, op1=ALU.mult)
            ntc = (n + 127) // 128
            for tc in range(ntc):
                w = min(128, n - tc * 128)
                o = ost[cnt % 2]
                for half in range(2):
                    bk = B.bank("aux")
                    for q in range(4):
                        m = half * 4 + q
                        PE.transpose(out=bk[0:w, q * 128:(q + 1) * 128], in_=yv[:, m, tc * 128:tc * 128 + w],
                                     identity=ident)
                    ACT.copy(out=o[0:w, half * 512:(half + 1) * 512], in_=bk[0:w, :])
                if c0 < TP:
                    r0 = blk * TP + c0 + tc * 128
                    B.dma(f"ost{cnt % 2}", out=y_p[r0:r0 + w, :], in_=o[0:w, :])
                else:
                    B.dma(f"ost{cnt % 2}", out=y_s[:, :], in_=o[0:w, :])
                cnt += 1
        B.release(mk)

    def ffn(i, blk):
        tiles = tiles_of(blk)
        rmsnorm_to(u, "g_ffn", i, blk)
        mk = B.mark()
        hid = B.alloc([11, T], BF16)
        for half in range(2):
            f0 = half * 11
            for fa in range(0, 11, 1):
                nf = 1
                cs = slice((f0 + fa) * 128, (f0 + fa + nf) * 128)
                wgv, wuv = ws.get(f"ffn_gu{i}_{half}_{fa}", [(wg[i][:, cs], 8), (wu[i][:, cs], 8)])
                for q in range(nf):
                    f = fa + q
                    for (c0, n) in tiles:
                        bg = B.bank("mm")
                        bu = B.bank("mm")
                        for k in range(8):
                            PE.matmul(out=bg[:, 0:n], lhsT=wgv[:, k, q * 128:(q + 1) * 128], rhs=u[:, k, c0:c0 + n],
                                      start=(k == 0), stop=(k == 7))
                        for k in range(8):
                            PE.matmul(out=bu[:, 0:n], lhsT=wuv[:, k, q * 128:(q + 1) * 128], rhs=u[:, k, c0:c0 + n],
                                      start=(k == 0), stop=(k == 7))
                        sg = sgb[cnt[0] % 2]
                        cnt[0] += 1
                        ACT.activation(out=sg[:, 0:n], in_=bg[:, 0:n], func=AF.Silu)
                        DVE.tensor_tensor(out=hid[:, f, c0:c0 + n], in0=bu[:, 0:n], in1=sg[:, 0:n], op=ALU.mult)
            for mo2 in range(8):
                (wdv,) = ws.get(f"ffn_d{i}_{half}_{mo2}",
                                [(wd[i][f0 * 128:(f0 + 11) * 128, mo2 * 128:(mo2 + 1) * 128], 11)])
                for q in range(1):
                    mo = mo2 + q
                    for (c0, n) in tiles:
                        bk = B.bank("mm")
                        for k in range(11):
                            PE.matmul(out=bk[:, 0:n], lhsT=wdv[:, k, q * 128:(q + 1) * 128], rhs=hid[:, k, c0:c0 + n],
                                      start=(k == 0), stop=(k == 10))
                        DVE.tensor_tensor(out=h[:, mo, c0:c0 + n], in0=bk[:, 0:n], in1=h[:, mo, c0:c0 + n], op=ALU.add)
        B.release(mk)

    def ple(i, blk):
        tiles = tiles_of(blk)
        mk = B.mark()
        pT = B.alloc([2, T], BF16)
        for tc in range(8):
            st = xin[tc % 2]
            r0 = blk * TP + tc * 128
            B.dma(f"xin{tc % 2}", out=st[:, 0:PLE], in_=p_p[i, r0:r0 + 128, :])
            bk = B.bank("aux")
            for q in range(2):
                PE.transpose(out=bk[:, q * 128:(q + 1) * 128], in_=st[:, q * 128:(q + 1) * 128], identity=ident)
            ACT.copy(out=pT[:, 0:2, tc * 128:(tc + 1) * 128], in_=bk[:, 0:256].rearrange("p (q t) -> p q t", q=2))
        if blk == NBLK - 1:
            st = xin[0]
            B.dma("xin0", out=st[0:NS, 0:PLE], in_=p_s[i])
            bk = B.bank("aux")
            for q in range(2):
                PE.transpose(out=bk[:, q * NS:(q + 1) * NS], in_=st[0:NS, q * 128:(q + 1) * 128],
                             identity=ident[0:NS, 0:NS])
            ACT.copy(out=pT[:, 0:2, TP:TP + NS], in_=bk[:, 0:2 * NS].rearrange("p (q t) -> p q t", q=2))
        rmsnorm_to(u, "g_ple", i, blk)
        for mo2 in range(8):
            cs = slice(mo2 * 128, (mo2 + 1) * 128)
            wgv, wpv = ws.get(f"ple{i}_{mo2}", [(wpg[i][:, cs], 8), (wpp[i][:, cs], 2)])
            for q in range(1):
                mo = mo2 + q
                for (c0, n) in tiles:
                    bg = B.bank("mm")
                    bp = B.bank("mm")
                    for k in range(8):
                        PE.matmul(out=bg[:, 0:n], lhsT=wgv[:, k, q * 128:(q + 1) * 128], rhs=u[:, k, c0:c0 + n],
                                  start=(k == 0), stop=(k == 7))
                    for k in range(2):
                        PE.matmul(out=bp[:, 0:n], lhsT=wpv[:, k, q * 128:(q + 1) * 128], rhs=pT[:, k, c0:c0 + n],
                                  start=(k == 0), stop=(k == 1))
                    sg = sgb[cnt[0] % 2]
                    tm = tmpb[cnt[0] % 2]
                    cnt[0] += 1
                    ACT.activation(out=sg[:, 0:n], in_=bg[:, 0:n], func=AF.Sigmoid)
                    DVE.tensor_tensor(out=tm[:, 0:n], in0=bp[:, 0:n], in1=sg[:, 0:n], op=ALU.mult)
                    DVE.tensor_tensor(out=h[:, mo, c0:c0 + n], in0=tm[:, 0:n], in1=h[:, mo, c0:c0 + n], op=ALU.add)
        B.release(mk)

    def bc8(ap2, n):
        return ap2[:, 0:n].unsqueeze(1).to_broadcast([128, 8, n])

    def conformer(i, blk):
        j = i // 2
        tiles = tiles_of(blk)
        last = blk == NBLK - 1
        rmsnorm_to(u, "g_mix", i, blk)
        mk = B.mark()
        v = B.alloc([8, CW - 1 + T], BF16)
        c32 = B.alloc([8, T], F32)
        diag = [B.alloc([CW, 128], BF16)]
        cb = scr_bf
        mean = B.alloc([512], F32)
        msq = B.alloc([512], F32)
        v32 = B.alloc([8, CW - 1], F32)
        vs32 = B.alloc([8, NS], F32)
        if blk == 0:
            DVE.memset(ap=v[:, :, 0:CW - 1], constant=0.0)
        else:
            DVE.tensor_copy(out=v[:, :, 0:CW - 1], in_=vprefix[j])
        for mo in range(8):
            wa, wgt = ws.get(f"cv1_{i}_{mo}", [(w1[j][:, mo * 128:(mo + 1) * 128], 8),
                                              (w1[j][:, D + mo * 128:D + (mo + 1) * 128], 8)])
            for (c0, n) in tiles:
                ba = B.bank("mm")
                bg = B.bank("mm")
                for k in range(8):
                    PE.matmul(out=ba[:, 0:n], lhsT=wa[:, k, :], rhs=u[:, k, c0:c0 + n], start=(k == 0), stop=(k == 7))
                for k in range(8):
                    PE.matmul(out=bg[:, 0:n], lhsT=wgt[:, k, :], rhs=u[:, k, c0:c0 + n], start=(k == 0), stop=(k == 7))
                sg = sgb[cnt[0] % 2]
                cnt[0] += 1
                ACT.activation(out=sg[:, 0:n], in_=bg[:, 0:n], func=AF.Sigmoid, bias=col("cv_b_pw1", j * 16 + 8 + mo))
                ba_col = col("cv_b_pw1", j * 16 + mo)
                DVE.scalar_tensor_tensor(out=v[:, mo, CW - 1 + c0:CW - 1 + c0 + n], in0=ba[:, 0:n], scalar=ba_col,
                                         in1=sg[:, 0:n], op0=ALU.add, op1=ALU.mult)
                if last and c0 == 512:
                    DVE.scalar_tensor_tensor(out=v32[:, mo, :], in0=ba[:, 512 - (CW - 1):512], scalar=ba_col,
                                             in1=sg[:, 512 - (CW - 1):512], op0=ALU.add, op1=ALU.mult)
                if last and c0 == TP:
                    DVE.scalar_tensor_tensor(out=vs32[:, mo, :], in0=ba[:, 0:NS], scalar=ba_col,
                                             in1=sg[:, 0:NS], op0=ALU.add, op1=ALU.mult)
        if not last:
            DVE.tensor_copy(out=vprefix[j], in_=v[:, :, TP:TP + CW - 1])
        stt = None
        if last:
            stt = B.alloc([8, NS, CW - 1], BF16)
            for s4 in range(4):
                st = xin[s4 % 2]
                B.dma(f"xin{s4 % 2}", out=st[0:120, :], in_=conv_s[j, s4 * 4:(s4 + 1) * 4].rearrange("s k c -> (s k) c"))
                for half in range(2):
                    bk = B.bank("aux")
                    for q in range(4):
                        PE.transpose(out=bk[:, q * 120:(q + 1) * 120], in_=st[0:120, (half * 4 + q) * 128:(half * 4 + q + 1) * 128],
                                     identity=ident[0:120, 0:120])
                    ACT.copy(out=stt[:, half * 4:half * 4 + 4, s4 * 4:(s4 + 1) * 4, :].rearrange("p m s k -> p m (s k)"),
                             in_=bk[:, 0:480].rearrange("p (q t) -> p q t", q=4))
            B.dma("ncs", out=nc_s[j, :, 0:CW - 2, :], in_=conv_s[j, :, 1:CW - 1, :])
            for (src, wdt, dst) in ((v32, CW - 1, nc_p[j]), (vs32, NS, nc_s[j, :, CW - 2, :])):
                o = xin[0] if wdt == NS else xin[1]
                for half in range(2):
                    bk = B.bank("aux")
                    for q in range(4):
                        PE.transpose(out=bk[0:wdt, q * 128:(q + 1) * 128], in_=src[:, half * 4 + q, :], identity=ident)
                    ACT.copy(out=o[0:wdt, half * 512:(half + 1) * 512], in_=bk[0:wdt, :])
                B.dma("xin0" if wdt == NS else "xin1", out=dst, in_=o[0:wdt, :])
        for mo in range(8):
            dg = diag[0]
            for k in range(CW):
                DVE.tensor_scalar(out=dg[:, k, :], in0=identb, scalar1=col("cv_w_dw", (j * CW + k) * 8 + mo), scalar2=None,
                                  op0=ALU.mult)
            for (c0, n) in tiles:
                bk = B.bank("mm")
                if c0 < TP:
                    for k in range(CW):
                        PE.matmul(out=bk[:, 0:n], lhsT=dg[:, k, :], rhs=v[:, mo, c0 + k:c0 + k + n],
                                  start=(k == 0), stop=(k == CW - 1))
                else:
                    for k in range(CW - 1):
                        PE.matmul(out=bk[:, 0:NS], lhsT=dg[:, k, :], rhs=stt[:, mo, :, k], start=(k == 0), stop=False)
                    PE.matmul(out=bk[:, 0:NS], lhsT=dg[:, CW - 1, :], rhs=v[:, mo, CW - 1 + TP:CW - 1 + TP + NS],
                              start=False, stop=True)
                ACT.activation(out=c32[:, mo, c0:c0 + n], in_=bk[:, 0:n], func=AF.Identity, bias=col("cv_b_dw", j * 8 + mo))
        for (c0, n) in tiles:
            ACT.activation(out=sqb[:, :, 0:n], in_=c32[:, :, c0:c0 + n], func=AF.Square)
            DVE.tensor_copy(out=cb[:, :, 0:n], in_=c32[:, :, c0:c0 + n])
            bs = B.bank("aux")
            bq = B.bank("aux")
            for m in range(8):
                PE.matmul(out=bs[:, 0:n], lhsT=onesb, rhs=cb[:, m, 0:n], start=(m == 0), stop=(m == 7))
            for m in range(8):
                PE.matmul(out=bq[:, 0:n], lhsT=onesb, rhs=sqb[:, m, 0:n], start=(m == 0), stop=(m == 7))
            DVE.tensor_scalar(out=mean[:, 0:n], in0=bs[:, 0:n], scalar1=1.0 / D, scalar2=None, op0=ALU.mult)
            DVE.tensor_tensor(out=msq[:, 0:n], in0=mean[:, 0:n], in1=mean[:, 0:n], op=ALU.mult)
            DVE.scalar_tensor_tensor(out=msq[:, 0:n], in0=bq[:, 0:n], scalar=1.0 / D, in1=msq[:, 0:n],
                                     op0=ALU.mult, op1=ALU.subtract)
            ACT.activation(out=rstd[:, 0:n], in_=msq[:, 0:n], func=AF.Sqrt, bias=epsc)
            DVE.reciprocal(out=rstd[:, 0:n], in_=rstd[:, 0:n])
            DVE.tensor_tensor(out=c32[:, :, c0:c0 + n], in0=c32[:, :, c0:c0 + n], in1=bc8(mean, n), op=ALU.subtract)
            DVE.tensor_tensor(out=c32[:, :, c0:c0 + n], in0=c32[:, :, c0:c0 + n], in1=bc8(rstd, n), op=ALU.mult)
            for m in range(8):
                ACT.activation(out=u[:, m, c0:c0 + n], in_=c32[:, m, c0:c0 + n], func=AF.Silu,
                               scale=col("cv_ln_g", j * 8 + m), bias=col("cv_ln_b", j * 8 + m))
        for mo in range(8):
            (w2v,) = ws.get(f"cv2_{i}_{mo}", [(w2[j][:, mo * 128:(mo + 1) * 128], 8)])
            for (c0, n) in tiles:
                bk = B.bank("mm")
                for k in range(8):
                    PE.matmul(out=bk[:, 0:n], lhsT=w2v[:, k, :], rhs=u[:, k, c0:c0 + n], start=(k == 0), stop=(k == 7))
                DVE.scalar_tensor_tensor(out=h[:, mo, c0:c0 + n], in0=bk[:, 0:n], scalar=col("cv_b_pw2", j * 8 + mo),
                                         in1=h[:, mo, c0:c0 + n], op0=ALU.add, op1=ALU.add)
        B.release(mk)

    def bcast(ap2, shape, axis):
        return ap2.unsqueeze(axis).to_broadcast(shape)

    def mamba(i, blk):
        j = i // 2
        tiles = tiles_of(blk)
        last = blk == NBLK - 1
        Tb = TP + (NS if last else 0)
        rmsnorm_to(u, "g_mix", i, blk)
        mk = B.mark()
        B.S.region = True
        B.S.rlimit = opts.get("rlimit", 10 ** 9)
        B.pools = B.POOLS_SSD
        mask01 = B.alloc([128], F32)
        selhp = B.alloc([256], F32)
        ones4 = B.alloc([128], F32)
        dtT = B.alloc([T], F32)
        acT = B.alloc([T], F32)
        tmpT = B.alloc([T], F32)
        sz = B.alloc([2, T], F32)
        xpre = B.alloc([4, SCW + T], BF16)
        xs = B.alloc([2, 512], F32)
        y32 = B.alloc([2, 512], F32)
        B32 = B.alloc([512], F32)
        BT = B.alloc([512], BF16)
        CT = B.alloc([512], BF16)
        C32s = B.alloc([NS], F32)
        diag4 = B.alloc([4, SCW, 128], BF16)
        nac = B.alloc([8, 4], F32)
        dtm = B.alloc([8, 4], F32)
        dd = B.alloc([8, 4], F32)
        dl = B.alloc([4], F32)
        aneg = B.alloc([1], F32)
        ctmp = []
        for _ in range(2):
            ctmp.append((B.alloc([256], BF16), B.alloc([256], BF16), B.alloc([128], BF16), B.alloc([4, 128], F32),
                         B.alloc([4, 128], F32), B.alloc([4, 128], BF16), B.alloc([4, 128], BF16), B.alloc([4, 128], F32)))
        ccnt = [0]
        prevT = B.alloc([256], BF16)
        ygb = B.alloc([2, 512], BF16)
        xl32 = B.alloc([32, SCW - 1], F32)
        xs32s = B.alloc([32, NS], F32)
        stg = ctmp[1][7][:, 2:4, :]
        t1 = scr4[:, 0:2, :]
        if last:
            sst = B.alloc([4, NS, SCW - 1], BF16)
            ea = B.alloc([2, NS], F32)
            xdths = B.alloc([2, NS], F32)
            ysm = B.alloc([2, NS], F32)
            BCs = B.alloc([256], F32)
            S0 = [B.alloc([4, 2, 128], F32) for _ in range(2)]
        B.dma("xin0", out=xin[0][:, 0:128], in_=consts[:, 128:256])
        DVE.tensor_scalar(out=mask01, in0=xin[0][:, 0:128], scalar1=0.0, scalar2=None, op0=ALU.is_equal)
        B.dma("c1", out=selhp[0:4, :], in_=consts[0:4, 640:896])
        DVE.memset(ap=ones4, constant=1.0)
        if blk == 0:
            DVE.memset(ap=STs[j], constant=0.0)
        skip = opts.get("skip", ())
        if last and "nscs" not in skip:
            B.dma("nscs", out=nsc_s[j, :, 0:SCW - 2, :], in_=ssmc_s[j, :, 1:SCW - 1, :])
        s0cnt = 0
        for g in range(NG):
            colz = g * 256
            colx = DIN + g * 256
            colB = 2 * DIN + g * 128
            colC = 2 * DIN + 1024 + g * 128
            coldt = DIN + SCD + 4 * g
            chunk_ids = [2 * g, 2 * g + 1, 16 + g, 24 + g]
            STg = STs[j][:, g, :].rearrange("p (a b) -> p a b", a=4)
            if blk == 0:
                DVE.memset(ap=xpre[:, :, 0:SCW], constant=0.0)
            else:
                DVE.tensor_copy(out=xpre[:, :, 0:SCW], in_=xprefix[j][:, g * 4:(g + 1) * 4, :])
            (wz,) = ws.get(f"mz{i}_{g}", [(win[j][:, colz:colz + 256], 8)])
            for q in range(2):
                for (c0, n) in tiles:
                    bk = B.bank("mm")
                    for k in range(8):
                        PE.matmul(out=bk[:, 0:n], lhsT=wz[:, k, q * 128:(q + 1) * 128], rhs=u[:, k, c0:c0 + n],
                                  start=(k == 0), stop=(k == 7))
                    ACT.activation(out=sz[:, q, c0:c0 + n], in_=bk[:, 0:n], func=AF.Silu)

            def evac_pre(bk, qi, c0, n):
                ACT.copy(out=xpre[:, qi, SCW + c0:SCW + c0 + n], in_=bk[:, 0:n])
                if last and c0 == 512:
                    DVE.tensor_copy(out=xl32[:, chunk_ids[qi], :], in_=bk[:, 512 - (SCW - 1):512])
                if last and c0 == TP:
                    DVE.tensor_copy(out=xs32s[:, chunk_ids[qi], :], in_=bk[:, 0:NS])

            (wx,) = ws.get(f"mx{i}_{g}", [(win[j][:, colx:colx + 256], 8)])
            for q in range(2):
                for (c0, n) in tiles:
                    bk = B.bank("mm")
                    for k in range(8):
                        PE.matmul(out=bk[:, 0:n], lhsT=wx[:, k, q * 128:(q + 1) * 128], rhs=u[:, k, c0:c0 + n],
                                  start=(k == 0), stop=(k == 7))
                    evac_pre(bk, q, c0, n)
            wB, wC, wdt = ws.get(f"mbc{i}_{g}", [(win[j][:, colB:colB + 128], 8), (win[j][:, colC:colC + 128], 8),
                                                 (win[j][:, coldt:coldt + 4], 8)])
            for qi, wv in ((2, wB), (3, wC)):
                for (c0, n) in tiles:
                    bk = B.bank("mm")
                    for k in range(8):
                        PE.matmul(out=bk[:, 0:n], lhsT=wv[:, k, :], rhs=u[:, k, c0:c0 + n], start=(k == 0), stop=(k == 7))
                    evac_pre(bk, qi, c0, n)
            dtb = col("ssm_dt_bias_g", j * 8 + g)[0:4, :]
            for (c0, n) in (tiles if "dt" not in skip else ()):
                bk = B.bank("mm")
                for k in range(8):
                    PE.matmul(out=bk[0:4, 0:n], lhsT=wdt[:, k, :], rhs=u[:, k, c0:c0 + n], start=(k == 0), stop=(k == 7))
                ACT.activation(out=tmpT[0:4, c0:c0 + n], in_=bk[0:4, 0:n], func=AF.Exp, bias=dtb)
                ACT.activation(out=dtT[0:4, c0:c0 + n], in_=tmpT[0:4, c0:c0 + n], func=AF.Ln, bias=ones4[0:4, 0:1])
            if not last:
                DVE.tensor_copy(out=xprefix[j][:, g * 4:(g + 1) * 4, :], in_=xpre[:, :, TP:TP + SCW])
            mcut = opts.get("mcut", 9)
            if mcut <= -3:
                continue
            ACT.activation(out=aneg[0:4, :], in_=col("ssm_a_log_g", j * 8 + g)[0:4, :], func=AF.Exp)
            DVE.tensor_scalar(out=aneg[0:4, :], in0=aneg[0:4, :], scalar1=-1.0, scalar2=None, op0=ALU.mult)
            DVE.tensor_scalar(out=tmpT[0:4, 0:Tb], in0=dtT[0:4, 0:Tb], scalar1=aneg[0:4, 0:1], scalar2=None, op0=ALU.mult)
            for c in range(8):
                cs = slice(c * 128, (c + 1) * 128)
                DVE.tensor_tensor_scan(out=acT[0:4, cs], data0=ones4[0:4, 0:128], data1=tmpT[0:4, cs], initial=0.0,
                                       op0=ALU.mult, op1=ALU.add)
            for c in range(8):
                cs = slice(c * 128, (c + 1) * 128)
                bk = B.bank("aux")
                PE.transpose(out=bk[:, 0:4], in_=acT[0:4, cs], identity=ident[0:4, 0:4])
                PE.transpose(out=bk[:, 4:8], in_=dtT[0:4, cs], identity=ident[0:4, 0:4])
                DVE.tensor_scalar(out=dl[0:4, 0:4], in0=ident[0:4, 0:4], scalar1=acT[0:4, c * 128 + 127:c * 128 + 128],
                                  scalar2=None, op0=ALU.mult)
                PE.matmul(out=bk[:, 8:12], lhsT=ones4[0:4, 0:128], rhs=dl[0:4, 0:4], start=True, stop=True)
                DVE.tensor_scalar(out=nac[:, c, :], in0=bk[:, 0:4], scalar1=-1.0, scalar2=None, op0=ALU.mult)
                DVE.tensor_tensor(out=dd[:, c, :], in0=bk[:, 8:12], in1=nac[:, c, :], op=ALU.add)
                ACT.activation(out=dd[:, c, :], in_=dd[:, c, :], func=AF.Exp)
                DVE.tensor_copy(out=dtm[:, c, :], in_=bk[:, 4:8])
                DVE.tensor_tensor(out=dd[:, c, :], in0=dd[:, c, :], in1=dtm[:, c, :], op=ALU.mult)
            for qi, q32 in enumerate(chunk_ids):
                for k in range(SCW):
                    DVE.tensor_scalar(out=diag4[:, qi, k, :], in0=identb, scalar1=col("ssm_w_conv", (j * SCW + k) * 32 + q32),
                                      scalar2=None, op0=ALU.mult)
            if last:
                st = xin[1]
                for qi, q32 in enumerate(chunk_ids):
                    B.dma("xin1", out=st[0:NS * (SCW - 1), qi * 128:(qi + 1) * 128],
                          in_=ssmc_s[j, :, :, q32 * 128:(q32 + 1) * 128].rearrange("s k c -> (s k) c"))
                bk = B.bank("aux")
                nr = NS * (SCW - 1)
                for qi in range(4):
                    PE.transpose(out=bk[:, qi * nr:(qi + 1) * nr], in_=st[0:nr, qi * 128:(qi + 1) * 128],
                                 identity=ident[0:nr, 0:nr])
                ACT.copy(out=sst.rearrange("p q s k -> p q (s k)"), in_=bk[:, 0:4 * nr].rearrange("p (q t) -> p q t", q=4))
            (wo,) = ws.get(f"mo{i}_{g}", [(wout[j][g * 256:(g + 1) * 256, :], 2)])
            ACT.copy(out=prevT, in_=STs[j][:, g, :])
            for (c0, n) in tiles:
                for qi, q32 in enumerate(chunk_ids):
                    bias = col("ssm_b_conv", j * 32 + q32)
                    bk = B.bank("mm")
                    if c0 < TP:
                        for k in range(SCW):
                            PE.matmul(out=bk[:, 0:n], lhsT=diag4[:, qi, k, :], rhs=xpre[:, qi, c0 + k + 1:c0 + k + 1 + n],
                                      start=(k == 0), stop=(k == SCW - 1))
                    else:
                        for k in range(SCW - 1):
                            PE.matmul(out=bk[:, 0:NS], lhsT=diag4[:, qi, k, :], rhs=sst[:, qi, :, k], start=(k == 0), stop=False)
                        PE.matmul(out=bk[:, 0:NS], lhsT=diag4[:, qi, SCW - 1, :],
                                  rhs=xpre[:, qi, SCW + TP:SCW + TP + NS], start=False, stop=True)
                    if qi < 2:
                        ACT.activation(out=xs[:, qi, 0:n], in_=bk[:, 0:n], func=AF.Silu, bias=bias)
                    elif qi == 2:
                        ACT.activation(out=B32[:, 0:n], in_=bk[:, 0:n], func=AF.Silu, bias=bias)
                        DVE.tensor_copy(out=BT[:, 0:n], in_=B32[:, 0:n])
                    else:
                        ACT.activation(out=CT[:, 0:n], in_=bk[:, 0:n], func=AF.Silu, bias=bias)
                        if c0 == TP:
                            ACT.activation(out=C32s[:, 0:NS], in_=bk[:, 0:NS], func=AF.Silu, bias=bias)
                def stage_a(cl):
                    c = c0 // 128 + cl
                    cs = slice(c * 128, (c + 1) * 128)
                    ls = slice(cl * 128, (cl + 1) * 128)
                    pp = ccnt[0] % 2
                    ccnt[0] += 1
                    xdt, dxdt, Btm, Ebuf, LT, Mb, Cs, R4 = ctmp[pp]
                    DVE.tensor_tensor(out=R4[0:4], in0=bcast(acT[0:4, cs], [4, 4, 128], 1),
                                      in1=bcast(ident[0:4, 0:4], [4, 4, 128], 2), op=ALU.mult)
                    bkT = B.bank("aux")
                    for q in range(2):
                        PE.transpose(out=bkT[:, q * 128:(q + 1) * 128], in_=xs[:, q, ls], identity=ident)
                    PE.transpose(out=bkT[:, 256:384], in_=B32[:, ls], identity=ident)
                    xT4 = bkT[:, 0:256].rearrange("p (a b) -> p a b", a=4)
                    DVE.tensor_tensor(out=xdt.rearrange("p (a b) -> p a b", a=4), in0=xT4,
                                      in1=bcast(dtm[:, c, :], [128, 4, 64], 2), op=ALU.mult)
                    DVE.tensor_tensor(out=dxdt.rearrange("p (a b) -> p a b", a=4), in0=xT4,
                                      in1=bcast(dd[:, c, :], [128, 4, 64], 2), op=ALU.mult)
                    ACT.copy(out=Btm, in_=bkT[:, 256:384])
                    bkC = B.bank("ssdC")
                    PE.matmul(out=bkC[:, 0:128], lhsT=BT[:, ls], rhs=CT[:, ls], start=True, stop=True)
                    bkE = B.bank("ssdE")
                    PE.matmul(out=bkE[:, 0:512], lhsT=ones4[0:4, 0:128], rhs=R4[0:4].rearrange("p a b -> p (a b)"),
                              start=True, stop=True)
                    ACT.activation(out=Ebuf.rearrange("p a b -> p (a b)"), in_=bkE[:, 0:512], func=AF.Exp)
                    return (ls, ctmp[pp], bkC, bkE, c)

                def stage_a2(info):
                    ls, (xdt, dxdt, Btm, Ebuf, LT, Mb, Cs, R4), bkC, bkE, c = info
                    for hh in range(4):
                        DVE.tensor_scalar(out=LT[:, hh, :], in0=bkE[:, hh * 128:(hh + 1) * 128], scalar1=nac[:, c, hh:hh + 1],
                                          scalar2=0.0, op0=ALU.add, op1=ALU.min)
                    ACT.activation(out=LT.rearrange("p a b -> p (a b)"), in_=LT.rearrange("p a b -> p (a b)"), func=AF.Exp)

                def stage_b(info):
                    ls, (xdt, dxdt, Btm, Ebuf, LT, Mb, Cs, R4), bkC, bkE, c = info
                    DVE.tensor_tensor(out=bkC[:, 0:128], in0=bkC[:, 0:128], in1=mask01, op=ALU.mult)
                    DVE.tensor_tensor(out=Mb, in0=LT, in1=bcast(bkC[:, 0:128], [128, 4, 128], 1), op=ALU.mult)
                    POOL.tensor_tensor(out=Cs, in0=Ebuf, in1=bcast(CT[:, ls], [128, 4, 128], 1), op=ALU.mult)
                    for m2 in range(2):
                        bkY = B.bank("mm")
                        for h2 in range(2):
                            hh = m2 * 2 + h2
                            PE.matmul(out=bkY[h2 * 64:(h2 + 1) * 64, 0:128], lhsT=xdt[:, hh * 64:(hh + 1) * 64], rhs=Mb[:, hh, :],
                                      start=True, stop=False)
                            PE.matmul(out=bkY[h2 * 64:(h2 + 1) * 64, 0:128], lhsT=prevT[:, hh * 64:(hh + 1) * 64], rhs=Cs[:, hh, :],
                                      start=False, stop=True)
                        DVE.scalar_tensor_tensor(out=y32[:, m2, ls], in0=xs[:, m2, ls], scalar=col("ssm_d_rep", j * 16 + 2 * g + m2),
                                                 in1=bkY[:, 0:128], op0=ALU.mult, op1=ALU.add)
                    bkS = B.bank("mm")
                    PE.matmul(out=bkS[:, 0:256], lhsT=Btm, rhs=dxdt, start=True, stop=True)
                    DVE.tensor_tensor(out=STg, in0=STg, in1=Ebuf[:, :, 127:128].to_broadcast([128, 4, 64]), op=ALU.mult)
                    DVE.tensor_tensor(out=STg, in0=STg, in1=bkS[:, 0:256].rearrange("p (a b) -> p a b", a=4), op=ALU.add)
                    ACT.copy(out=prevT, in_=STs[j][:, g, :])

                nch = n // 128 if c0 < TP else 0
                pend = stage_a(0) if nch else None
                if nch:
                    stage_a2(pend)
                for cl in range(nch):
                    nxt = stage_a(cl + 1) if cl + 1 < nch else None
                    stage_b(pend)
                    if nxt is not None:
                        stage_a2(nxt)
                    pend = nxt
                if last and c0 == 512:
                    bk = B.bank("aux")
                    for m2 in range(2):
                        PE.transpose(out=bk[:, m2 * 128:(m2 + 1) * 128], in_=STs[j][:, g, m2 * 128:(m2 + 1) * 128], identity=ident)
                    ACT.copy(out=stg, in_=bk[:, 0:256].rearrange("p (a b) -> p a b", a=2))
                    B.dma("stg", out=ns_p[j, g * 256:(g + 1) * 256, :].rearrange("(m q) n -> q m n", m=2), in_=stg)
                if c0 == TP:
                    sc = slice(TP, TP + NS)
                    R4s = ctmp[0][7]
                    RB = ctmp[0][5]
                    RC = ctmp[0][6]
                    tBs = [R4s[:, 0, :], R4s[:, 2, :]]
                    junk = R4s[:, 1, :]
                    bk = B.bank("aux")
                    for m2 in range(2):
                        PE.matmul(out=bk[:, m2 * 32:m2 * 32 + NS], lhsT=selhp[0:4, m2 * 128:(m2 + 1) * 128], rhs=tmpT[0:4, sc],
                                  start=True, stop=True)
                        PE.matmul(out=bk[:, 64 + m2 * 32:64 + m2 * 32 + NS], lhsT=selhp[0:4, m2 * 128:(m2 + 1) * 128],
                                  rhs=dtT[0:4, sc], start=True, stop=True)
                    for m2 in range(2):
                        ACT.activation(out=ea[:, m2, :], in_=bk[:, m2 * 32:m2 * 32 + NS], func=AF.Exp)
                        DVE.tensor_tensor(out=xdths[:, m2, :], in0=bk[:, 64 + m2 * 32:64 + m2 * 32 + NS], in1=xs[:, m2, 0:NS], op=ALU.mult)
                    bk2 = B.bank("aux")
                    PE.transpose(out=bk2[0:NS, 0:128], in_=B32[:, 0:NS], identity=ident)
                    PE.transpose(out=bk2[0:NS, 128:256], in_=C32s[:, 0:NS], identity=ident)
                    ACT.copy(out=BCs[0:NS, :], in_=bk2[0:NS, 0:256])
                    def s_load(s4):
                        S = S0[s4 % 2]
                        for sl in range(4):
                            B.dma(f"s0_{s4 % 2}", out=S[:, sl],
                                  in_=ssm_s[j, s4 * 4 + sl, g * 256:(g + 1) * 256, :].rearrange("(m q) n -> q m n", m=2))

                    s_load(0)
                    s_load(1)
                    for s4 in range(4):
                        idn = bcast(ident[0:NS, s4 * 4:(s4 + 1) * 4], [NS, 4, 128], 2)
                        DVE.tensor_tensor(out=RB[0:NS], in0=bcast(BCs[0:NS, 0:128], [NS, 4, 128], 1), in1=idn, op=ALU.mult)
                        DVE.tensor_tensor(out=RC[0:NS], in0=bcast(BCs[0:NS, 128:256], [NS, 4, 128], 1), in1=idn, op=ALU.mult)
                        bkB = B.bank("ssd")
                        bkC2 = B.bank("ssd")
                        PE.matmul(out=bkB[:, 0:512], lhsT=onesb[0:NS, 0:128], rhs=RB[0:NS].rearrange("p a b -> p (a b)"),
                                  start=True, stop=True)
                        PE.matmul(out=bkC2[:, 0:512], lhsT=onesb[0:NS, 0:128], rhs=RC[0:NS].rearrange("p a b -> p (a b)"),
                                  start=True, stop=True)
                        S = S0[s4 % 2]
                        key = f"s0_{s4 % 2}"
                        for sl in range(4):
                            sidx = s4 * 4 + sl
                            for m2 in range(2):
                                tB = tBs[m2]
                                DVE.tensor_scalar(out=tB, in0=bkB[:, sl * 128:(sl + 1) * 128], scalar1=xdths[:, m2, sidx:sidx + 1],
                                                  scalar2=None, op0=ALU.mult)
                                DVE.scalar_tensor_tensor(out=S[:, sl, m2, :], in0=S[:, sl, m2, :], scalar=ea[:, m2, sidx:sidx + 1],
                                                         in1=tB, op0=ALU.mult, op1=ALU.add)
                                DVE.scalar_tensor_tensor(out=junk, in0=S[:, sl, m2, :], scalar=1.0, in1=bkC2[:, sl * 128:(sl + 1) * 128],
                                                         op0=ALU.mult, op1=ALU.mult, accum_out=ysm[:, m2, sidx:sidx + 1])
                        for sl in range(4):
                            B.dma(key, out=ns_s[j, s4 * 4 + sl, g * 256:(g + 1) * 256, :].rearrange("(m q) n -> q m n", m=2), in_=S[:, sl])
                        if s4 + 2 < 4:
                            s_load(s4 + 2)
                    for m2 in range(2):
                        DVE.scalar_tensor_tensor(out=y32[:, m2, 0:NS], in0=xs[:, m2, 0:NS], scalar=col("ssm_d_rep", j * 16 + 2 * g + m2),
                                                 in1=ysm[:, m2, :], op0=ALU.mult, op1=ALU.add)
                DVE.tensor_tensor(out=t1[:, :, 0:n], in0=y32[:, :, 0:n], in1=sz[:, :, c0:c0 + n], op=ALU.mult)
                ACT.activation(out=sqb[:, 0:2, 0:n], in_=t1[:, :, 0:n], func=AF.Square)
                bk = B.bank("aux")
                for m in range(2):
                    PE.matmul(out=bk[:, 0:n], lhsT=onesb, rhs=sqb[:, m, 0:n], start=(m == 0), stop=(m == 1))
                ACT.activation(out=rstd[:, 0:n], in_=bk[:, 0:n], func=AF.Sqrt, scale=1.0 / 256.0, bias=epsc)
                DVE.reciprocal(out=rstd[:, 0:n], in_=rstd[:, 0:n])
                for q in range(2):
                    DVE.scalar_tensor_tensor(out=ygb[:, q, 0:n], in0=t1[:, q, 0:n], scalar=col("ssm_norm_g", j * 16 + 2 * g + q),
                                             in1=rstd[:, 0:n], op0=ALU.mult, op1=ALU.mult)
                for mo in range(8):
                    bk = B.bank("mm")
                    for k in range(2):
                        PE.matmul(out=bk[:, 0:n], lhsT=wo[:, k, mo * 128:(mo + 1) * 128], rhs=ygb[:, k, 0:n],
                                  start=(k == 0), stop=(k == 1))
                    DVE.tensor_tensor(out=h[:, mo, c0:c0 + n], in0=bk[:, 0:n], in1=h[:, mo, c0:c0 + n], op=ALU.add)
        if last and opts.get("mcut", 9) >= 1:
            for (src, wdt, dst) in ((xl32, SCW - 1, nsc_p[j]), (xs32s, NS, nsc_s[j, :, SCW - 2, :])):
                for pc in range(4):
                    o = xin[pc % 2]
                    for half in range(2):
                        bk = B.bank("aux")
                        for q in range(4):
                            PE.transpose(out=bk[0:wdt, q * 128:(q + 1) * 128], in_=src[:, pc * 8 + half * 4 + q, :], identity=ident)
                        ACT.copy(out=o[0:wdt, half * 512:(half + 1) * 512], in_=bk[0:wdt, :])
                    B.dma(f"xin{pc % 2}", out=dst[:, pc * 1024:(pc + 1) * 1024], in_=o[0:wdt, :])
        B.S.region = False
        B.pools = B.POOLS_DENSE
        B.release(mk)

    stage = opts.get("stage", "full")

    def program():
        for blk in range(NBLK):
            load_x(blk)
            for i in range(nlayers):
                if stage in ("s3", "full") and i % 2 == 0:
                    conformer(i, blk)
                if stage in ("s4", "full") and i % 2 == 1:
                    mamba(i, blk)
                if stage in ("s2", "s3", "full"):
                    ffn(i, blk)
                if stage in ("s2", "s3", "full"):
                    ple(i, blk)
            final_out(blk)

    B.S.dry = True
    ws.planning = True
    program()
    B.S.dry = False
    ws.planning = False
    B.bank_rr = {}
    cnt[0] = 0
    program()

    B.S.finalize(nc, B.stack)
    B.stack.close()
    return B


def _small_params(inp):
    rows = []
    idx = {}

    def add(name, arr):
        a = np.ascontiguousarray(arr, dtype=np.float32).reshape(-1)
        pad = (-a.size) % 128
        if pad:
            a = np.concatenate([a, np.zeros(pad, np.float32)])
        idx[name] = sum(r.shape[0] for r in rows)
        rows.append(a.reshape(-1, 128))

    for nm in ("g_mix", "g_ffn", "g_ple", "g_final", "cv_b_pw1", "cv_b_dw", "cv_ln_g", "cv_ln_b", "cv_b_pw2", "cv_w_dw"):
        add(nm, inp[nm])
    for nm in ("ssm_b_conv", "ssm_w_conv", "ssm_norm_g"):
        add(nm, inp[nm])
    add("ssm_d_rep", np.repeat(np.asarray(inp["ssm_d"]), HD, axis=1))
    for nm in ("ssm_dt_bias", "ssm_a_log"):
        a = np.zeros((2, NG, 128), np.float32)
        a[:, :, 0:4] = np.asarray(inp[nm]).reshape(2, NG, 4)
        add(nm + "_g", a)
    return np.concatenate(rows, 0), idx


def _consts():
    c = np.zeros((128, 128 + 512 + 4096 + 1024 + 128), np.float32)
    c[:, 0:128] = np.eye(128, dtype=np.float32)
    j = np.arange(128)[:, None]
    i = np.arange(128)[None, :]
    neg = np.where(i < j, -30000.0, 0.0).astype(np.float32)
    c[:, 128:128 + 512] = np.tile(neg, (1, 4))
    sel = np.zeros((128, 2, 128), np.float32)
    for m2 in range(2):
        sel[2 * m2, m2, 0:64] = 1.0
        sel[2 * m2 + 1, m2, 64:128] = 1.0
    c[:, 640:640 + 256] = sel.reshape(128, 256)
    rm = np.ones((128, 1024), np.float32)
    rm[:, ::128] = 0.0
    c[:, 4736:4736 + 1024] = rm
    c[:, 5760:5888] = 1.0
    return c


_CACHE = {}


_NEED = {
    "s1": {"x_p", "x_s", "consts", "smallp"},
    "s2": {"x_p", "x_s", "consts", "smallp", "p_p", "p_s", "ffn_w_gate", "ffn_w_up", "ffn_w_down", "ple_w_proj", "ple_w_gate"},
    "s3": {"x_p", "x_s", "consts", "smallp", "p_p", "p_s", "ffn_w_gate", "ffn_w_up", "ffn_w_down", "ple_w_proj", "ple_w_gate",
           "cv_w_pw1", "cv_w_pw2", "conv_s"},
    "s4": {"x_p", "x_s", "consts", "smallp", "ssm_w_in", "ssm_w_out", "ssm_s", "ssmc_s"},
}


def kernel(**inp):
    opts = dict(inp.pop("_opts", {}) or {})
    if opts.get("stage", "full") in _NEED:
        opts["need"] = _NEED[opts.get("need_as", opts["stage"])]
    smallp, colidx = _small_params(inp)
    opts["ncolrows"] = smallp.shape[0]
    opts["colidx"] = colidx
    B = build(opts)
    global _LASTB
    _LASTB = B
    cst = _consts()
    in_maps = []
    for c in range(NCORES):
        m = {
            "x_p": np.ascontiguousarray(inp["x_prompt"][c]),
            "x_s": np.ascontiguousarray(inp["x_sample"][c * NS:(c + 1) * NS, 0, :]),
            "consts": cst,
            "smallp": smallp,
            "p_p": np.ascontiguousarray(inp["p_prompt"][:, c]),
            "p_s": np.ascontiguousarray(inp["p_sample"][:, c * NS:(c + 1) * NS, 0, :]),
        }
        m["conv_s"] = np.ascontiguousarray(inp["state_conv_mixer"][:, c * NS:(c + 1) * NS])
        m["ssm_s"] = np.ascontiguousarray(inp["state_ssm"][:, c * NS:(c + 1) * NS]).reshape(2, NS, NH * HD, DST)
        m["ssmc_s"] = np.ascontiguousarray(inp["state_ssm_conv"][:, c * NS:(c + 1) * NS])
        for wn in ("ffn_w_gate", "ffn_w_up", "ffn_w_down", "ple_w_proj", "ple_w_gate", "cv_w_pw1", "cv_w_pw2",
                   "ssm_w_in", "ssm_w_out"):
            m[wn] = inp[wn]
        in_maps.append({k: (v if tuple(v.shape) == B.in_shapes[k] else np.zeros(B.in_shapes[k], np.float32))
                        for k, v in m.items() if k in B.ins})
    ncr = opts.get("ncores", NCORES)
    res = run_bass_kernel_spmd(B.nc, in_maps[:ncr], core_ids=list(range(ncr)))
    R = list(res.results) + [res.results[0]] * (NCORES - ncr)
    y_prompt = np.stack([R[c]["y_p"] for c in range(NCORES)], 0)
    y_sample = np.concatenate([R[c]["y_s"] for c in range(NCORES)], 0)[:, None, :]
    nc_p = np.stack([R[c]["nc_p"] for c in range(NCORES)], 1)
    nc_s = np.concatenate([R[c]["nc_s"] for c in range(NCORES)], 1)
    ns_p = np.stack([R[c]["ns_p"] for c in range(NCORES)], 1).reshape(2, NCORES, NH, HD, DST)
    nsc_p = np.stack([R[c]["nsc_p"] for c in range(NCORES)], 1)
    ns_s = np.concatenate([R[c]["ns_s"] for c in range(NCORES)], 1).reshape(2, NCORES * NS, NH, HD, DST)
    nsc_s = np.concatenate([R[c]["nsc_s"] for c in range(NCORES)], 1)
    return (y_prompt, y_sample, nc_p, ns_p, nsc_p, nc_s, ns_s, nsc_s)
```

```python
import os
import numpy as np
from contextlib import ExitStack
import concourse.bass as bass
import concourse.mybir as mybir
from concourse.ap import AP as _AP
from concourse.bass_utils import run_bass_kernel_spmd

F32 = mybir.dt.float32
BF16 = mybir.dt.bfloat16
ALU = mybir.AluOpType
AF = mybir.ActivationFunctionType
ISZ = {F32: 4, BF16: 2, mybir.dt.float32r: 4}

NCORES = 8
D = 1024
SEQ = 2048
DEPTH = 4
NBLK = 2
TP = 1024
NS = 16
PLE = 256
DFF = 2816
CW = 31
DIN = 2048
NH = 32
HD = 64
NG = 8
DST = 128
SCW = 4
SCD = 4096
SIN = DIN + SCD + NH
EPS = 1e-6
COMPUTE = ("pe", "act", "dve", "pool")
SAME_ENGINE_SYNC = True


class _Op:
    __slots__ = ("meth", "kw", "waits", "sig", "sigidx", "dmakey", "dmaidx")


class Sched:
    def __init__(self):
        self.ops = {e: [] for e in ("pe", "act", "dve", "pool", "sp")}
        self.recs = {"SB": [], "PSUM": []}
        self.known = {e: {} for e in self.ops}
        self.dmacnt = {}
        self.base = {}

    def _box(self, ap):
        isz = ISZ[ap.dtype]
        pat = ap.ap
        pstride = pat[0][0] * isz
        off = ap.offset * isz
        p0 = off // pstride
        f0 = off % pstride + self.base[ap.name]
        ext = 0
        for st, cnt in pat[1:]:
            ext += abs(st) * (cnt - 1)
        p1 = p0 + pat[0][1]
        f1 = f0 + (ext + 1) * isz
        if str(ap.space) == "PSUM":
            p0 = p0 // 32 * 32
            p1 = (p1 + 31) // 32 * 32
            f0 = f0 // 2048 * 2048
            f1 = f0 + 2048
        return (p0, p1, f0, f1)

    dry = False

    region = False
    rcount = 0
    rlimit = 10 ** 9

    def emit(self, eng, meth, dmakey=None, **kw):
        if self.dry:
            return None
        if self.region:
            self.rcount += 1
            if self.rcount > self.rlimit:
                return None
        op = _Op()
        op.meth = meth
        op.kw = kw
        op.sig = False
        op.dmakey = dmakey
        lst = self.ops[eng]
        if dmakey is not None:
            idx = self.dmacnt.get(dmakey, 0)
            self.dmacnt[dmakey] = idx + 1
            op.dmaidx = idx
            prod, pseq = "dma:" + dmakey, idx
        else:
            prod, pseq = eng, len(lst)
        deps = {}
        acc = []
        for key, v in kw.items():
            if not isinstance(v, _AP):
                continue
            sp = str(v.space)
            if sp not in ("SB", "PSUM"):
                continue
            isw = key in ("out", "accum_out", "ap")
            box = self._box(v)
            acc.append((sp, box, isw))
            for r in self.recs[sp]:
                if r[0] < box[1] and box[0] < r[1] and r[2] < box[3] and box[2] < r[3]:
                    if isw or r[4] or (sp == "PSUM" and r[5] != prod):
                        rp, rs = r[5], r[6]
                        if rp == prod:
                            if not rp.startswith("dma:"):
                                if eng == "pe" or not SAME_ENGINE_SYNC:
                                    continue
                                if not r[4]:
                                    continue
                            if rs == pseq:
                                continue
                        if rp.startswith("dma:"):
                            rs = self.dmacnt[rp[4:]] - 1 - (1 if rp == prod else 0)
                        if deps.get(rp, -1) < rs:
                            deps[rp] = rs
        kn = self.known[eng]
        waits = []
        for rp, rs in deps.items():
            if kn.get(rp, -1) >= rs:
                continue
            kn[rp] = rs
            waits.append((rp, rs))
            if not rp.startswith("dma:"):
                self.ops[rp][rs].sig = True
        op.waits = waits
        for sp, box, isw in acc:
            l2 = self.recs[sp]
            if isw:
                l2[:] = [r for r in l2 if not (box[0] <= r[0] and r[1] <= box[1] and box[2] <= r[2] and r[3] <= box[3])]
            else:
                l2[:] = [r for r in l2 if not (r[5] == prod and not r[4] and box[0] <= r[0] and r[1] <= box[1]
                                               and box[2] <= r[2] and r[3] <= box[3])]
            l2.append((box[0], box[1], box[2], box[3], isw, prod, pseq))
        lst.append(op)
        return op

    def finalize(self, nc, stack):
        CH = 16000
        sems = {}
        for e in COMPUTE:
            n = 0
            for op in self.ops[e]:
                if op.sig:
                    op.sigidx = n
                    n += 1
            sems[e] = [stack.enter_context(nc.semaphore(f"s_{e}_{i}")) for i in range(n // CH + 1)]
        for key in self.dmacnt:
            sems["dma:" + key] = stack.enter_context(nc.semaphore("d_" + key))
        block = stack.enter_context(nc.Block())
        ops = self.ops
        dmacnt = self.dmacnt

        def body(ename, final=False):
            def f(eng):
                for op in ops[ename]:
                    for rp, rs in op.waits:
                        if rp.startswith("dma:"):
                            eng.wait_ge(sems[rp], 16 * (rs + 1))
                        else:
                            si = ops[rp][rs].sigidx
                            eng.wait_ge(sems[rp][si // CH], si % CH + 1)
                    ins = getattr(eng, op.meth)(**op.kw)
                    if op.dmakey is not None:
                        ins.then_inc(sems["dma:" + op.dmakey], 16)
                    elif op.sig:
                        ins.then_inc(sems[ename][op.sigidx // CH], 1)
                if final:
                    for key, cnt in dmacnt.items():
                        eng.wait_ge(sems["dma:" + key], 16 * cnt)
            return f

        block.tensor(body("pe"))
        block.scalar(body("act"))
        block.vector(body("dve"))
        block.gpsimd(body("pool"))
        block.sync(body("sp", final=True))


class Eng:
    def __init__(self, sched, name):
        self.s, self.n = sched, name

    def __getattr__(self, meth):
        def f(**kw):
            return self.s.emit(self.n, meth, **kw)
        return f


class Builder:
    def __init__(self, opts):
        self.opts = opts
        self.nc = bass.Bass("TRN2", target_bir_lowering=False)
        self.stack = ExitStack()
        self.S = Sched()
        self.PE, self.ACT, self.DVE, self.POOL = (Eng(self.S, e) for e in COMPUTE)
        self.ins = {}
        self.in_shapes = {}
        self.outs = {}
        self.arena_off = 0

    def din(self, name, shape):
        need = self.opts.get("need")
        if need is not None and name not in need:
            shape = [1] * len(shape)
        self.in_shapes[name] = tuple(shape)
        t = self.nc.dram_tensor(name, list(shape), F32, kind="ExternalInput").ap()
        self.ins[name] = t
        return t

    def dout(self, name, shape):
        t = self.nc.dram_tensor(name, list(shape), F32, kind="ExternalOutput").ap()
        self.outs[name] = t
        return t

    def dma(self, key, out, in_):
        return self.S.emit("sp", "dma_start", dmakey=key, out=out, in_=in_)

    def setup_mem(self):
        nc = self.nc
        self.ARENA_BYTES = 212480
        self.arena = self.stack.enter_context(nc.sbuf_tensor("arena", [128, self.ARENA_BYTES // 4], F32))
        self.S.base["arena"] = 0
        self.banks = []
        for i in range(8):
            b = self.stack.enter_context(nc.psum_tensor(f"bank{i}", [128, 512], F32))
            self.S.base[f"bank{i}"] = i * 2048
            self.banks.append(b)
        self.bank_rr = {}
        self.POOLS_DENSE = {"mm": (0, 1, 2, 3, 4), "aux": (5, 6, 7)}
        self.POOLS_SSD = {"mm": (0, 1, 2, 6), "aux": (5,), "ssdC": (3, 4), "ssdE": (7,), "ssd": (7, 6)}
        self.pools = self.POOLS_DENSE

    def alloc(self, shape, dtype):
        n = 1
        for s in shape:
            n *= s
        nbytes = (n * ISZ[dtype] + 31) // 32 * 32
        off = self.arena_off
        assert off + nbytes <= self.ARENA_BYTES, ("SBUF arena overflow", off, nbytes)
        self.arena_off = off + nbytes
        v = self.arena[:, off // 4:(off + nbytes) // 4]
        if dtype != F32:
            v = v.bitcast(dtype)
        v = v[:, 0:n]
        if len(shape) == 2:
            v = v.rearrange("p (a b) -> p a b", a=shape[0])
        elif len(shape) == 3:
            v = v.rearrange("p (a b c) -> p a b c", a=shape[0], b=shape[1])
        elif len(shape) == 4:
            v = v.rearrange("p (a b c d) -> p a b c d", a=shape[0], b=shape[1], c=shape[2])
        return v

    def mark(self):
        return self.arena_off

    def release(self, m):
        self.arena_off = m

    def bank(self, pool):
        pools = self.pools
        ids = pools[pool]
        i = self.bank_rr.get(pool, 0)
        self.bank_rr[pool] = i + 1
        return self.banks[ids[i % len(ids)]]


class WStream:
    NST, NWB, CAP = 2, 3, 2112

    def __init__(self, B):
        self.B = B
        self.st = [B.alloc([self.CAP], F32) for _ in range(self.NST)]
        self.wb = [B.alloc([self.CAP], BF16) for _ in range(self.NWB)]
        self.plan = []
        self.planning = True
        self.nl = 0
        self.nu = 0

    def _views(self, buf, pieces):
        out = []
        off = 0
        for src, kc in pieces:
            n = src.shape[1]
            out.append(buf[:, off:off + kc * n].rearrange("p (k n) -> p k n", k=kc))
            off += kc * n
        assert off <= self.CAP, off
        return out

    def _load(self, i):
        name, pieces = self.plan[i]
        sv = self._views(self.st[i % self.NST], pieces)
        for (src, kc), v in zip(pieces, sv):
            self.B.dma(f"w{i % self.NST}", out=v, in_=src.rearrange("(k p) n -> p k n", p=128))
        tot = sum(kc * src.shape[1] for src, kc in pieces)
        self.B.POOL.tensor_copy(out=self.wb[i % self.NWB][:, 0:tot], in_=self.st[i % self.NST][:, 0:tot])

    def get(self, name, pieces):
        if self.planning:
            self.plan.append((name, pieces))
            return self._views(self.wb[0], pieces)
        i = self.nu
        assert self.plan[i][0] == name, (self.plan[i][0], name)
        while self.nl < min(len(self.plan), i + 3):
            self._load(self.nl)
            self.nl += 1
        self.nu += 1
        return self._views(self.wb[i % self.NWB], pieces)


def tiles_of(blk):
    t = [(0, 512), (512, 512)]
    if blk == NBLK - 1:
        t.append((1024, NS))
    return t


def build(opts):
    B = Builder(opts)
    nc = B.nc
    PE, ACT, DVE, POOL = B.PE, B.ACT, B.DVE, B.POOL
    nlayers = opts.get("nlayers", DEPTH)
    x_p = B.din("x_p", [SEQ, D])
    x_s = B.din("x_s", [NS, D])
    consts = B.din("consts", [128, 128 + 512 + 4096 + 1024 + 128])
    NCOLROWS = opts["ncolrows"]
    smallp = B.din("smallp", [NCOLROWS, 128])
    g_cols = opts["colidx"]
    y_p = B.dout("y_p", [SEQ, D])
    y_s = B.dout("y_s", [NS, D])
    p_p = B.din("p_p", [DEPTH, SEQ, PLE])
    p_s = B.din("p_s", [DEPTH, NS, PLE])
    wg = B.din("ffn_w_gate", [DEPTH, D, DFF])
    wu = B.din("ffn_w_up", [DEPTH, D, DFF])
    wd = B.din("ffn_w_down", [DEPTH, DFF, D])
    wpp = B.din("ple_w_proj", [DEPTH, PLE, D])
    wpg = B.din("ple_w_gate", [DEPTH, D, D])
    w1 = B.din("cv_w_pw1", [2, D, 2 * D])
    w2 = B.din("cv_w_pw2", [2, D, D])
    conv_s = B.din("conv_s", [2, NS, CW - 1, D])
    win = B.din("ssm_w_in", [2, D, SIN])
    wout = B.din("ssm_w_out", [2, DIN, D])
    ssm_s = B.din("ssm_s", [2, NS, NH * HD, DST])
    ssmc_s = B.din("ssmc_s", [2, NS, SCW - 1, SCD])
    ns_p = B.dout("ns_p", [2, NH * HD, DST])
    nsc_p = B.dout("nsc_p", [2, SCW - 1, SCD])
    ns_s = B.dout("ns_s", [2, NS, NH * HD, DST])
    nsc_s = B.dout("nsc_s", [2, NS, SCW - 1, SCD])
    nc_p = B.dout("nc_p", [2, CW - 1, D])
    nc_s = B.dout("nc_s", [2, NS, CW - 1, D])

    B.setup_mem()
    T = TP + NS
    h = B.alloc([8, T], F32)
    u = B.alloc([8, T], BF16)
    ident = B.alloc([128], F32)
    onesb = B.alloc([128], BF16)
    ncolt = (NCOLROWS + 127) // 128
    cols = B.alloc([ncolt * 128], F32)
    identb = B.alloc([128], BF16)
    vprefix = [B.alloc([8, CW - 1], BF16) for _ in range(2)]
    xin = [B.alloc([D], F32) for _ in range(2)]
    rstd = B.alloc([512], F32)
    sqb = B.alloc([8, 512], BF16)
    ws = WStream(B)
    scr4 = B.alloc([4, 512], F32)
    sgb = [scr4[:, 0, :], scr4[:, 1, :]]
    tmpb = [scr4[:, 2, :], scr4[:, 3, :]]
    scr_bf = B.arena[:, (B.arena_off - 4 * 512 * 4) // 4:B.arena_off // 4].bitcast(BF16).rearrange("p (a b) -> p a b", a=8)
    STs = [B.alloc([NG, 256], F32) for _ in range(2)]
    xprefix = [B.alloc([32, SCW], BF16) for _ in range(2)]
    cnt = [0]

    B.dma("c0", out=ident, in_=consts[:, 0:128])
    DVE.memset(ap=onesb, constant=1.0)
    DVE.tensor_copy(out=identb, in_=ident)
    for t in range(ncolt):
        r0 = t * 128
        nr = min(128, NCOLROWS - r0)
        st = xin[t % 2]
        B.dma(f"xin{t % 2}", out=st[0:nr, 0:128], in_=smallp[r0:r0 + nr, :])
        bk = B.bank("aux")
        PE.transpose(out=bk[:, 0:nr], in_=st[0:nr, 0:128], identity=ident[0:nr, 0:nr])
        DVE.tensor_copy(out=cols[:, r0:r0 + nr], in_=bk[:, 0:nr])

    def col(name, i=0):
        c = g_cols[name] + i
        return cols[:, c:c + 1]

    def load_x(blk):
        for tc in range(8):
            st = xin[tc % 2]
            B.dma(f"xin{tc % 2}", out=st, in_=x_p[blk * TP + tc * 128: blk * TP + (tc + 1) * 128, :])
            for half in range(2):
                bk = B.bank("aux")
                for q in range(4):
                    m = half * 4 + q
                    PE.transpose(out=bk[:, q * 128:(q + 1) * 128], in_=st[:, m * 128:(m + 1) * 128], identity=ident)
                ACT.copy(out=h[:, half * 4:half * 4 + 4, tc * 128:(tc + 1) * 128],
                         in_=bk[:, :].rearrange("p (q t) -> p q t", q=4))
        if blk == NBLK - 1:
            st = xin[0]
            B.dma("xin0", out=st[0:NS, :], in_=x_s)
            for half in range(2):
                bk = B.bank("aux")
                for q in range(4):
                    m = half * 4 + q
                    PE.transpose(out=bk[:, q * NS:(q + 1) * NS], in_=st[0:NS, m * 128:(m + 1) * 128],
                                 identity=ident[0:NS, 0:NS])
                ACT.copy(out=h[:, half * 4:half * 4 + 4, TP:TP + NS],
                         in_=bk[:, 0:4 * NS].rearrange("p (q t) -> p q t", q=4))

    def rms_stats(src, nch, c0, n, dim):
        ACT.activation(out=sqb[:, 0:nch, 0:n], in_=src[:, 0:nch, c0:c0 + n], func=AF.Square)
        bk = B.bank("aux")
        for m in range(nch):
            PE.matmul(out=bk[:, 0:n], lhsT=onesb, rhs=sqb[:, m, 0:n], start=(m == 0), stop=(m == nch - 1))
        ACT.activation(out=rstd[:, 0:n], in_=bk[:, 0:n], func=AF.Sqrt, scale=1.0 / dim, bias=epsc)
        DVE.reciprocal(out=rstd[:, 0:n], in_=rstd[:, 0:n])

    epsc = B.alloc([1], F32)
    DVE.memset(ap=epsc, constant=EPS)

    def rmsnorm_to(dst, gname, gi, blk):
        for (c0, n) in tiles_of(blk):
            rms_stats(h, 8, c0, n, D)
            for m in range(8):
                DVE.scalar_tensor_tensor(out=dst[:, m, c0:c0 + n], in0=h[:, m, c0:c0 + n], scalar=col(gname, gi * 8 + m),
                                         in1=rstd[:, 0:n], op0=ALU.mult, op1=ALU.mult)

    def final_out(blk):
        mk = B.mark()
        yv = B.alloc([8, 512], F32)
        ost = [B.alloc([D], F32) for _ in range(2)]
        cnt = 0
        for (c0, n) in tiles_of(blk):
            rms_stats(h, 8, c0, n, D)
            for m in range(8):
                DVE.scalar_tensor_tensor(out=yv[:, m, 0:n], in0=h[:, m, c0:c0 + n], scalar=col("g_final", m),
                                         in1=rstd[:, 0:n], op0=ALU.mult, op1=ALU.mult)
            ntc = (n + 127) // 128
            for tc in range(ntc):
                w = min(128, n - tc * 128)
                o = ost[cnt % 2]
                for half in range(2):
                    bk = B.bank("aux")
                    for q in range(4):
                        m = half * 4 + q
                        PE.transpose(out=bk[0:w, q * 128:(q + 1) * 128], in_=yv[:, m, tc * 128:tc * 128 + w],
                                     identity=ident)
                    ACT.copy(out=o[0:w, half * 512:(half + 1) * 512], in_=bk[0:w, :])
                if c0 < TP:
                    r0 = blk * TP + c0 + tc * 128
                    B.dma(f"ost{cnt % 2}", out=y_p[r0:r0 + w, :], in_=o[0:w, :])
                else:
                    B.dma(f"ost{cnt % 2}", out=y_s[:, :], in_=o[0:w, :])
                cnt += 1
        B.release(mk)

    def ffn(i, blk):
        tiles = tiles_of(blk)
        rmsnorm_to(u, "g_ffn", i, blk)
        mk = B.mark()
        hid = B.alloc([11, T], BF16)
        for half in range(2):
            f0 = half * 11
            for fa in range(0, 11, 1):
                nf = 1
                cs = slice((f0 + fa) * 128, (f0 + fa + nf) * 128)
                wgv, wuv = ws.get(f"ffn_gu{i}_{half}_{fa}", [(wg[i][:, cs], 8), (wu[i][:, cs], 8)])
                for q in range(nf):
                    f = fa + q
                    for (c0, n) in tiles:
                        bg = B.bank("mm")
                        bu = B.bank("mm")
                        for k in range(8):
                            PE.matmul(out=bg[:, 0:n], lhsT=wgv[:, k, q * 128:(q + 1) * 128], rhs=u[:, k, c0:c0 + n],
                                      start=(k == 0), stop=(k == 7))
                        for k in range(8):
                            PE.matmul(out=bu[:, 0:n], lhsT=wuv[:, k, q * 128:(q + 1) * 128], rhs=u[:, k, c0:c0 + n],
                                      start=(k == 0), stop=(k == 7))
                        sg = sgb[cnt[0] % 2]
                        cnt[0] += 1
                        ACT.activation(out=sg[:, 0:n], in_=bg[:, 0:n], func=AF.Silu)
                        DVE.tensor_tensor(out=hid[:, f, c0:c0 + n], in0=bu[:, 0:n], in1=sg[:, 0:n], op=ALU.mult)
            for mo2 in range(8):
                (wdv,) = ws.get(f"ffn_d{i}_{half}_{mo2}",
                                [(wd[i][f0 * 128:(f0 + 11) * 128, mo2 * 128:(mo2 + 1) * 128], 11)])
                for q in range(1):
                    mo = mo2 + q
                    for (c0, n) in tiles:
                        bk = B.bank("mm")
                        for k in range(11):
                            PE.matmul(out=bk[:, 0:n], lhsT=wdv[:, k, q * 128:(q + 1) * 128], rhs=hid[:, k, c0:c0 + n],
                                      start=(k == 0), stop=(k == 10))
                        DVE.tensor_tensor(out=h[:, mo, c0:c0 + n], in0=bk[:, 0:n], in1=h[:, mo, c0:c0 + n], op=ALU.add)
        B.release(mk)

    def ple(i, blk):
        tiles = tiles_of(blk)
        mk = B.mark()
        pT = B.alloc([2, T], BF16)
        for tc in range(8):
            st = xin[tc % 2]
            r0 = blk * TP + tc * 128
            B.dma(f"xin{tc % 2}", out=st[:, 0:PLE], in_=p_p[i, r0:r0 + 128, :])
            bk = B.bank("aux")
            for q in range(2):
                PE.transpose(out=bk[:, q * 128:(q + 1) * 128], in_=st[:, q * 128:(q + 1) * 128], identity=ident)
            ACT.copy(out=pT[:, 0:2, tc * 128:(tc + 1) * 128], in_=bk[:, 0:256].rearrange("p (q t) -> p q t", q=2))
        if blk == NBLK - 1:
            st = xin[0]
            B.dma("xin0", out=st[0:NS, 0:PLE], in_=p_s[i])
            bk = B.bank("aux")
            for q in range(2):
                PE.transpose(out=bk[:, q * NS:(q + 1) * NS], in_=st[0:NS, q * 128:(q + 1) * 128],
                             identity=ident[0:NS, 0:NS])
            ACT.copy(out=pT[:, 0:2, TP:TP + NS], in_=bk[:, 0:2 * NS].rearrange("p (q t) -> p q t", q=2))
        rmsnorm_to(u, "g_ple", i, blk)
        for mo2 in range(8):
            cs = slice(mo2 * 128, (mo2 + 1) * 128)
            wgv, wpv = ws.get(f"ple{i}_{mo2}", [(wpg[i][:, cs], 8), (wpp[i][:, cs], 2)])
            for q in range(1):
                mo = mo2 + q
                for (c0, n) in tiles:
                    bg = B.bank("mm")
                    bp = B.bank("mm")
                    for k in range(8):
                        PE.matmul(out=bg[:, 0:n], lhsT=wgv[:, k, q * 128:(q + 1) * 128], rhs=u[:, k, c0:c0 + n],
                                  start=(k == 0), stop=(k == 7))
                    for k in range(2):
                        PE.matmul(out=bp[:, 0:n], lhsT=wpv[:, k, q * 128:(q + 1) * 128], rhs=pT[:, k, c0:c0 + n],
                                  start=(k == 0), stop=(k == 1))
                    sg = sgb[cnt[0] % 2]
                    tm = tmpb[cnt[0] % 2]
                    cnt[0] += 1
                    ACT.activation(out=sg[:, 0:n], in_=bg[:, 0:n], func=AF.Sigmoid)
                    DVE.tensor_tensor(out=tm[:, 0:n], in0=bp[:, 0:n], in1=sg[:, 0:n], op=ALU.mult)
                    DVE.tensor_tensor(out=h[:, mo, c0:c0 + n], in0=tm[:, 0:n], in1=h[:, mo, c0:c0 + n], op=ALU.add)
        B.release(mk)

    def bc8(ap2, n):
        return ap2[:, 0:n].unsqueeze(1).to_broadcast([128, 8, n])

    def conformer(i, blk):
        j = i // 2
        tiles = tiles_of(blk)
        last = blk == NBLK - 1
        rmsnorm_to(u, "g_mix", i, blk)
        mk = B.mark()
        v = B.alloc([8, CW - 1 + T], BF16)
        c32 = B.alloc([8, T], F32)
        diag = [B.alloc([CW, 128], BF16)]
        cb = scr_bf
        mean = B.alloc([512], F32)
        msq = B.alloc([512], F32)
        v32 = B.alloc([8, CW - 1], F32)
        vs32 = B.alloc([8, NS], F32)
        if blk == 0:
            DVE.memset(ap=v[:, :, 0:CW - 1], constant=0.0)
        else:
            DVE.tensor_copy(out=v[:, :, 0:CW - 1], in_=vprefix[j])
        for mo in range(8):
            wa, wgt = ws.get(f"cv1_{i}_{mo}", [(w1[j][:, mo * 128:(mo + 1) * 128], 8),
                                              (w1[j][:, D + mo * 128:D + (mo + 1) * 128], 8)])
            for (c0, n) in tiles:
                ba = B.bank("mm")
                bg = B.bank("mm")
                for k in range(8):
                    PE.matmul(out=ba[:, 0:n], lhsT=wa[:, k, :], rhs=u[:, k, c0:c0 + n], start=(k == 0), stop=(k == 7))
                for k in range(8):
                    PE.matmul(out=bg[:, 0:n], lhsT=wgt[:, k, :], rhs=u[:, k, c0:c0 + n], start=(k == 0), stop=(k == 7))
                sg = sgb[cnt[0] % 2]
                cnt[0] += 1
                ACT.activation(out=sg[:, 0:n], in_=bg[:, 0:n], func=AF.Sigmoid, bias=col("cv_b_pw1", j * 16 + 8 + mo))
                ba_col = col("cv_b_pw1", j * 16 + mo)
                DVE.scalar_tensor_tensor(out=v[:, mo, CW - 1 + c0:CW - 1 + c0 + n], in0=ba[:, 0:n], scalar=ba_col,
                                         in1=sg[:, 0:n], op0=ALU.add, op1=ALU.mult)
                if last and c0 == 512:
                    DVE.scalar_tensor_tensor(out=v32[:, mo, :], in0=ba[:, 512 - (CW - 1):512], scalar=ba_col,
                                             in1=sg[:, 512 - (CW - 1):512], op0=ALU.add, op1=ALU.mult)
                if last and c0 == TP:
                    DVE.scalar_tensor_tensor(out=vs32[:, mo, :], in0=ba[:, 0:NS], scalar=ba_col,
                                             in1=sg[:, 0:NS], op0=ALU.add, op1=ALU.mult)
        if not last:
            DVE.tensor_copy(out=vprefix[j], in_=v[:, :, TP:TP + CW - 1])
        stt = None
        if last:
            stt = B.alloc([8, NS, CW - 1], BF16)
            for s4 in range(4):
                st = xin[s4 % 2]
                B.dma(f"xin{s4 % 2}", out=st[0:120, :], in_=conv_s[j, s4 * 4:(s4 + 1) * 4].rearrange("s k c -> (s k) c"))
                for half in range(2):
                    bk = B.bank("aux")
                    for q in range(4):
                        PE.transpose(out=bk[:, q * 120:(q + 1) * 120], in_=st[0:120, (half * 4 + q) * 128:(half * 4 + q + 1) * 128],
                                     identity=ident[0:120, 0:120])
                    ACT.copy(out=stt[:, half * 4:half * 4 + 4, s4 * 4:(s4 + 1) * 4, :].rearrange("p m s k -> p m (s k)"),
                             in_=bk[:, 0:480].rearrange("p (q t) -> p q t", q=4))
            B.dma("ncs", out=nc_s[j, :, 0:CW - 2, :], in_=conv_s[j, :, 1:CW - 1, :])
            for (src, wdt, dst) in ((v32, CW - 1, nc_p[j]), (vs32, NS, nc_s[j, :, CW - 2, :])):
                o = xin[0] if wdt == NS else xin[1]
                for half in range(2):
                    bk = B.bank("aux")
                    for q in range(4):
                        PE.transpose(out=bk[0:wdt, q * 128:(q + 1) * 128], in_=src[:, half * 4 + q, :], identity=ident)
                    ACT.copy(out=o[0:wdt, half * 512:(half + 1) * 512], in_=bk[0:wdt, :])
                B.dma("xin0" if wdt == NS else "xin1", out=dst, in_=o[0:wdt, :])
        for mo in range(8):
            dg = diag[0]
            for k in range(CW):
                DVE.tensor_scalar(out=dg[:, k, :], in0=identb, scalar1=col("cv_w_dw", (j * CW + k) * 8 + mo), scalar2=None,
                                  op0=ALU.mult)
            for (c0, n) in tiles:
                bk = B.bank("mm")
                if c0 < TP:
                    for k in range(CW):
                        PE.matmul(out=bk[:, 0:n], lhsT=dg[:, k, :], rhs=v[:, mo, c0 + k:c0 + k + n],
                                  start=(k == 0), stop=(k == CW - 1))
                else:
                    for k in range(CW - 1):
                        PE.matmul(out=bk[:, 0:NS], lhsT=dg[:, k, :], rhs=stt[:, mo, :, k], start=(k == 0), stop=False)
                    PE.matmul(out=bk[:, 0:NS], lhsT=dg[:, CW - 1, :], rhs=v[:, mo, CW - 1 + TP:CW - 1 + TP + NS],
                              start=False, stop=True)
                ACT.activation(out=c32[:, mo, c0:c0 + n], in_=bk[:, 0:n], func=AF.Identity, bias=col("cv_b_dw", j * 8 + mo))
        for (c0, n) in tiles:
            ACT.activation(out=sqb[:, :, 0:n], in_=c32[:, :, c0:c0 + n], func=AF.Square)
            DVE.tensor_copy(out=cb[:, :, 0:n], in_=c32[:, :, c0:c0 + n])
            bs = B.bank("aux")
            bq = B.bank("aux")
            for m in range(8):
                PE.matmul(out=bs[:, 0:n], lhsT=onesb, rhs=cb[:, m, 0:n], start=(m == 0), stop=(m == 7))
            for m in range(8):
                PE.matmul(out=bq[:, 0:n], lhsT=onesb, rhs=sqb[:, m, 0:n], start=(m == 0), stop=(m == 7))
            DVE.tensor_scalar(out=mean[:, 0:n], in0=bs[:, 0:n], scalar1=1.0 / D, scalar2=None, op0=ALU.mult)
            DVE.tensor_tensor(out=msq[:, 0:n], in0=mean[:, 0:n], in1=mean[:, 0:n], op=ALU.mult)
            DVE.scalar_tensor_tensor(out=msq[:, 0:n], in0=bq[:, 0:n], scalar=1.0 / D, in1=msq[:, 0:n],
                                     op0=ALU.mult, op1=ALU.subtract)
            ACT.activation(out=rstd[:, 0:n], in_=msq[:, 0:n], func=AF.Sqrt, bias=epsc)
            DVE.reciprocal(out=rstd[:, 0:n], in_=rstd[:, 0:n])
            DVE.tensor_tensor(out=c32[:, :, c0:c0 + n], in0=c32[:, :, c0:c0 + n], in1=bc8(mean, n), op=ALU.subtract)
            DVE.tensor_tensor(out=c32[:, :, c0:c0 + n], in0=c32[:, :, c0:c0 + n], in1=bc8(rstd, n), op=ALU.mult)
            for m in range(8):
                ACT.activation(out=u[:, m, c0:c0 + n], in_=c32[:, m, c0:c0 + n], func=AF.Silu,
                               scale=col("cv_ln_g", j * 8 + m), bias=col("cv_ln_b", j * 8 + m))
        for mo in range(8):
            (w2v,) = ws.get(f"cv2_{i}_{mo}", [(w2[j][:, mo * 128:(mo + 1) * 128], 8)])
            for (c0, n) in tiles:
                bk = B.bank("mm")
                for k in range(8):
                    PE.matmul(out=bk[:, 0:n], lhsT=w2v[:, k, :], rhs=u[:, k, c0:c0 + n], start=(k == 0), stop=(k == 7))
                DVE.scalar_tensor_tensor(out=h[:, mo, c0:c0 + n], in0=bk[:, 0:n], scalar=col("cv_b_pw2", j * 8 + mo),
                                         in1=h[:, mo, c0:c0 + n], op0=ALU.add, op1=ALU.add)
        B.release(mk)

    def bcast(ap2, shape, axis):
        return ap2.unsqueeze(axis).to_broadcast(shape)

    def mamba(i, blk):
        j = i // 2
        tiles = tiles_of(blk)
        last = blk == NBLK - 1
        Tb = TP + (NS if last else 0)
        rmsnorm_to(u, "g_mix", i, blk)
        mk = B.mark()
        B.S.region = True
        B.S.rlimit = opts.get("rlimit", 10 ** 9)
        B.pools = B.POOLS_SSD
        mask01 = B.alloc([128], F32)
        selhp = B.alloc([256], F32)
        ones4 = B.alloc([128], F32)
        dtT = B.alloc([T], F32)
        acT = B.alloc([T], F32)
        tmpT = B.alloc([T], F32)
        sz = B.alloc([2, T], F32)
        xpre = B.alloc([4, SCW + T], BF16)
        xs = B.alloc([2, 512], F32)
        y32 = B.alloc([2, 512], F32)
        B32 = B.alloc([512], F32)
        BT = B.alloc([512], BF16)
        CT = B.alloc([512], BF16)
        C32s = B.alloc([NS], F32)
        diag4 = B.alloc([4, SCW, 128], BF16)
        nac = B.alloc([8, 4], F32)
        dtm = B.alloc([8, 4], F32)
        dd = B.alloc([8, 4], F32)
        dl = B.alloc([4], F32)
        aneg = B.alloc([1], F32)
        ctmp = []
        for _ in range(2):
            ctmp.append((B.alloc([256], BF16), B.alloc([256], BF16), B.alloc([128], BF16), B.alloc([4, 128], F32),
                         B.alloc([4, 128], F32), B.alloc([4, 128], BF16), B.alloc([4, 128], BF16), B.alloc([4, 128], F32)))
        ccnt = [0]
        prevT = B.alloc([256], BF16)
        ygb = B.alloc([2, 512], BF16)
        xl32 = B.alloc([32, SCW - 1], F32)
        xs32s = B.alloc([32, NS], F32)
        stg = ctmp[1][7][:, 2:4, :]
        t1 = scr4[:, 0:2, :]
        if last:
            sst = B.alloc([4, NS, SCW - 1], BF16)
            ea = B.alloc([2, NS], F32)
            xdths = B.alloc([2, NS], F32)
            ysm = B.alloc([2, NS], F32)
            BCs = B.alloc([256], F32)
            S0 = [B.alloc([4, 2, 128], F32) for _ in range(2)]
        B.dma("xin0", out=xin[0][:, 0:128], in_=consts[:, 128:256])
        DVE.tensor_scalar(out=mask01, in0=xin[0][:, 0:128], scalar1=0.0, scalar2=None, op0=ALU.is_equal)
        B.dma("c1", out=selhp[0:4, :], in_=consts[0:4, 640:896])
        DVE.memset(ap=ones4, constant=1.0)
        if blk == 0:
            DVE.memset(ap=STs[j], constant=0.0)
        skip = opts.get("skip", ())
        if last and "nscs" not in skip:
            B.dma("nscs", out=nsc_s[j, :, 0:SCW - 2, :], in_=ssmc_s[j, :, 1:SCW - 1, :])
        s0cnt = 0
        for g in range(NG):
            colz = g * 256
            colx = DIN + g * 256
            colB = 2 * DIN + g * 128
            colC = 2 * DIN + 1024 + g * 128
            coldt = DIN + SCD + 4 * g
            chunk_ids = [2 * g, 2 * g + 1, 16 + g, 24 + g]
            STg = STs[j][:, g, :].rearrange("p (a b) -> p a b", a=4)
            if blk == 0:
                DVE.memset(ap=xpre[:, :, 0:SCW], constant=0.0)
            else:
                DVE.tensor_copy(out=xpre[:, :, 0:SCW], in_=xprefix[j][:, g * 4:(g + 1) * 4, :])
            (wz,) = ws.get(f"mz{i}_{g}", [(win[j][:, colz:colz + 256], 8)])
            for q in range(2):
                for (c0, n) in tiles:
                    bk = B.bank("mm")
                    for k in range(8):
                        PE.matmul(out=bk[:, 0:n], lhsT=wz[:, k, q * 128:(q + 1) * 128], rhs=u[:, k, c0:c0 + n],
                                  start=(k == 0), stop=(k == 7))
                    ACT.activation(out=sz[:, q, c0:c0 + n], in_=bk[:, 0:n], func=AF.Silu)

            def evac_pre(bk, qi, c0, n):
                ACT.copy(out=xpre[:, qi, SCW + c0:SCW + c0 + n], in_=bk[:, 0:n])
                if last and c0 == 512:
                    DVE.tensor_copy(out=xl32[:, chunk_ids[qi], :], in_=bk[:, 512 - (SCW - 1):512])
                if last and c0 == TP:
                    DVE.tensor_copy(out=xs32s[:, chunk_ids[qi], :], in_=bk[:, 0:NS])

            (wx,) = ws.get(f"mx{i}_{g}", [(win[j][:, colx:colx + 256], 8)])
            for q in range(2):
                for (c0, n) in tiles:
                    bk = B.bank("mm")
                    for k in range(8):
                        PE.matmul(out=bk[:, 0:n], lhsT=wx[:, k, q * 128:(q + 1) * 128], rhs=u[:, k, c0:c0 + n],
                                  start=(k == 0), stop=(k == 7))
                    evac_pre(bk, q, c0, n)
            wB, wC, wdt = ws.get(f"mbc{i}_{g}", [(win[j][:, colB:colB + 128], 8), (win[j][:, colC:colC + 128], 8),
                                                 (win[j][:, coldt:coldt + 4], 8)])
            for qi, wv in ((2, wB), (3, wC)):
                for (c0, n) in tiles:
                    bk = B.bank("mm")
                    for k in range(8):
                        PE.matmul(out=bk[:, 0:n], lhsT=wv[:, k, :], rhs=u[:, k, c0:c0 + n], start=(k == 0), stop=(k == 7))
                    evac_pre(bk, qi, c0, n)
            dtb = col("ssm_dt_bias_g", j * 8 + g)[0:4, :]
            for (c0, n) in (tiles if "dt" not in skip else ()):
                bk = B.bank("mm")
                for k in range(8):
                    PE.matmul(out=bk[0:4, 0:n], lhsT=wdt[:, k, :], rhs=u[:, k, c0:c0 + n], start=(k == 0), stop=(k == 7))
                ACT.activation(out=tmpT[0:4, c0:c0 + n], in_=bk[0:4, 0:n], func=AF.Exp, bias=dtb)
                ACT.activation(out=dtT[0:4, c0:c0 + n], in_=tmpT[0:4, c0:c0 + n], func=AF.Ln, bias=ones4[0:4, 0:1])
            if not last:
                DVE.tensor_copy(out=xprefix[j][:, g * 4:(g + 1) * 4, :], in_=xpre[:, :, TP:TP + SCW])
            mcut = opts.get("mcut", 9)
            if mcut <= -3:
                continue
            ACT.activation(out=aneg[0:4, :], in_=col("ssm_a_log_g", j * 8 + g)[0:4, :], func=AF.Exp)
            DVE.tensor_scalar(out=aneg[0:4, :], in0=aneg[0:4, :], scalar1=-1.0, scalar2=None, op0=ALU.mult)
            DVE.tensor_scalar(out=tmpT[0:4, 0:Tb], in0=dtT[0:4, 0:Tb], scalar1=aneg[0:4, 0:1], scalar2=None, op0=ALU.mult)
            for c in range(8):
                cs = slice(c * 128, (c + 1) * 128)
                DVE.tensor_tensor_scan(out=acT[0:4, cs], data0=ones4[0:4, 0:128], data1=tmpT[0:4, cs], initial=0.0,
                                       op0=ALU.mult, op1=ALU.add)
            for c in range(8):
                cs = slice(c * 128, (c + 1) * 128)
                bk = B.bank("aux")
                PE.transpose(out=bk[:, 0:4], in_=acT[0:4, cs], identity=ident[0:4, 0:4])
                PE.transpose(out=bk[:, 4:8], in_=dtT[0:4, cs], identity=ident[0:4, 0:4])
                DVE.tensor_scalar(out=dl[0:4, 0:4], in0=ident[0:4, 0:4], scalar1=acT[0:4, c * 128 + 127:c * 128 + 128],
                                  scalar2=None, op0=ALU.mult)
                PE.matmul(out=bk[:, 8:12], lhsT=ones4[0:4, 0:128], rhs=dl[0:4, 0:4], start=True, stop=True)
                DVE.tensor_scalar(out=nac[:, c, :], in0=bk[:, 0:4], scalar1=-1.0, scalar2=None, op0=ALU.mult)
                DVE.tensor_tensor(out=dd[:, c, :], in0=bk[:, 8:12], in1=nac[:, c, :], op=ALU.add)
                ACT.activation(out=dd[:, c, :], in_=dd[:, c, :], func=AF.Exp)
                DVE.tensor_copy(out=dtm[:, c, :], in_=bk[:, 4:8])
                DVE.tensor_tensor(out=dd[:, c, :], in0=dd[:, c, :], in1=dtm[:, c, :], op=ALU.mult)
            for qi, q32 in enumerate(chunk_ids):
                for k in range(SCW):
                    DVE.tensor_scalar(out=diag4[:, qi, k, :], in0=identb, scalar1=col("ssm_w_conv", (j * SCW + k) * 32 + q32),
                                      scalar2=None, op0=ALU.mult)
            if last:
                st = xin[1]
                for qi, q32 in enumerate(chunk_ids):
                    B.dma("xin1", out=st[0:NS * (SCW - 1), qi * 128:(qi + 1) * 128],
                          in_=ssmc_s[j, :, :, q32 * 128:(q32 + 1) * 128].rearrange("s k c -> (s k) c"))
                bk = B.bank("aux")
                nr = NS * (SCW - 1)
                for qi in range(4):
                    PE.transpose(out=bk[:, qi * nr:(qi + 1) * nr], in_=st[0:nr, qi * 128:(qi + 1) * 128],
                                 identity=ident[0:nr, 0:nr])
                ACT.copy(out=sst.rearrange("p q s k -> p q (s k)"), in_=bk[:, 0:4 * nr].rearrange("p (q t) -> p q t", q=4))
            (wo,) = ws.get(f"mo{i}_{g}", [(wout[j][g * 256:(g + 1) * 256, :], 2)])
            ACT.copy(out=prevT, in_=STs[j][:, g, :])
            for (c0, n) in tiles:
                for qi, q32 in enumerate(chunk_ids):
                    bias = col("ssm_b_conv", j * 32 + q32)
                    bk = B.bank("mm")
                    if c0 < TP:
                        for k in range(SCW):
                            PE.matmul(out=bk[:, 0:n], lhsT=diag4[:, qi, k, :], rhs=xpre[:, qi, c0 + k + 1:c0 + k + 1 + n],
                                      start=(k == 0), stop=(k == SCW - 1))
                    else:
                        for k in range(SCW - 1):
                            PE.matmul(out=bk[:, 0:NS], lhsT=diag4[:, qi, k, :], rhs=sst[:, qi, :, k], start=(k == 0), stop=False)
                        PE.matmul(out=bk[:, 0:NS], lhsT=diag4[:, qi, SCW - 1, :],
                                  rhs=xpre[:, qi, SCW + TP:SCW + TP + NS], start=False, stop=True)
                    if qi < 2:
                        ACT.activation(out=xs[:, qi, 0:n], in_=bk[:, 0:n], func=AF.Silu, bias=bias)
                    elif qi == 2:
                        ACT.activation(out=B32[:, 0:n], in_=bk[:, 0:n], func=AF.Silu, bias=bias)
                        DVE.tensor_copy(out=BT[:, 0:n], in_=B32[:, 0:n])
                    else:
                        ACT.activation(out=CT[:, 0:n], in_=bk[:, 0:n], func=AF.Silu, bias=bias)
                        if c0 == TP:
                            ACT.activation(out=C32s[:, 0:NS], in_=bk[:, 0:NS], func=AF.Silu, bias=bias)
                def stage_a(cl):
                    c = c0 // 128 + cl
                    cs = slice(c * 128, (c + 1) * 128)
                    ls = slice(cl * 128, (cl + 1) * 128)
                    pp = ccnt[0] % 2
                    ccnt[0] += 1
                    xdt, dxdt, Btm, Ebuf, LT, Mb, Cs, R4 = ctmp[pp]
                    POOL.tensor_tensor(out=R4[0:4], in0=bcast(acT[0:4, cs], [4, 4, 128], 1),
                                       in1=bcast(ident[0:4, 0:4], [4, 4, 128], 2), op=ALU.mult)
                    bkT = B.bank("aux")
                    for q in range(2):
                        PE.transpose(out=bkT[:, q * 128:(q + 1) * 128], in_=xs[:, q, ls], identity=ident)
                    PE.transpose(out=bkT[:, 256:384], in_=B32[:, ls], identity=ident)
                    xT4 = bkT[:, 0:256].rearrange("p (a b) -> p a b", a=4)
                    DVE.tensor_tensor(out=xdt.rearrange("p (a b) -> p a b", a=4), in0=xT4,
                                      in1=bcast(dtm[:, c, :], [128, 4, 64], 2), op=ALU.mult)
                    DVE.tensor_tensor(out=dxdt.rearrange("p (a b) -> p a b", a=4), in0=xT4,
                                      in1=bcast(dd[:, c, :], [128, 4, 64], 2), op=ALU.mult)
                    ACT.copy(out=Btm, in_=bkT[:, 256:384])
                    bkC = B.bank("ssdC")
                    PE.matmul(out=bkC[:, 0:128], lhsT=BT[:, ls], rhs=CT[:, ls], start=True, stop=True)
                    bkE = B.bank("ssdE")
                    PE.matmul(out=bkE[:, 0:512], lhsT=ones4[0:4, 0:128], rhs=R4[0:4].rearrange("p a b -> p (a b)"),
                              start=True, stop=True)
                    ACT.activation(out=Ebuf.rearrange("p a b -> p (a b)"), in_=bkE[:, 0:512], func=AF.Exp)
                    return (ls, ctmp[pp], bkC, bkE, c)

                def stage_a2(info):
                    ls, (xdt, dxdt, Btm, Ebuf, LT, Mb, Cs, R4), bkC, bkE, c = info
                    for hh in range(4):
                        DVE.tensor_scalar(out=LT[:, hh, :], in0=bkE[:, hh * 128:(hh + 1) * 128], scalar1=nac[:, c, hh:hh + 1],
                                          scalar2=0.0, op0=ALU.add, op1=ALU.min)
                    ACT.activation(out=LT.rearrange("p a b -> p (a b)"), in_=LT.rearrange("p a b -> p (a b)"), func=AF.Exp)

                def stage_b(info):
                    ls, (xdt, dxdt, Btm, Ebuf, LT, Mb, Cs, R4), bkC, bkE, c = info
                    DVE.tensor_tensor(out=bkC[:, 0:128], in0=bkC[:, 0:128], in1=mask01, op=ALU.mult)
                    DVE.tensor_tensor(out=Mb, in0=LT, in1=bcast(bkC[:, 0:128], [128, 4, 128], 1), op=ALU.mult)
                    POOL.tensor_tensor(out=Cs, in0=Ebuf, in1=bcast(CT[:, ls], [128, 4, 128], 1), op=ALU.mult)
                    for m2 in range(2):
                        bkY = B.bank("mm")
                        for h2 in range(2):
                            hh = m2 * 2 + h2
                            PE.matmul(out=bkY[h2 * 64:(h2 + 1) * 64, 0:128], lhsT=xdt[:, hh * 64:(hh + 1) * 64], rhs=Mb[:, hh, :],
                                      start=True, stop=False)
                            PE.matmul(out=bkY[h2 * 64:(h2 + 1) * 64, 0:128], lhsT=prevT[:, hh * 64:(hh + 1) * 64], rhs=Cs[:, hh, :],
                                      start=False, stop=True)
                        DVE.scalar_tensor_tensor(out=y32[:, m2, ls], in0=xs[:, m2, ls], scalar=col("ssm_d_rep", j * 16 + 2 * g + m2),
                                                 in1=bkY[:, 0:128], op0=ALU.mult, op1=ALU.add)
                    bkS = B.bank("mm")
                    PE.matmul(out=bkS[:, 0:256], lhsT=Btm, rhs=dxdt, start=True, stop=True)
                    for hh in range(4):
                        DVE.scalar_tensor_tensor(out=STg[:, hh, :], in0=STg[:, hh, :], scalar=Ebuf[:, hh, 127:128],
                                                 in1=bkS[:, hh * 64:(hh + 1) * 64], op0=ALU.mult, op1=ALU.add)
                    ACT.copy(out=prevT, in_=STs[j][:, g, :])

                nch = n // 128 if c0 < TP else 0
                pend = stage_a(0) if nch else None
                if nch:
                    stage_a2(pend)
                for cl in range(nch):
                    nxt = stage_a(cl + 1) if cl + 1 < nch else None
                    stage_b(pend)
                    if nxt is not None:
                        stage_a2(nxt)
                    pend = nxt
                if last and c0 == 512:
                    bk = B.bank("aux")
                    for m2 in range(2):
                        PE.transpose(out=bk[:, m2 * 128:(m2 + 1) * 128], in_=STs[j][:, g, m2 * 128:(m2 + 1) * 128], identity=ident)
                    ACT.copy(out=stg, in_=bk[:, 0:256].rearrange("p (a b) -> p a b", a=2))
                    B.dma("stg", out=ns_p[j, g * 256:(g + 1) * 256, :].rearrange("(m q) n -> q m n", m=2), in_=stg)
                if c0 == TP:
                    sc = slice(TP, TP + NS)
                    R4s = ctmp[0][7]
                    RB = ctmp[0][5]
                    RC = ctmp[0][6]
                    tBs = [R4s[:, 0, :], R4s[:, 2, :]]
                    junk = R4s[:, 1, :]
                    bk = B.bank("aux")
                    for m2 in range(2):
                        PE.matmul(out=bk[:, m2 * 32:m2 * 32 + NS], lhsT=selhp[0:4, m2 * 128:(m2 + 1) * 128], rhs=tmpT[0:4, sc],
                                  start=True, stop=True)
                        PE.matmul(out=bk[:, 64 + m2 * 32:64 + m2 * 32 + NS], lhsT=selhp[0:4, m2 * 128:(m2 + 1) * 128],
                                  rhs=dtT[0:4, sc], start=True, stop=True)
                    for m2 in range(2):
                        ACT.activation(out=ea[:, m2, :], in_=bk[:, m2 * 32:m2 * 32 + NS], func=AF.Exp)
                        DVE.tensor_tensor(out=xdths[:, m2, :], in0=bk[:, 64 + m2 * 32:64 + m2 * 32 + NS], in1=xs[:, m2, 0:NS], op=ALU.mult)
                    bk2 = B.bank("aux")
                    PE.transpose(out=bk2[0:NS, 0:128], in_=B32[:, 0:NS], identity=ident)
                    PE.transpose(out=bk2[0:NS, 128:256], in_=C32s[:, 0:NS], identity=ident)
                    ACT.copy(out=BCs[0:NS, :], in_=bk2[0:NS, 0:256])
                    def s_load(s4):
                        S = S0[s4 % 2]
                        for sl in range(4):
                            B.dma(f"s0_{s4 % 2}", out=S[:, sl],
                                  in_=ssm_s[j, s4 * 4 + sl, g * 256:(g + 1) * 256, :].rearrange("(m q) n -> q m n", m=2))

                    s_load(0)
                    s_load(1)
                    for s4 in range(4):
                        idn = bcast(ident[0:NS, s4 * 4:(s4 + 1) * 4], [NS, 4, 128], 2)
                        DVE.tensor_tensor(out=RB[0:NS], in0=bcast(BCs[0:NS, 0:128], [NS, 4, 128], 1), in1=idn, op=ALU.mult)
                        DVE.tensor_tensor(out=RC[0:NS], in0=bcast(BCs[0:NS, 128:256], [NS, 4, 128], 1), in1=idn, op=ALU.mult)
                        bkB = B.bank("ssd")
                        bkC2 = B.bank("ssd")
                        PE.matmul(out=bkB[:, 0:512], lhsT=onesb[0:NS, 0:128], rhs=RB[0:NS].rearrange("p a b -> p (a b)"),
                                  start=True, stop=True)
                        PE.matmul(out=bkC2[:, 0:512], lhsT=onesb[0:NS, 0:128], rhs=RC[0:NS].rearrange("p a b -> p (a b)"),
                                  start=True, stop=True)
                        S = S0[s4 % 2]
                        key = f"s0_{s4 % 2}"
                        for sl in range(4):
                            sidx = s4 * 4 + sl
                            for m2 in range(2):
                                tB = tBs[m2]
                                DVE.tensor_scalar(out=tB, in0=bkB[:, sl * 128:(sl + 1) * 128], scalar1=xdths[:, m2, sidx:sidx + 1],
                                                  scalar2=None, op0=ALU.mult)
                                DVE.scalar_tensor_tensor(out=S[:, sl, m2, :], in0=S[:, sl, m2, :], scalar=ea[:, m2, sidx:sidx + 1],
                                                         in1=tB, op0=ALU.mult, op1=ALU.add)
                                DVE.scalar_tensor_tensor(out=junk, in0=S[:, sl, m2, :], scalar=1.0, in1=bkC2[:, sl * 128:(sl + 1) * 128],
                                                         op0=ALU.mult, op1=ALU.mult, accum_out=ysm[:, m2, sidx:sidx + 1])
                        for sl in range(4):
                            B.dma(key, out=ns_s[j, s4 * 4 + sl, g * 256:(g + 1) * 256, :].rearrange("(m q) n -> q m n", m=2), in_=S[:, sl])
                        if s4 + 2 < 4:
                            s_load(s4 + 2)
                    for m2 in range(2):
                        DVE.scalar_tensor_tensor(out=y32[:, m2, 0:NS], in0=xs[:, m2, 0:NS], scalar=col("ssm_d_rep", j * 16 + 2 * g + m2),
                                                 in1=ysm[:, m2, :], op0=ALU.mult, op1=ALU.add)
                DVE.tensor_tensor(out=t1[:, :, 0:n], in0=y32[:, :, 0:n], in1=sz[:, :, c0:c0 + n], op=ALU.mult)
                ACT.activation(out=sqb[:, 0:2, 0:n], in_=t1[:, :, 0:n], func=AF.Square)
                bk = B.bank("aux")
                for m in range(2):
                    PE.matmul(out=bk[:, 0:n], lhsT=onesb, rhs=sqb[:, m, 0:n], start=(m == 0), stop=(m == 1))
                ACT.activation(out=rstd[:, 0:n], in_=bk[:, 0:n], func=AF.Sqrt, scale=1.0 / 256.0, bias=epsc)
                DVE.reciprocal(out=rstd[:, 0:n], in_=rstd[:, 0:n])
                for q in range(2):
                    DVE.scalar_tensor_tensor(out=ygb[:, q, 0:n], in0=t1[:, q, 0:n], scalar=col("ssm_norm_g", j * 16 + 2 * g + q),
                                             in1=rstd[:, 0:n], op0=ALU.mult, op1=ALU.mult)
                for mo in range(8):
                    bk = B.bank("mm")
                    for k in range(2):
                        PE.matmul(out=bk[:, 0:n], lhsT=wo[:, k, mo * 128:(mo + 1) * 128], rhs=ygb[:, k, 0:n],
                                  start=(k == 0), stop=(k == 1))
                    DVE.tensor_tensor(out=h[:, mo, c0:c0 + n], in0=bk[:, 0:n], in1=h[:, mo, c0:c0 + n], op=ALU.add)
        if last and opts.get("mcut", 9) >= 1:
            for (src, wdt, dst) in ((xl32, SCW - 1, nsc_p[j]), (xs32s, NS, nsc_s[j, :, SCW - 2, :])):
                for pc in range(4):
                    o = xin[pc % 2]
                    for half in range(2):
                        bk = B.bank("aux")
                        for q in range(4):
                            PE.transpose(out=bk[0:wdt, q * 128:(q + 1) * 128], in_=src[:, pc * 8 + half * 4 + q, :], identity=ident)
                        ACT.copy(out=o[0:wdt, half * 512:(half + 1) * 512], in_=bk[0:wdt, :])
                    B.dma(f"xin{pc % 2}", out=dst[:, pc * 1024:(pc + 1) * 1024], in_=o[0:wdt, :])
        B.S.region = False
        B.pools = B.POOLS_DENSE
        B.release(mk)

    stage = opts.get("stage", "full")

    def program():
        for blk in range(NBLK):
            load_x(blk)
            for i in range(nlayers):
                if stage in ("s3", "full") and i % 2 == 0:
                    conformer(i, blk)
                if stage in ("s4", "full") and i % 2 == 1:
                    mamba(i, blk)
                if stage in ("s2", "s3", "full"):
                    ffn(i, blk)
                if stage in ("s2", "s3", "full"):
                    ple(i, blk)
            final_out(blk)

    B.S.dry = True
    ws.planning = True
    program()
    B.S.dry = False
    ws.planning = False
    B.bank_rr = {}
    cnt[0] = 0
    program()

    B.S.finalize(nc, B.stack)
    B.stack.close()
    return B


def _small_params(inp):
    rows = []
    idx = {}

    def add(name, arr):
        a = np.ascontiguousarray(arr, dtype=np.float32).reshape(-1)
        pad = (-a.size) % 128
        if pad:
            a = np.concatenate([a, np.zeros(pad, np.float32)])
        idx[name] = sum(r.shape[0] for r in rows)
        rows.append(a.reshape(-1, 128))

    for nm in ("g_mix", "g_ffn", "g_ple", "g_final", "cv_b_pw1", "cv_b_dw", "cv_ln_g", "cv_ln_b", "cv_b_pw2", "cv_w_dw"):
        add(nm, inp[nm])
    for nm in ("ssm_b_conv", "ssm_w_conv", "ssm_norm_g"):
        add(nm, inp[nm])
    add("ssm_d_rep", np.repeat(np.asarray(inp["ssm_d"]), HD, axis=1))
    for nm in ("ssm_dt_bias", "ssm_a_log"):
        a = np.zeros((2, NG, 128), np.float32)
        a[:, :, 0:4] = np.asarray(inp[nm]).reshape(2, NG, 4)
        add(nm + "_g", a)
    return np.concatenate(rows, 0), idx


def _consts():
    c = np.zeros((128, 128 + 512 + 4096 + 1024 + 128), np.float32)
    c[:, 0:128] = np.eye(128, dtype=np.float32)
    j = np.arange(128)[:, None]
    i = np.arange(128)[None, :]
    neg = np.where(i < j, -30000.0, 0.0).astype(np.float32)
    c[:, 128:128 + 512] = np.tile(neg, (1, 4))
    sel = np.zeros((128, 2, 128), np.float32)
    for m2 in range(2):
        sel[2 * m2, m2, 0:64] = 1.0
        sel[2 * m2 + 1, m2, 64:128] = 1.0
    c[:, 640:640 + 256] = sel.reshape(128, 256)
    rm = np.ones((128, 1024), np.float32)
    rm[:, ::128] = 0.0
    c[:, 4736:4736 + 1024] = rm
    c[:, 5760:5888] = 1.0
    return c


_CACHE = {}


_NEED = {
    "s1": {"x_p", "x_s", "consts", "smallp"},
    "s2": {"x_p", "x_s", "consts", "smallp", "p_p", "p_s", "ffn_w_gate", "ffn_w_up", "ffn_w_down", "ple_w_proj", "ple_w_gate"},
    "s3": {"x_p", "x_s", "consts", "smallp", "p_p", "p_s", "ffn_w_gate", "ffn_w_up", "ffn_w_down", "ple_w_proj", "ple_w_gate",
           "cv_w_pw1", "cv_w_pw2", "conv_s"},
    "s4": {"x_p", "x_s", "consts", "smallp", "ssm_w_in", "ssm_w_out", "ssm_s", "ssmc_s"},
}


def kernel(**inp):
    opts = dict(inp.pop("_opts", {}) or {})
    if opts.get("stage", "full") in _NEED:
        opts["need"] = _NEED[opts.get("need_as", opts["stage"])]
    smallp, colidx = _small_params(inp)
    opts["ncolrows"] = smallp.shape[0]
    opts["colidx"] = colidx
    B = build(opts)
    global _LASTB
    _LASTB = B
    cst = _consts()
    in_maps = []
    for c in range(NCORES):
        m = {
            "x_p": np.ascontiguousarray(inp["x_prompt"][c]),
            "x_s": np.ascontiguousarray(inp["x_sample"][c * NS:(c + 1) * NS, 0, :]),
            "consts": cst,
            "smallp": smallp,
            "p_p": np.ascontiguousarray(inp["p_prompt"][:, c]),
            "p_s": np.ascontiguousarray(inp["p_sample"][:, c * NS:(c + 1) * NS, 0, :]),
        }
        m["conv_s"] = np.ascontiguousarray(inp["state_conv_mixer"][:, c * NS:(c + 1) * NS])
        m["ssm_s"] = np.ascontiguousarray(inp["state_ssm"][:, c * NS:(c + 1) * NS]).reshape(2, NS, NH * HD, DST)
        m["ssmc_s"] = np.ascontiguousarray(inp["state_ssm_conv"][:, c * NS:(c + 1) * NS])
        for wn in ("ffn_w_gate", "ffn_w_up", "ffn_w_down", "ple_w_proj", "ple_w_gate", "cv_w_pw1", "cv_w_pw2",
                   "ssm_w_in", "ssm_w_out"):
            m[wn] = inp[wn]
        in_maps.append({k: (v if tuple(v.shape) == B.in_shapes[k] else np.zeros(B.in_shapes[k], np.float32))
                        for k, v in m.items() if k in B.ins})
    ncr = opts.get("ncores", NCORES)
    res = run_bass_kernel_spmd(B.nc, in_maps[:ncr], core_ids=list(range(ncr)))
    R = list(res.results) + [res.results[0]] * (NCORES - ncr)
    y_prompt = np.stack([R[c]["y_p"] for c in range(NCORES)], 0)
    y_sample = np.concatenate([R[c]["y_s"] for c in range(NCORES)], 0)[:, None, :]
    nc_p = np.stack([R[c]["nc_p"] for c in range(NCORES)], 1)
    nc_s = np.concatenate([R[c]["nc_s"] for c in range(NCORES)], 1)
    ns_p = np.stack([R[c]["ns_p"] for c in range(NCORES)], 1).reshape(2, NCORES, NH, HD, DST)
    nsc_p = np.stack([R[c]["nsc_p"] for c in range(NCORES)], 1)
    ns_s = np.concatenate([R[c]["ns_s"] for c in range(NCORES)], 1).reshape(2, NCORES * NS, NH, HD, DST)
    nsc_s = np.concatenate([R[c]["nsc_s"] for c in range(NCORES)], 1)
    return (y_prompt, y_sample, nc_p, ns_p, nsc_p, nc_s, ns_s, nsc_s)
```

```python
import os
import numpy as np
from contextlib import ExitStack
import concourse.bass as bass
import concourse.mybir as mybir
from concourse.ap import AP as _AP
from concourse.bass_utils import run_bass_kernel_spmd

F32 = mybir.dt.float32
BF16 = mybir.dt.bfloat16
ALU = mybir.AluOpType
AF = mybir.ActivationFunctionType
ISZ = {F32: 4, BF16: 2, mybir.dt.float32r: 4}

NCORES = 8
D = 1024
SEQ = 2048
DEPTH = 4
NBLK = 2
TP = 1024
NS = 16
PLE = 256
DFF = 2816
CW = 31
DIN = 2048
NH = 32
HD = 64
NG = 8
DST = 128
SCW = 4
SCD = 4096
SIN = DIN + SCD + NH
EPS = 1e-6
COMPUTE = ("pe", "act", "dve", "pool")
SAME_ENGINE_SYNC = True


class _Op:
    __slots__ = ("meth", "kw", "waits", "sig", "sigidx", "dmakey", "dmaidx")


class Sched:
    def __init__(self):
        self.ops = {e: [] for e in ("pe", "act", "dve", "pool", "sp")}
        self.recs = {"SB": [], "PSUM": []}
        self.known = {e: {} for e in self.ops}
        self.dmacnt = {}
        self.base = {}

    def _box(self, ap):
        isz = ISZ[ap.dtype]
        pat = ap.ap
        pstride = pat[0][0] * isz
        off = ap.offset * isz
        p0 = off // pstride
        f0 = off % pstride + self.base[ap.name]
        ext = 0
        for st, cnt in pat[1:]:
            ext += abs(st) * (cnt - 1)
        p1 = p0 + pat[0][1]
        f1 = f0 + (ext + 1) * isz
        if str(ap.space) == "PSUM":
            p0 = p0 // 32 * 32
            p1 = (p1 + 31) // 32 * 32
            f0 = f0 // 2048 * 2048
            f1 = f0 + 2048
        return (p0, p1, f0, f1)

    dry = False

    region = False
    rcount = 0
    rlimit = 10 ** 9

    def emit(self, eng, meth, dmakey=None, **kw):
        if self.dry:
            return None
        if self.region:
            self.rcount += 1
            if self.rcount > self.rlimit:
                return None
        op = _Op()
        op.meth = meth
        op.kw = kw
        op.sig = False
        op.dmakey = dmakey
        lst = self.ops[eng]
        if dmakey is not None:
            idx = self.dmacnt.get(dmakey, 0)
            self.dmacnt[dmakey] = idx + 1
            op.dmaidx = idx
            prod, pseq = "dma:" + dmakey, idx
        else:
            prod, pseq = eng, len(lst)
        deps = {}
        acc = []
        for key, v in kw.items():
            if not isinstance(v, _AP):
                continue
            sp = str(v.space)
            if sp not in ("SB", "PSUM"):
                continue
            isw = key in ("out", "accum_out", "ap")
            box = self._box(v)
            acc.append((sp, box, isw))
            for r in self.recs[sp]:
                if r[0] < box[1] and box[0] < r[1] and r[2] < box[3] and box[2] < r[3]:
                    if isw or r[4] or (sp == "PSUM" and r[5] != prod):
                        rp, rs = r[5], r[6]
                        if rp == prod:
                            if not rp.startswith("dma:"):
                                if eng == "pe" or not SAME_ENGINE_SYNC:
                                    continue
                                if not r[4]:
                                    continue
                            if rs == pseq:
                                continue
                        if rp.startswith("dma:"):
                            rs = self.dmacnt[rp[4:]] - 1 - (1 if rp == prod else 0)
                        if deps.get(rp, -1) < rs:
                            deps[rp] = rs
        kn = self.known[eng]
        waits = []
        for rp, rs in deps.items():
            if kn.get(rp, -1) >= rs:
                continue
            kn[rp] = rs
            waits.append((rp, rs))
            if not rp.startswith("dma:"):
                self.ops[rp][rs].sig = True
        op.waits = waits
        for sp, box, isw in acc:
            l2 = self.recs[sp]
            if isw:
                l2[:] = [r for r in l2 if not (box[0] <= r[0] and r[1] <= box[1] and box[2] <= r[2] and r[3] <= box[3])]
            else:
                l2[:] = [r for r in l2 if not (r[5] == prod and not r[4] and box[0] <= r[0] and r[1] <= box[1]
                                               and box[2] <= r[2] and r[3] <= box[3])]
            l2.append((box[0], box[1], box[2], box[3], isw, prod, pseq))
        lst.append(op)
        return op

    def finalize(self, nc, stack):
        CH = 16000
        sems = {}
        for e in COMPUTE:
            n = 0
            for op in self.ops[e]:
                if op.sig:
                    op.sigidx = n
                    n += 1
            sems[e] = [stack.enter_context(nc.semaphore(f"s_{e}_{i}")) for i in range(n // CH + 1)]
        for key in self.dmacnt:
            sems["dma:" + key] = stack.enter_context(nc.semaphore("d_" + key))
        block = stack.enter_context(nc.Block())
        ops = self.ops
        dmacnt = self.dmacnt

        def body(ename, final=False):
            def f(eng):
                for op in ops[ename]:
                    for rp, rs in op.waits:
                        if rp.startswith("dma:"):
                            eng.wait_ge(sems[rp], 16 * (rs + 1))
                        else:
                            si = ops[rp][rs].sigidx
                            eng.wait_ge(sems[rp][si // CH], si % CH + 1)
                    ins = getattr(eng, op.meth)(**op.kw)
                    if op.dmakey is not None:
                        ins.then_inc(sems["dma:" + op.dmakey], 16)
                    elif op.sig:
                        ins.then_inc(sems[ename][op.sigidx // CH], 1)
                if final:
                    for key, cnt in dmacnt.items():
                        eng.wait_ge(sems["dma:" + key], 16 * cnt)
            return f

        block.tensor(body("pe"))
        block.scalar(body("act"))
        block.vector(body("dve"))
        block.gpsimd(body("pool"))
        block.sync(body("sp", final=True))


class Eng:
    def __init__(self, sched, name):
        self.s, self.n = sched, name

    def __getattr__(self, meth):
        def f(**kw):
            return self.s.emit(self.n, meth, **kw)
        return f


class Builder:
    def __init__(self, opts):
        self.opts = opts
        self.nc = bass.Bass("TRN2", target_bir_lowering=False)
        self.stack = ExitStack()
        self.S = Sched()
        self.PE, self.ACT, self.DVE, self.POOL = (Eng(self.S, e) for e in COMPUTE)
        self.ins = {}
        self.in_shapes = {}
        self.outs = {}
        self.arena_off = 0

    def din(self, name, shape):
        need = self.opts.get("need")
        if need is not None and name not in need:
            shape = [1] * len(shape)
        self.in_shapes[name] = tuple(shape)
        t = self.nc.dram_tensor(name, list(shape), F32, kind="ExternalInput").ap()
        self.ins[name] = t
        return t

    def dout(self, name, shape):
        t = self.nc.dram_tensor(name, list(shape), F32, kind="ExternalOutput").ap()
        self.outs[name] = t
        return t

    def dma(self, key, out, in_):
        return self.S.emit("sp", "dma_start", dmakey=key, out=out, in_=in_)

    def setup_mem(self):
        nc = self.nc
        self.ARENA_BYTES = 212480
        self.arena = self.stack.enter_context(nc.sbuf_tensor("arena", [128, self.ARENA_BYTES // 4], F32))
        self.S.base["arena"] = 0
        self.banks = []
        for i in range(8):
            b = self.stack.enter_context(nc.psum_tensor(f"bank{i}", [128, 512], F32))
            self.S.base[f"bank{i}"] = i * 2048
            self.banks.append(b)
        self.bank_rr = {}
        self.POOLS_DENSE = {"mm": (0, 1, 2, 3, 4), "aux": (5, 6, 7)}
        self.POOLS_SSD = {"mm": (0, 1, 2, 6), "aux": (5,), "ssdC": (3, 4), "ssdE": (7,), "ssd": (7, 6)}
        self.pools = self.POOLS_DENSE

    def alloc(self, shape, dtype):
        n = 1
        for s in shape:
            n *= s
        nbytes = (n * ISZ[dtype] + 31) // 32 * 32
        off = self.arena_off
        assert off + nbytes <= self.ARENA_BYTES, ("SBUF arena overflow", off, nbytes)
        self.arena_off = off + nbytes
        v = self.arena[:, off // 4:(off + nbytes) // 4]
        if dtype != F32:
            v = v.bitcast(dtype)
        v = v[:, 0:n]
        if len(shape) == 2:
            v = v.rearrange("p (a b) -> p a b", a=shape[0])
        elif len(shape) == 3:
            v = v.rearrange("p (a b c) -> p a b c", a=shape[0], b=shape[1])
        elif len(shape) == 4:
            v = v.rearrange("p (a b c d) -> p a b c d", a=shape[0], b=shape[1], c=shape[2])
        return v

    def mark(self):
        return self.arena_off

    def release(self, m):
        self.arena_off = m

    def bank(self, pool):
        pools = self.pools
        ids = pools[pool]
        i = self.bank_rr.get(pool, 0)
        self.bank_rr[pool] = i + 1
        return self.banks[ids[i % len(ids)]]


class WStream:
    NST, NWB, CAP = 2, 3, 2112

    def __init__(self, B):
        self.B = B
        self.st = [B.alloc([self.CAP], F32) for _ in range(self.NST)]
        self.wb = [B.alloc([self.CAP], BF16) for _ in range(self.NWB)]
        self.plan = []
        self.planning = True
        self.nl = 0
        self.nu = 0

    def _views(self, buf, pieces):
        out = []
        off = 0
        for src, kc in pieces:
            n = src.shape[1]
            out.append(buf[:, off:off + kc * n].rearrange("p (k n) -> p k n", k=kc))
            off += kc * n
        assert off <= self.CAP, off
        return out

    def _load(self, i):
        name, pieces = self.plan[i]
        sv = self._views(self.st[i % self.NST], pieces)
        for (src, kc), v in zip(pieces, sv):
            self.B.dma(f"w{i % self.NST}", out=v, in_=src.rearrange("(k p) n -> p k n", p=128))
        tot = sum(kc * src.shape[1] for src, kc in pieces)
        self.B.POOL.tensor_copy(out=self.wb[i % self.NWB][:, 0:tot], in_=self.st[i % self.NST][:, 0:tot])

    def get(self, name, pieces):
        if self.planning:
            self.plan.append((name, pieces))
            return self._views(self.wb[0], pieces)
        i = self.nu
        assert self.plan[i][0] == name, (self.plan[i][0], name)
        while self.nl < min(len(self.plan), i + 3):
            self._load(self.nl)
            self.nl += 1
        self.nu += 1
        return self._views(self.wb[i % self.NWB], pieces)


def tiles_of(blk):
    t = [(0, 512), (512, 512)]
    if blk == NBLK - 1:
        t.append((1024, NS))
    return t


def build(opts):
    B = Builder(opts)
    nc = B.nc
    PE, ACT, DVE, POOL = B.PE, B.ACT, B.DVE, B.POOL
    nlayers = opts.get("nlayers", DEPTH)
    x_p = B.din("x_p", [SEQ, D])
    x_s = B.din("x_s", [NS, D])
    consts = B.din("consts", [128, 128 + 512 + 4096 + 1024 + 128])
    NCOLROWS = opts["ncolrows"]
    smallp = B.din("smallp", [NCOLROWS, 128])
    g_cols = opts["colidx"]
    y_p = B.dout("y_p", [SEQ, D])
    y_s = B.dout("y_s", [NS, D])
    p_p = B.din("p_p", [DEPTH, SEQ, PLE])
    p_s = B.din("p_s", [DEPTH, NS, PLE])
    wg = B.din("ffn_w_gate", [DEPTH, D, DFF])
    wu = B.din("ffn_w_up", [DEPTH, D, DFF])
    wd = B.din("ffn_w_down", [DEPTH, DFF, D])
    wpp = B.din("ple_w_proj", [DEPTH, PLE, D])
    wpg = B.din("ple_w_gate", [DEPTH, D, D])
    w1 = B.din("cv_w_pw1", [2, D, 2 * D])
    w2 = B.din("cv_w_pw2", [2, D, D])
    conv_s = B.din("conv_s", [2, NS, CW - 1, D])
    win = B.din("ssm_w_in", [2, D, SIN])
    wout = B.din("ssm_w_out", [2, DIN, D])
    ssm_s = B.din("ssm_s", [2, NS, NH * HD, DST])
    ssmc_s = B.din("ssmc_s", [2, NS, SCW - 1, SCD])
    ns_p = B.dout("ns_p", [2, NH * HD, DST])
    nsc_p = B.dout("nsc_p", [2, SCW - 1, SCD])
    ns_s = B.dout("ns_s", [2, NS, NH * HD, DST])
    nsc_s = B.dout("nsc_s", [2, NS, SCW - 1, SCD])
    nc_p = B.dout("nc_p", [2, CW - 1, D])
    nc_s = B.dout("nc_s", [2, NS, CW - 1, D])

    B.setup_mem()
    T = TP + NS
    h = B.alloc([8, T], F32)
    u = B.alloc([8, T], BF16)
    ident = B.alloc([128], F32)
    onesb = B.alloc([128], BF16)
    ncolt = (NCOLROWS + 127) // 128
    cols = B.alloc([ncolt * 128], F32)
    identb = B.alloc([128], BF16)
    vprefix = [B.alloc([8, CW - 1], BF16) for _ in range(2)]
    xin = [B.alloc([D], F32) for _ in range(2)]
    rstd = B.alloc([512], F32)
    sqb = B.alloc([8, 512], BF16)
    ws = WStream(B)
    scr4 = B.alloc([4, 512], F32)
    sgb = [scr4[:, 0, :], scr4[:, 1, :]]
    tmpb = [scr4[:, 2, :], scr4[:, 3, :]]
    scr_bf = B.arena[:, (B.arena_off - 4 * 512 * 4) // 4:B.arena_off // 4].bitcast(BF16).rearrange("p (a b) -> p a b", a=8)
    STs = [B.alloc([NG, 256], F32) for _ in range(2)]
    xprefix = [B.alloc([32, SCW], BF16) for _ in range(2)]
    cnt = [0]

    B.dma("c0", out=ident, in_=consts[:, 0:128])
    DVE.memset(ap=onesb, constant=1.0)
    DVE.tensor_copy(out=identb, in_=ident)
    for t in range(ncolt):
        r0 = t * 128
        nr = min(128, NCOLROWS - r0)
        st = xin[t % 2]
        B.dma(f"xin{t % 2}", out=st[0:nr, 0:128], in_=smallp[r0:r0 + nr, :])
        bk = B.bank("aux")
        PE.transpose(out=bk[:, 0:nr], in_=st[0:nr, 0:128], identity=ident[0:nr, 0:nr])
        DVE.tensor_copy(out=cols[:, r0:r0 + nr], in_=bk[:, 0:nr])

    def col(name, i=0):
        c = g_cols[name] + i
        return cols[:, c:c + 1]

    def load_x(blk):
        for tc in range(8):
            st = xin[tc % 2]
            B.dma(f"xin{tc % 2}", out=st, in_=x_p[blk * TP + tc * 128: blk * TP + (tc + 1) * 128, :])
            for half in range(2):
                bk = B.bank("aux")
                for q in range(4):
                    m = half * 4 + q
                    PE.transpose(out=bk[:, q * 128:(q + 1) * 128], in_=st[:, m * 128:(m + 1) * 128], identity=ident)
                ACT.copy(out=h[:, half * 4:half * 4 + 4, tc * 128:(tc + 1) * 128],
                         in_=bk[:, :].rearrange("p (q t) -> p q t", q=4))
        if blk == NBLK - 1:
            st = xin[0]
            B.dma("xin0", out=st[0:NS, :], in_=x_s)
            for half in range(2):
                bk = B.bank("aux")
                for q in range(4):
                    m = half * 4 + q
                    PE.transpose(out=bk[:, q * NS:(q + 1) * NS], in_=st[0:NS, m * 128:(m + 1) * 128],
                                 identity=ident[0:NS, 0:NS])
                ACT.copy(out=h[:, half * 4:half * 4 + 4, TP:TP + NS],
                         in_=bk[:, 0:4 * NS].rearrange("p (q t) -> p q t", q=4))

    def rms_stats(src, nch, c0, n, dim):
        ACT.activation(out=sqb[:, 0:nch, 0:n], in_=src[:, 0:nch, c0:c0 + n], func=AF.Square)
        bk = B.bank("aux")
        for m in range(nch):
            PE.matmul(out=bk[:, 0:n], lhsT=onesb, rhs=sqb[:, m, 0:n], start=(m == 0), stop=(m == nch - 1))
        ACT.activation(out=rstd[:, 0:n], in_=bk[:, 0:n], func=AF.Ln, scale=1.0 / dim, bias=epsc)
        ACT.activation(out=rstd[:, 0:n], in_=rstd[:, 0:n], func=AF.Exp, scale=-0.5)

    epsc = B.alloc([1], F32)
    DVE.memset(ap=epsc, constant=EPS)

    def rmsnorm_to(dst, gname, gi, blk):
        for (c0, n) in tiles_of(blk):
            rms_stats(h, 8, c0, n, D)
            for m in range(8):
                DVE.scalar_tensor_tensor(out=dst[:, m, c0:c0 + n], in0=h[:, m, c0:c0 + n], scalar=col(gname, gi * 8 + m),
                                         in1=rstd[:, 0:n], op0=ALU.mult, op1=ALU.mult)

    def final_out(blk):
        mk = B.mark()
        yv = B.alloc([8, 512], F32)
        ost = [B.alloc([D], F32) for _ in range(2)]
        cnt = 0
        for (c0, n) in tiles_of(blk):
            rms_stats(h, 8, c0, n, D)
            for m in range(8):
                DVE.scalar_tensor_tensor(out=yv[:, m, 0:n], in0=h[:, m, c0:c0 + n], scalar=col("g_final", m),
                                         in1=rstd[:, 0:n], op0=ALU.mult, op1=ALU.mult)
            ntc = (n + 127) // 128
            for tc in range(ntc):
                w = min(128, n - tc * 128)
                o = ost[cnt % 2]
                for half in range(2):
                    bk = B.bank("aux")
                    for q in range(4):
                        m = half * 4 + q
                        PE.transpose(out=bk[0:w, q * 128:(q + 1) * 128], in_=yv[:, m, tc * 128:tc * 128 + w],
                                     identity=ident)
                    ACT.copy(out=o[0:w, half * 512:(half + 1) * 512], in_=bk[0:w, :])
                if c0 < TP:
                    r0 = blk * TP + c0 + tc * 128
                    B.dma(f"ost{cnt % 2}", out=y_p[r0:r0 + w, :], in_=o[0:w, :])
                else:
                    B.dma(f"ost{cnt % 2}", out=y_s[:, :], in_=o[0:w, :])
                cnt += 1
        B.release(mk)

    def ffn(i, blk):
        tiles = tiles_of(blk)
        rmsnorm_to(u, "g_ffn", i, blk)
        mk = B.mark()
        hid = B.alloc([11, T], BF16)
        for half in range(2):
            f0 = half * 11
            for fa in range(0, 11, 1):
                nf = 1
                cs = slice((f0 + fa) * 128, (f0 + fa + nf) * 128)
                wgv, wuv = ws.get(f"ffn_gu{i}_{half}_{fa}", [(wg[i][:, cs], 8), (wu[i][:, cs], 8)])
                for q in range(nf):
                    f = fa + q
                    for (c0, n) in tiles:
                        bg = B.bank("mm")
                        bu = B.bank("mm")
                        for k in range(8):
                            PE.matmul(out=bg[:, 0:n], lhsT=wgv[:, k, q * 128:(q + 1) * 128], rhs=u[:, k, c0:c0 + n],
                                      start=(k == 0), stop=(k == 7))
                        for k in range(8):
                            PE.matmul(out=bu[:, 0:n], lhsT=wuv[:, k, q * 128:(q + 1) * 128], rhs=u[:, k, c0:c0 + n],
                                      start=(k == 0), stop=(k == 7))
                        sg = sgb[cnt[0] % 2]
                        cnt[0] += 1
                        ACT.activation(out=sg[:, 0:n], in_=bg[:, 0:n], func=AF.Silu)
                        DVE.tensor_tensor(out=hid[:, f, c0:c0 + n], in0=bu[:, 0:n], in1=sg[:, 0:n], op=ALU.mult)
            for mo2 in range(8):
                (wdv,) = ws.get(f"ffn_d{i}_{half}_{mo2}",
                                [(wd[i][f0 * 128:(f0 + 11) * 128, mo2 * 128:(mo2 + 1) * 128], 11)])
                for q in range(1):
                    mo = mo2 + q
                    for (c0, n) in tiles:
                        bk = B.bank("mm")
                        for k in range(11):
                            PE.matmul(out=bk[:, 0:n], lhsT=wdv[:, k, q * 128:(q + 1) * 128], rhs=hid[:, k, c0:c0 + n],
                                      start=(k == 0), stop=(k == 10))
                        DVE.tensor_tensor(out=h[:, mo, c0:c0 + n], in0=bk[:, 0:n], in1=h[:, mo, c0:c0 + n], op=ALU.add)
        B.release(mk)

    def ple(i, blk):
        tiles = tiles_of(blk)
        mk = B.mark()
        pT = B.alloc([2, T], BF16)
        for tc in range(8):
            st = xin[tc % 2]
            r0 = blk * TP + tc * 128
            B.dma(f"xin{tc % 2}", out=st[:, 0:PLE], in_=p_p[i, r0:r0 + 128, :])
            bk = B.bank("aux")
            for q in range(2):
                PE.transpose(out=bk[:, q * 128:(q + 1) * 128], in_=st[:, q * 128:(q + 1) * 128], identity=ident)
            ACT.copy(out=pT[:, 0:2, tc * 128:(tc + 1) * 128], in_=bk[:, 0:256].rearrange("p (q t) -> p q t", q=2))
        if blk == NBLK - 1:
            st = xin[0]
            B.dma("xin0", out=st[0:NS, 0:PLE], in_=p_s[i])
            bk = B.bank("aux")
            for q in range(2):
                PE.transpose(out=bk[:, q * NS:(q + 1) * NS], in_=st[0:NS, q * 128:(q + 1) * 128],
                             identity=ident[0:NS, 0:NS])
            ACT.copy(out=pT[:, 0:2, TP:TP + NS], in_=bk[:, 0:2 * NS].rearrange("p (q t) -> p q t", q=2))
        rmsnorm_to(u, "g_ple", i, blk)
        for mo2 in range(8):
            cs = slice(mo2 * 128, (mo2 + 1) * 128)
            wgv, wpv = ws.get(f"ple{i}_{mo2}", [(wpg[i][:, cs], 8), (wpp[i][:, cs], 2)])
            for q in range(1):
                mo = mo2 + q
                for (c0, n) in tiles:
                    bg = B.bank("mm")
                    bp = B.bank("mm")
                    for k in range(8):
                        PE.matmul(out=bg[:, 0:n], lhsT=wgv[:, k, q * 128:(q + 1) * 128], rhs=u[:, k, c0:c0 + n],
                                  start=(k == 0), stop=(k == 7))
                    for k in range(2):
                        PE.matmul(out=bp[:, 0:n], lhsT=wpv[:, k, q * 128:(q + 1) * 128], rhs=pT[:, k, c0:c0 + n],
                                  start=(k == 0), stop=(k == 1))
                    sg = sgb[cnt[0] % 2]
                    tm = tmpb[cnt[0] % 2]
                    cnt[0] += 1
                    ACT.activation(out=sg[:, 0:n], in_=bg[:, 0:n], func=AF.Sigmoid)
                    DVE.tensor_tensor(out=tm[:, 0:n], in0=bp[:, 0:n], in1=sg[:, 0:n], op=ALU.mult)
                    DVE.tensor_tensor(out=h[:, mo, c0:c0 + n], in0=tm[:, 0:n], in1=h[:, mo, c0:c0 + n], op=ALU.add)
        B.release(mk)

    def bc8(ap2, n):
        return ap2[:, 0:n].unsqueeze(1).to_broadcast([128, 8, n])

    def conformer(i, blk):
        j = i // 2
        tiles = tiles_of(blk)
        last = blk == NBLK - 1
        rmsnorm_to(u, "g_mix", i, blk)
        mk = B.mark()
        v = B.alloc([8, CW - 1 + T], BF16)
        c32 = B.alloc([8, T], F32)
        diag = [B.alloc([CW, 128], BF16)]
        cb = scr_bf
        mean = B.alloc([512], F32)
        msq = B.alloc([512], F32)
        v32 = B.alloc([8, CW - 1], F32)
        vs32 = B.alloc([8, NS], F32)
        if blk == 0:
            DVE.memset(ap=v[:, :, 0:CW - 1], constant=0.0)
        else:
            DVE.tensor_copy(out=v[:, :, 0:CW - 1], in_=vprefix[j])
        for mo in range(8):
            wa, wgt = ws.get(f"cv1_{i}_{mo}", [(w1[j][:, mo * 128:(mo + 1) * 128], 8),
                                              (w1[j][:, D + mo * 128:D + (mo + 1) * 128], 8)])
            for (c0, n) in tiles:
                ba = B.bank("mm")
                bg = B.bank("mm")
                for k in range(8):
                    PE.matmul(out=ba[:, 0:n], lhsT=wa[:, k, :], rhs=u[:, k, c0:c0 + n], start=(k == 0), stop=(k == 7))
                for k in range(8):
                    PE.matmul(out=bg[:, 0:n], lhsT=wgt[:, k, :], rhs=u[:, k, c0:c0 + n], start=(k == 0), stop=(k == 7))
                sg = sgb[cnt[0] % 2]
                cnt[0] += 1
                ACT.activation(out=sg[:, 0:n], in_=bg[:, 0:n], func=AF.Sigmoid, bias=col("cv_b_pw1", j * 16 + 8 + mo))
                ba_col = col("cv_b_pw1", j * 16 + mo)
                DVE.scalar_tensor_tensor(out=v[:, mo, CW - 1 + c0:CW - 1 + c0 + n], in0=ba[:, 0:n], scalar=ba_col,
                                         in1=sg[:, 0:n], op0=ALU.add, op1=ALU.mult)
                if last and c0 == 512:
                    DVE.scalar_tensor_tensor(out=v32[:, mo, :], in0=ba[:, 512 - (CW - 1):512], scalar=ba_col,
                                             in1=sg[:, 512 - (CW - 1):512], op0=ALU.add, op1=ALU.mult)
                if last and c0 == TP:
                    DVE.scalar_tensor_tensor(out=vs32[:, mo, :], in0=ba[:, 0:NS], scalar=ba_col,
                                             in1=sg[:, 0:NS], op0=ALU.add, op1=ALU.mult)
        if not last:
            DVE.tensor_copy(out=vprefix[j], in_=v[:, :, TP:TP + CW - 1])
        stt = None
        if last:
            stt = B.alloc([8, NS, CW - 1], BF16)
            for s4 in range(4):
                st = xin[s4 % 2]
                B.dma(f"xin{s4 % 2}", out=st[0:120, :], in_=conv_s[j, s4 * 4:(s4 + 1) * 4].rearrange("s k c -> (s k) c"))
                for half in range(2):
                    bk = B.bank("aux")
                    for q in range(4):
                        PE.transpose(out=bk[:, q * 120:(q + 1) * 120], in_=st[0:120, (half * 4 + q) * 128:(half * 4 + q + 1) * 128],
                                     identity=ident[0:120, 0:120])
                    ACT.copy(out=stt[:, half * 4:half * 4 + 4, s4 * 4:(s4 + 1) * 4, :].rearrange("p m s k -> p m (s k)"),
                             in_=bk[:, 0:480].rearrange("p (q t) -> p q t", q=4))
            B.dma("ncs", out=nc_s[j, :, 0:CW - 2, :], in_=conv_s[j, :, 1:CW - 1, :])
            for (src, wdt, dst) in ((v32, CW - 1, nc_p[j]), (vs32, NS, nc_s[j, :, CW - 2, :])):
                o = xin[0] if wdt == NS else xin[1]
                for half in range(2):
                    bk = B.bank("aux")
                    for q in range(4):
                        PE.transpose(out=bk[0:wdt, q * 128:(q + 1) * 128], in_=src[:, half * 4 + q, :], identity=ident)
                    ACT.copy(out=o[0:wdt, half * 512:(half + 1) * 512], in_=bk[0:wdt, :])
                B.dma("xin0" if wdt == NS else "xin1", out=dst, in_=o[0:wdt, :])
        for mo in range(8):
            dg = diag[0]
            for k in range(CW):
                DVE.tensor_scalar(out=dg[:, k, :], in0=identb, scalar1=col("cv_w_dw", (j * CW + k) * 8 + mo), scalar2=None,
                                  op0=ALU.mult)
            for (c0, n) in tiles:
                bk = B.bank("mm")
                if c0 < TP:
                    for k in range(CW):
                        PE.matmul(out=bk[:, 0:n], lhsT=dg[:, k, :], rhs=v[:, mo, c0 + k:c0 + k + n],
                                  start=(k == 0), stop=(k == CW - 1))
                else:
                    for k in range(CW - 1):
                        PE.matmul(out=bk[:, 0:NS], lhsT=dg[:, k, :], rhs=stt[:, mo, :, k], start=(k == 0), stop=False)
                    PE.matmul(out=bk[:, 0:NS], lhsT=dg[:, CW - 1, :], rhs=v[:, mo, CW - 1 + TP:CW - 1 + TP + NS],
                              start=False, stop=True)
                ACT.activation(out=c32[:, mo, c0:c0 + n], in_=bk[:, 0:n], func=AF.Identity, bias=col("cv_b_dw", j * 8 + mo))
        for (c0, n) in tiles:
            ACT.activation(out=sqb[:, :, 0:n], in_=c32[:, :, c0:c0 + n], func=AF.Square)
            DVE.tensor_copy(out=cb[:, :, 0:n], in_=c32[:, :, c0:c0 + n])
            bs = B.bank("aux")
            bq = B.bank("aux")
            for m in range(8):
                PE.matmul(out=bs[:, 0:n], lhsT=onesb, rhs=cb[:, m, 0:n], start=(m == 0), stop=(m == 7))
            for m in range(8):
                PE.matmul(out=bq[:, 0:n], lhsT=onesb, rhs=sqb[:, m, 0:n], start=(m == 0), stop=(m == 7))
            DVE.tensor_scalar(out=mean[:, 0:n], in0=bs[:, 0:n], scalar1=1.0 / D, scalar2=None, op0=ALU.mult)
            DVE.tensor_tensor(out=msq[:, 0:n], in0=mean[:, 0:n], in1=mean[:, 0:n], op=ALU.mult)
            DVE.scalar_tensor_tensor(out=msq[:, 0:n], in0=bq[:, 0:n], scalar=1.0 / D, in1=msq[:, 0:n],
                                     op0=ALU.mult, op1=ALU.subtract)
            ACT.activation(out=rstd[:, 0:n], in_=msq[:, 0:n], func=AF.Ln, bias=epsc)
            ACT.activation(out=rstd[:, 0:n], in_=rstd[:, 0:n], func=AF.Exp, scale=-0.5)
            DVE.tensor_tensor(out=c32[:, :, c0:c0 + n], in0=c32[:, :, c0:c0 + n], in1=bc8(mean, n), op=ALU.subtract)
            DVE.tensor_tensor(out=c32[:, :, c0:c0 + n], in0=c32[:, :, c0:c0 + n], in1=bc8(rstd, n), op=ALU.mult)
            for m in range(8):
                ACT.activation(out=u[:, m, c0:c0 + n], in_=c32[:, m, c0:c0 + n], func=AF.Silu,
                               scale=col("cv_ln_g", j * 8 + m), bias=col("cv_ln_b", j * 8 + m))
        for mo in range(8):
            (w2v,) = ws.get(f"cv2_{i}_{mo}", [(w2[j][:, mo * 128:(mo + 1) * 128], 8)])
            for (c0, n) in tiles:
                bk = B.bank("mm")
                for k in range(8):
                    PE.matmul(out=bk[:, 0:n], lhsT=w2v[:, k, :], rhs=u[:, k, c0:c0 + n], start=(k == 0), stop=(k == 7))
                DVE.scalar_tensor_tensor(out=h[:, mo, c0:c0 + n], in0=bk[:, 0:n], scalar=col("cv_b_pw2", j * 8 + mo),
                                         in1=h[:, mo, c0:c0 + n], op0=ALU.add, op1=ALU.add)
        B.release(mk)

    def bcast(ap2, shape, axis):
        return ap2.unsqueeze(axis).to_broadcast(shape)

    def mamba(i, blk):
        j = i // 2
        tiles = tiles_of(blk)
        last = blk == NBLK - 1
        Tb = TP + (NS if last else 0)
        rmsnorm_to(u, "g_mix", i, blk)
        mk = B.mark()
        B.S.region = True
        B.S.rlimit = opts.get("rlimit", 10 ** 9)
        B.pools = B.POOLS_SSD
        mask01 = B.alloc([128], F32)
        selhp = B.alloc([256], F32)
        ones4 = B.alloc([128], F32)
        dtT = B.alloc([T], F32)
        acT = B.alloc([T], F32)
        tmpT = B.alloc([T], F32)
        sz = B.alloc([2, T], F32)
        xpre = B.alloc([4, SCW + T], BF16)
        xs = B.alloc([2, 512], F32)
        y32 = B.alloc([2, 512], F32)
        B32 = B.alloc([512], F32)
        BT = B.alloc([512], BF16)
        CT = B.alloc([512], BF16)
        C32s = B.alloc([NS], F32)
        diag4 = B.alloc([4, SCW, 128], BF16)
        nac = B.alloc([8, 4], F32)
        dtm = B.alloc([8, 4], F32)
        dd = B.alloc([8, 4], F32)
        dl = B.alloc([4], F32)
        aneg = B.alloc([1], F32)
        ctmp = []
        for _ in range(2):
            ctmp.append((B.alloc([256], BF16), B.alloc([256], BF16), B.alloc([128], BF16), B.alloc([4, 128], F32),
                         B.alloc([4, 128], F32), B.alloc([4, 128], BF16), B.alloc([4, 128], BF16), B.alloc([4, 128], F32)))
        ccnt = [0]
        prevT = B.alloc([256], BF16)
        ygb = B.alloc([2, 512], BF16)
        xl32 = B.alloc([32, SCW - 1], F32)
        xs32s = B.alloc([32, NS], F32)
        stg = ctmp[1][7][:, 2:4, :]
        t1 = scr4[:, 0:2, :]
        if last:
            sst = B.alloc([4, NS, SCW - 1], BF16)
            ea = B.alloc([2, NS], F32)
            xdths = B.alloc([2, NS], F32)
            ysm = B.alloc([2, NS], F32)
            BCs = B.alloc([256], F32)
            S0 = [B.alloc([4, 2, 128], F32) for _ in range(2)]
        B.dma("xin0", out=xin[0][:, 0:128], in_=consts[:, 128:256])
        DVE.tensor_scalar(out=mask01, in0=xin[0][:, 0:128], scalar1=0.0, scalar2=None, op0=ALU.is_equal)
        B.dma("c1", out=selhp[0:4, :], in_=consts[0:4, 640:896])
        DVE.memset(ap=ones4, constant=1.0)
        if blk == 0:
            DVE.memset(ap=STs[j], constant=0.0)
        skip = opts.get("skip", ())
        if last and "nscs" not in skip:
            B.dma("nscs", out=nsc_s[j, :, 0:SCW - 2, :], in_=ssmc_s[j, :, 1:SCW - 1, :])
        s0cnt = 0
        for g in range(NG):
            colz = g * 256
            colx = DIN + g * 256
            colB = 2 * DIN + g * 128
            colC = 2 * DIN + 1024 + g * 128
            coldt = DIN + SCD + 4 * g
            chunk_ids = [2 * g, 2 * g + 1, 16 + g, 24 + g]
            STg = STs[j][:, g, :].rearrange("p (a b) -> p a b", a=4)
            if blk == 0:
                DVE.memset(ap=xpre[:, :, 0:SCW], constant=0.0)
            else:
                DVE.tensor_copy(out=xpre[:, :, 0:SCW], in_=xprefix[j][:, g * 4:(g + 1) * 4, :])
            (wz,) = ws.get(f"mz{i}_{g}", [(win[j][:, colz:colz + 256], 8)])
            for q in range(2):
                for (c0, n) in tiles:
                    bk = B.bank("mm")
                    for k in range(8):
                        PE.matmul(out=bk[:, 0:n], lhsT=wz[:, k, q * 128:(q + 1) * 128], rhs=u[:, k, c0:c0 + n],
                                  start=(k == 0), stop=(k == 7))
                    ACT.activation(out=sz[:, q, c0:c0 + n], in_=bk[:, 0:n], func=AF.Silu)

            def evac_pre(bk, qi, c0, n):
                ACT.copy(out=xpre[:, qi, SCW + c0:SCW + c0 + n], in_=bk[:, 0:n])
                if last and c0 == 512:
                    DVE.tensor_copy(out=xl32[:, chunk_ids[qi], :], in_=bk[:, 512 - (SCW - 1):512])
                if last and c0 == TP:
                    DVE.tensor_copy(out=xs32s[:, chunk_ids[qi], :], in_=bk[:, 0:NS])

            (wx,) = ws.get(f"mx{i}_{g}", [(win[j][:, colx:colx + 256], 8)])
            for q in range(2):
                for (c0, n) in tiles:
                    bk = B.bank("mm")
                    for k in range(8):
                        PE.matmul(out=bk[:, 0:n], lhsT=wx[:, k, q * 128:(q + 1) * 128], rhs=u[:, k, c0:c0 + n],
                                  start=(k == 0), stop=(k == 7))
                    evac_pre(bk, q, c0, n)
            wB, wC, wdt = ws.get(f"mbc{i}_{g}", [(win[j][:, colB:colB + 128], 8), (win[j][:, colC:colC + 128], 8),
                                                 (win[j][:, coldt:coldt + 4], 8)])
            for qi, wv in ((2, wB), (3, wC)):
                for (c0, n) in tiles:
                    bk = B.bank("mm")
                    for k in range(8):
                        PE.matmul(out=bk[:, 0:n], lhsT=wv[:, k, :], rhs=u[:, k, c0:c0 + n], start=(k == 0), stop=(k == 7))
                    evac_pre(bk, qi, c0, n)
            dtb = col("ssm_dt_bias_g", j * 8 + g)[0:4, :]
            for (c0, n) in (tiles if "dt" not in skip else ()):
                bk = B.bank("mm")
                for k in range(8):
                    PE.matmul(out=bk[0:4, 0:n], lhsT=wdt[:, k, :], rhs=u[:, k, c0:c0 + n], start=(k == 0), stop=(k == 7))
                ACT.activation(out=tmpT[0:4, c0:c0 + n], in_=bk[0:4, 0:n], func=AF.Exp, bias=dtb)
                ACT.activation(out=dtT[0:4, c0:c0 + n], in_=tmpT[0:4, c0:c0 + n], func=AF.Ln, bias=ones4[0:4, 0:1])
            if not last:
                DVE.tensor_copy(out=xprefix[j][:, g * 4:(g + 1) * 4, :], in_=xpre[:, :, TP:TP + SCW])
            mcut = opts.get("mcut", 9)
            if mcut <= -3:
                continue
            ACT.activation(out=aneg[0:4, :], in_=col("ssm_a_log_g", j * 8 + g)[0:4, :], func=AF.Exp)
            DVE.tensor_scalar(out=aneg[0:4, :], in0=aneg[0:4, :], scalar1=-1.0, scalar2=None, op0=ALU.mult)
            DVE.tensor_scalar(out=tmpT[0:4, 0:Tb], in0=dtT[0:4, 0:Tb], scalar1=aneg[0:4, 0:1], scalar2=None, op0=ALU.mult)
            for c in range(8):
                cs = slice(c * 128, (c + 1) * 128)
                DVE.tensor_tensor_scan(out=acT[0:4, cs], data0=ones4[0:4, 0:128], data1=tmpT[0:4, cs], initial=0.0,
                                       op0=ALU.mult, op1=ALU.add)
            for c in range(8):
                cs = slice(c * 128, (c + 1) * 128)
                bk = B.bank("aux")
                PE.transpose(out=bk[:, 0:4], in_=acT[0:4, cs], identity=ident[0:4, 0:4])
                PE.transpose(out=bk[:, 4:8], in_=dtT[0:4, cs], identity=ident[0:4, 0:4])
                DVE.tensor_scalar(out=dl[0:4, 0:4], in0=ident[0:4, 0:4], scalar1=acT[0:4, c * 128 + 127:c * 128 + 128],
                                  scalar2=None, op0=ALU.mult)
                PE.matmul(out=bk[:, 8:12], lhsT=ones4[0:4, 0:128], rhs=dl[0:4, 0:4], start=True, stop=True)
                DVE.tensor_scalar(out=nac[:, c, :], in0=bk[:, 0:4], scalar1=-1.0, scalar2=None, op0=ALU.mult)
                DVE.tensor_tensor(out=dd[:, c, :], in0=bk[:, 8:12], in1=nac[:, c, :], op=ALU.add)
                ACT.activation(out=dd[:, c, :], in_=dd[:, c, :], func=AF.Exp)
                DVE.tensor_copy(out=dtm[:, c, :], in_=bk[:, 4:8])
                DVE.tensor_tensor(out=dd[:, c, :], in0=dd[:, c, :], in1=dtm[:, c, :], op=ALU.mult)
            for qi, q32 in enumerate(chunk_ids):
                for k in range(SCW):
                    DVE.tensor_scalar(out=diag4[:, qi, k, :], in0=identb, scalar1=col("ssm_w_conv", (j * SCW + k) * 32 + q32),
                                      scalar2=None, op0=ALU.mult)
            if last:
                st = xin[1]
                for qi, q32 in enumerate(chunk_ids):
                    B.dma("xin1", out=st[0:NS * (SCW - 1), qi * 128:(qi + 1) * 128],
                          in_=ssmc_s[j, :, :, q32 * 128:(q32 + 1) * 128].rearrange("s k c -> (s k) c"))
                bk = B.bank("aux")
                nr = NS * (SCW - 1)
                for qi in range(4):
                    PE.transpose(out=bk[:, qi * nr:(qi + 1) * nr], in_=st[0:nr, qi * 128:(qi + 1) * 128],
                                 identity=ident[0:nr, 0:nr])
                ACT.copy(out=sst.rearrange("p q s k -> p q (s k)"), in_=bk[:, 0:4 * nr].rearrange("p (q t) -> p q t", q=4))
            (wo,) = ws.get(f"mo{i}_{g}", [(wout[j][g * 256:(g + 1) * 256, :], 2)])
            ACT.copy(out=prevT, in_=STs[j][:, g, :])
            for (c0, n) in tiles:
                for qi, q32 in enumerate(chunk_ids):
                    bias = col("ssm_b_conv", j * 32 + q32)
                    bk = B.bank("mm")
                    if c0 < TP:
                        for k in range(SCW):
                            PE.matmul(out=bk[:, 0:n], lhsT=diag4[:, qi, k, :], rhs=xpre[:, qi, c0 + k + 1:c0 + k + 1 + n],
                                      start=(k == 0), stop=(k == SCW - 1))
                    else:
                        for k in range(SCW - 1):
                            PE.matmul(out=bk[:, 0:NS], lhsT=diag4[:, qi, k, :], rhs=sst[:, qi, :, k], start=(k == 0), stop=False)
                        PE.matmul(out=bk[:, 0:NS], lhsT=diag4[:, qi, SCW - 1, :],
                                  rhs=xpre[:, qi, SCW + TP:SCW + TP + NS], start=False, stop=True)
                    if qi < 2:
                        ACT.activation(out=xs[:, qi, 0:n], in_=bk[:, 0:n], func=AF.Silu, bias=bias)
                    elif qi == 2:
                        ACT.activation(out=B32[:, 0:n], in_=bk[:, 0:n], func=AF.Silu, bias=bias)
                        DVE.tensor_copy(out=BT[:, 0:n], in_=B32[:, 0:n])
                    else:
                        ACT.activation(out=CT[:, 0:n], in_=bk[:, 0:n], func=AF.Silu, bias=bias)
                        if c0 == TP:
                            ACT.activation(out=C32s[:, 0:NS], in_=bk[:, 0:NS], func=AF.Silu, bias=bias)
                def stage_a(cl):
                    c = c0 // 128 + cl
                    cs = slice(c * 128, (c + 1) * 128)
                    ls = slice(cl * 128, (cl + 1) * 128)
                    pp = ccnt[0] % 2
                    ccnt[0] += 1
                    xdt, dxdt, Btm, Ebuf, LT, Mb, Cs, R4 = ctmp[pp]
                    POOL.tensor_tensor(out=R4[0:4], in0=bcast(acT[0:4, cs], [4, 4, 128], 1),
                                       in1=bcast(ident[0:4, 0:4], [4, 4, 128], 2), op=ALU.mult)
                    bkT = B.bank("aux")
                    for q in range(2):
                        PE.transpose(out=bkT[:, q * 128:(q + 1) * 128], in_=xs[:, q, ls], identity=ident)
                    PE.transpose(out=bkT[:, 256:384], in_=B32[:, ls], identity=ident)
                    xT4 = bkT[:, 0:256].rearrange("p (a b) -> p a b", a=4)
                    DVE.tensor_tensor(out=xdt.rearrange("p (a b) -> p a b", a=4), in0=xT4,
                                      in1=bcast(dtm[:, c, :], [128, 4, 64], 2), op=ALU.mult)
                    DVE.tensor_tensor(out=dxdt.rearrange("p (a b) -> p a b", a=4), in0=xT4,
                                      in1=bcast(dd[:, c, :], [128, 4, 64], 2), op=ALU.mult)
                    ACT.copy(out=Btm, in_=bkT[:, 256:384])
                    bkC = B.bank("ssdC")
                    PE.matmul(out=bkC[:, 0:128], lhsT=BT[:, ls], rhs=CT[:, ls], start=True, stop=True)
                    bkE = B.bank("ssdE")
                    PE.matmul(out=bkE[:, 0:512], lhsT=ones4[0:4, 0:128], rhs=R4[0:4].rearrange("p a b -> p (a b)"),
                              start=True, stop=True)
                    ACT.activation(out=Ebuf.rearrange("p a b -> p (a b)"), in_=bkE[:, 0:512], func=AF.Exp)
                    return (ls, ctmp[pp], bkC, bkE, c)

                def stage_a2(info):
                    ls, (xdt, dxdt, Btm, Ebuf, LT, Mb, Cs, R4), bkC, bkE, c = info
                    for hh in range(4):
                        DVE.tensor_scalar(out=LT[:, hh, :], in0=bkE[:, hh * 128:(hh + 1) * 128], scalar1=nac[:, c, hh:hh + 1],
                                          scalar2=0.0, op0=ALU.add, op1=ALU.min)
                    ACT.activation(out=LT.rearrange("p a b -> p (a b)"), in_=LT.rearrange("p a b -> p (a b)"), func=AF.Exp)

                def stage_b(info):
                    ls, (xdt, dxdt, Btm, Ebuf, LT, Mb, Cs, R4), bkC, bkE, c = info
                    DVE.tensor_tensor(out=bkC[:, 0:128], in0=bkC[:, 0:128], in1=mask01, op=ALU.mult)
                    DVE.tensor_tensor(out=Mb, in0=LT, in1=bcast(bkC[:, 0:128], [128, 4, 128], 1), op=ALU.mult)
                    POOL.tensor_tensor(out=Cs, in0=Ebuf, in1=bcast(CT[:, ls], [128, 4, 128], 1), op=ALU.mult)
                    for m2 in range(2):
                        bkY = B.bank("mm")
                        for h2 in range(2):
                            hh = m2 * 2 + h2
                            PE.matmul(out=bkY[h2 * 64:(h2 + 1) * 64, 0:128], lhsT=xdt[:, hh * 64:(hh + 1) * 64], rhs=Mb[:, hh, :],
                                      start=True, stop=False)
                            PE.matmul(out=bkY[h2 * 64:(h2 + 1) * 64, 0:128], lhsT=prevT[:, hh * 64:(hh + 1) * 64], rhs=Cs[:, hh, :],
                                      start=False, stop=True)
                        DVE.scalar_tensor_tensor(out=y32[:, m2, ls], in0=xs[:, m2, ls], scalar=col("ssm_d_rep", j * 16 + 2 * g + m2),
                                                 in1=bkY[:, 0:128], op0=ALU.mult, op1=ALU.add)
                    bkS = B.bank("mm")
                    PE.matmul(out=bkS[:, 0:256], lhsT=Btm, rhs=dxdt, start=True, stop=True)
                    for hh in range(4):
                        DVE.scalar_tensor_tensor(out=STg[:, hh, :], in0=STg[:, hh, :], scalar=Ebuf[:, hh, 127:128],
                                                 in1=bkS[:, hh * 64:(hh + 1) * 64], op0=ALU.mult, op1=ALU.add)
                    ACT.copy(out=prevT, in_=STs[j][:, g, :])

                nch = n // 128 if c0 < TP else 0
                pend = stage_a(0) if nch else None
                if nch:
                    stage_a2(pend)
                for cl in range(nch):
                    nxt = stage_a(cl + 1) if cl + 1 < nch else None
                    stage_b(pend)
                    if nxt is not None:
                        stage_a2(nxt)
                    pend = nxt
                if last and c0 == 512:
                    bk = B.bank("aux")
                    for m2 in range(2):
                        PE.transpose(out=bk[:, m2 * 128:(m2 + 1) * 128], in_=STs[j][:, g, m2 * 128:(m2 + 1) * 128], identity=ident)
                    ACT.copy(out=stg, in_=bk[:, 0:256].rearrange("p (a b) -> p a b", a=2))
                    B.dma("stg", out=ns_p[j, g * 256:(g + 1) * 256, :].rearrange("(m q) n -> q m n", m=2), in_=stg)
                if c0 == TP:
                    sc = slice(TP, TP + NS)
                    R4s = ctmp[0][7]
                    RB = ctmp[0][5]
                    RC = ctmp[0][6]
                    tBs = [R4s[:, 0, :], R4s[:, 2, :]]
                    junk = R4s[:, 1, :]
                    bk = B.bank("aux")
                    for m2 in range(2):
                        PE.matmul(out=bk[:, m2 * 32:m2 * 32 + NS], lhsT=selhp[0:4, m2 * 128:(m2 + 1) * 128], rhs=tmpT[0:4, sc],
                                  start=True, stop=True)
                        PE.matmul(out=bk[:, 64 + m2 * 32:64 + m2 * 32 + NS], lhsT=selhp[0:4, m2 * 128:(m2 + 1) * 128],
                                  rhs=dtT[0:4, sc], start=True, stop=True)
                    for m2 in range(2):
                        ACT.activation(out=ea[:, m2, :], in_=bk[:, m2 * 32:m2 * 32 + NS], func=AF.Exp)
                        DVE.tensor_tensor(out=xdths[:, m2, :], in0=bk[:, 64 + m2 * 32:64 + m2 * 32 + NS], in1=xs[:, m2, 0:NS], op=ALU.mult)
                    bk2 = B.bank("aux")
                    PE.transpose(out=bk2[0:NS, 0:128], in_=B32[:, 0:NS], identity=ident)
                    PE.transpose(out=bk2[0:NS, 128:256], in_=C32s[:, 0:NS], identity=ident)
                    ACT.copy(out=BCs[0:NS, :], in_=bk2[0:NS, 0:256])
                    def s_load(s4):
                        S = S0[s4 % 2]
                        for sl in range(4):
                            B.dma(f"s0_{s4 % 2}", out=S[:, sl],
                                  in_=ssm_s[j, s4 * 4 + sl, g * 256:(g + 1) * 256, :].rearrange("(m q) n -> q m n", m=2))

                    s_load(0)
                    s_load(1)
                    for s4 in range(4):
                        idn = bcast(ident[0:NS, s4 * 4:(s4 + 1) * 4], [NS, 4, 128], 2)
                        DVE.tensor_tensor(out=RB[0:NS], in0=bcast(BCs[0:NS, 0:128], [NS, 4, 128], 1), in1=idn, op=ALU.mult)
                        DVE.tensor_tensor(out=RC[0:NS], in0=bcast(BCs[0:NS, 128:256], [NS, 4, 128], 1), in1=idn, op=ALU.mult)
                        bkB = B.bank("ssd")
                        bkC2 = B.bank("ssd")
                        PE.matmul(out=bkB[:, 0:512], lhsT=onesb[0:NS, 0:128], rhs=RB[0:NS].rearrange("p a b -> p (a b)"),
                                  start=True, stop=True)
                        PE.matmul(out=bkC2[:, 0:512], lhsT=onesb[0:NS, 0:128], rhs=RC[0:NS].rearrange("p a b -> p (a b)"),
                                  start=True, stop=True)
                        S = S0[s4 % 2]
                        key = f"s0_{s4 % 2}"
                        for sl in range(4):
                            sidx = s4 * 4 + sl
                            for m2 in range(2):
                                tB = tBs[m2]
                                DVE.tensor_scalar(out=tB, in0=bkB[:, sl * 128:(sl + 1) * 128], scalar1=xdths[:, m2, sidx:sidx + 1],
                                                  scalar2=None, op0=ALU.mult)
                                DVE.scalar_tensor_tensor(out=S[:, sl, m2, :], in0=S[:, sl, m2, :], scalar=ea[:, m2, sidx:sidx + 1],
                                                         in1=tB, op0=ALU.mult, op1=ALU.add)
                                DVE.scalar_tensor_tensor(out=junk, in0=S[:, sl, m2, :], scalar=1.0, in1=bkC2[:, sl * 128:(sl + 1) * 128],
                                                         op0=ALU.mult, op1=ALU.mult, accum_out=ysm[:, m2, sidx:sidx + 1])
                        for sl in range(4):
                            B.dma(key, out=ns_s[j, s4 * 4 + sl, g * 256:(g + 1) * 256, :].rearrange("(m q) n -> q m n", m=2), in_=S[:, sl])
                        if s4 + 2 < 4:
                            s_load(s4 + 2)
                    for m2 in range(2):
                        DVE.scalar_tensor_tensor(out=y32[:, m2, 0:NS], in0=xs[:, m2, 0:NS], scalar=col("ssm_d_rep", j * 16 + 2 * g + m2),
                                                 in1=ysm[:, m2, :], op0=ALU.mult, op1=ALU.add)
                DVE.tensor_tensor(out=t1[:, :, 0:n], in0=y32[:, :, 0:n], in1=sz[:, :, c0:c0 + n], op=ALU.mult)
                ACT.activation(out=sqb[:, 0:2, 0:n], in_=t1[:, :, 0:n], func=AF.Square)
                bk = B.bank("aux")
                for m in range(2):
                    PE.matmul(out=bk[:, 0:n], lhsT=onesb, rhs=sqb[:, m, 0:n], start=(m == 0), stop=(m == 1))
                ACT.activation(out=rstd[:, 0:n], in_=bk[:, 0:n], func=AF.Ln, scale=1.0 / 256.0, bias=epsc)
                ACT.activation(out=rstd[:, 0:n], in_=rstd[:, 0:n], func=AF.Exp, scale=-0.5)
                for q in range(2):
                    DVE.scalar_tensor_tensor(out=ygb[:, q, 0:n], in0=t1[:, q, 0:n], scalar=col("ssm_norm_g", j * 16 + 2 * g + q),
                                             in1=rstd[:, 0:n], op0=ALU.mult, op1=ALU.mult)
                for mo in range(8):
                    bk = B.bank("mm")
                    for k in range(2):
                        PE.matmul(out=bk[:, 0:n], lhsT=wo[:, k, mo * 128:(mo + 1) * 128], rhs=ygb[:, k, 0:n],
                                  start=(k == 0), stop=(k == 1))
                    DVE.tensor_tensor(out=h[:, mo, c0:c0 + n], in0=bk[:, 0:n], in1=h[:, mo, c0:c0 + n], op=ALU.add)
        if last and opts.get("mcut", 9) >= 1:
            for (src, wdt, dst) in ((xl32, SCW - 1, nsc_p[j]), (xs32s, NS, nsc_s[j, :, SCW - 2, :])):
                for pc in range(4):
                    o = xin[pc % 2]
                    for half in range(2):
                        bk = B.bank("aux")
                        for q in range(4):
                            PE.transpose(out=bk[0:wdt, q * 128:(q + 1) * 128], in_=src[:, pc * 8 + half * 4 + q, :], identity=ident)
                        ACT.copy(out=o[0:wdt, half * 512:(half + 1) * 512], in_=bk[0:wdt, :])
                    B.dma(f"xin{pc % 2}", out=dst[:, pc * 1024:(pc + 1) * 1024], in_=o[0:wdt, :])
        B.S.region = False
        B.pools = B.POOLS_DENSE
        B.release(mk)

    stage = opts.get("stage", "full")

    def program():
        for blk in range(NBLK):
            load_x(blk)
            for i in range(nlayers):
                if stage in ("s3", "full") and i % 2 == 0:
                    conformer(i, blk)
                if stage in ("s4", "full") and i % 2 == 1:
                    mamba(i, blk)
                if stage in ("s2", "s3", "full"):
                    ffn(i, blk)
                if stage in ("s2", "s3", "full"):
                    ple(i, blk)
            final_out(blk)

    B.S.dry = True
    ws.planning = True
    program()
    B.S.dry = False
    ws.planning = False
    B.bank_rr = {}
    cnt[0] = 0
    program()

    B.S.finalize(nc, B.stack)
    B.stack.close()
    return B


def _small_params(inp):
    rows = []
    idx = {}

    def add(name, arr):
        a = np.ascontiguousarray(arr, dtype=np.float32).reshape(-1)
        pad = (-a.size) % 128
        if pad:
            a = np.concatenate([a, np.zeros(pad, np.float32)])
        idx[name] = sum(r.shape[0] for r in rows)
        rows.append(a.reshape(-1, 128))

    for nm in ("g_mix", "g_ffn", "g_ple", "g_final", "cv_b_pw1", "cv_b_dw", "cv_ln_g", "cv_ln_b", "cv_b_pw2", "cv_w_dw"):
        add(nm, inp[nm])
    for nm in ("ssm_b_conv", "ssm_w_conv", "ssm_norm_g"):
        add(nm, inp[nm])
    add("ssm_d_rep", np.repeat(np.asarray(inp["ssm_d"]), HD, axis=1))
    for nm in ("ssm_dt_bias", "ssm_a_log"):
        a = np.zeros((2, NG, 128), np.float32)
        a[:, :, 0:4] = np.asarray(inp[nm]).reshape(2, NG, 4)
        add(nm + "_g", a)
    return np.concatenate(rows, 0), idx


def _consts():
    c = np.zeros((128, 128 + 512 + 4096 + 1024 + 128), np.float32)
    c[:, 0:128] = np.eye(128, dtype=np.float32)
    j = np.arange(128)[:, None]
    i = np.arange(128)[None, :]
    neg = np.where(i < j, -30000.0, 0.0).astype(np.float32)
    c[:, 128:128 + 512] = np.tile(neg, (1, 4))
    sel = np.zeros((128, 2, 128), np.float32)
    for m2 in range(2):
        sel[2 * m2, m2, 0:64] = 1.0
        sel[2 * m2 + 1, m2, 64:128] = 1.0
    c[:, 640:640 + 256] = sel.reshape(128, 256)
    rm = np.ones((128, 1024), np.float32)
    rm[:, ::128] = 0.0
    c[:, 4736:4736 + 1024] = rm
    c[:, 5760:5888] = 1.0
    return c


_CACHE = {}


_NEED = {
    "s1": {"x_p", "x_s", "consts", "smallp"},
    "s2": {"x_p", "x_s", "consts", "smallp", "p_p", "p_s", "ffn_w_gate", "ffn_w_up", "ffn_w_down", "ple_w_proj", "ple_w_gate"},
    "s3": {"x_p", "x_s", "consts", "smallp", "p_p", "p_s", "ffn_w_gate", "ffn_w_up", "ffn_w_down", "ple_w_proj", "ple_w_gate",
           "cv_w_pw1", "cv_w_pw2", "conv_s"},
    "s4": {"x_p", "x_s", "consts", "smallp", "ssm_w_in", "ssm_w_out", "ssm_s", "ssmc_s"},
}


def kernel(**inp):
    opts = dict(inp.pop("_opts", {}) or {})
    if opts.get("stage", "full") in _NEED:
        opts["need"] = _NEED[opts.get("need_as", opts["stage"])]
    smallp, colidx = _small_params(inp)
    opts["ncolrows"] = smallp.shape[0]
    opts["colidx"] = colidx
    B = build(opts)
    global _LASTB
    _LASTB = B
    cst = _consts()
    in_maps = []
    for c in range(NCORES):
        m = {
            "x_p": np.ascontiguousarray(inp["x_prompt"][c]),
            "x_s": np.ascontiguousarray(inp["x_sample"][c * NS:(c + 1) * NS, 0, :]),
            "consts": cst,
            "smallp": smallp,
            "p_p": np.ascontiguousarray(inp["p_prompt"][:, c]),
            "p_s": np.ascontiguousarray(inp["p_sample"][:, c * NS:(c + 1) * NS, 0, :]),
        }
        m["conv_s"] = np.ascontiguousarray(inp["state_conv_mixer"][:, c * NS:(c + 1) * NS])
        m["ssm_s"] = np.ascontiguousarray(inp["state_ssm"][:, c * NS:(c + 1) * NS]).reshape(2, NS, NH * HD, DST)
        m["ssmc_s"] = np.ascontiguousarray(inp["state_ssm_conv"][:, c * NS:(c + 1) * NS])
        for wn in ("ffn_w_gate", "ffn_w_up", "ffn_w_down", "ple_w_proj", "ple_w_gate", "cv_w_pw1", "cv_w_pw2",
                   "ssm_w_in", "ssm_w_out"):
            m[wn] = inp[wn]
        in_maps.append({k: (v if tuple(v.shape) == B.in_shapes[k] else np.zeros(B.in_shapes[k], np.float32))
                        for k, v in m.items() if k in B.ins})
    ncr = opts.get("ncores", NCORES)
    res = run_bass_kernel_spmd(B.nc, in_maps[:ncr], core_ids=list(range(ncr)))
    R = list(res.results) + [res.results[0]] * (NCORES - ncr)
    y_prompt = np.stack([R[c]["y_p"] for c in range(NCORES)], 0)
    y_sample = np.concatenate([R[c]["y_s"] for c in range(NCORES)], 0)[:, None, :]
    nc_p = np.stack([R[c]["nc_p"] for c in range(NCORES)], 1)
    nc_s = np.concatenate([R[c]["nc_s"] for c in range(NCORES)], 1)
    ns_p = np.stack([R[c]["ns_p"] for c in range(NCORES)], 1).reshape(2, NCORES, NH, HD, DST)
    nsc_p = np.stack([R[c]["nsc_p"] for c in range(NCORES)], 1)
    ns_s = np.concatenate([R[c]["ns_s"] for c in range(NCORES)], 1).reshape(2, NCORES * NS, NH, HD, DST)
    nsc_s = np.concatenate([R[c]["nsc_s"] for c in range(NCORES)], 1)
    return (y_prompt, y_sample, nc_p, ns_p, nsc_p, nc_s, ns_s, nsc_s)
```

```python
import os
import numpy as np
from contextlib import ExitStack
import concourse.bass as bass
import concourse.mybir as mybir
from concourse.ap import AP as _AP
from concourse.bass_utils import run_bass_kernel_spmd

F32 = mybir.dt.float32
BF16 = mybir.dt.bfloat16
ALU = mybir.AluOpType
AF = mybir.ActivationFunctionType
ISZ = {F32: 4, BF16: 2, mybir.dt.float32r: 4}

NCORES = 8
D = 1024
SEQ = 2048
DEPTH = 4
NBLK = 2
TP = 1024
NS = 16
PLE = 256
DFF = 2816
CW = 31
DIN = 2048
NH = 32
HD = 64
NG = 8
DST = 128
SCW = 4
SCD = 4096
SIN = DIN + SCD + NH
EPS = 1e-6
COMPUTE = ("pe", "act", "dve", "pool")
SAME_ENGINE_SYNC = True


class _Op:
    __slots__ = ("meth", "kw", "waits", "sig", "sigidx", "dmakey", "dmaidx")


class Sched:
    def __init__(self):
        self.ops = {e: [] for e in ("pe", "act", "dve", "pool", "sp")}
        self.recs = {"SB": [], "PSUM": []}
        self.known = {e: {} for e in self.ops}
        self.dmacnt = {}
        self.base = {}

    def _box(self, ap):
        isz = ISZ[ap.dtype]
        pat = ap.ap
        pstride = pat[0][0] * isz
        off = ap.offset * isz
        p0 = off // pstride
        f0 = off % pstride + self.base[ap.name]
        ext = 0
        for st, cnt in pat[1:]:
            ext += abs(st) * (cnt - 1)
        p1 = p0 + pat[0][1]
        f1 = f0 + (ext + 1) * isz
        if str(ap.space) == "PSUM":
            p0 = p0 // 32 * 32
            p1 = (p1 + 31) // 32 * 32
            f0 = f0 // 2048 * 2048
            f1 = f0 + 2048
        return (p0, p1, f0, f1)

    dry = False

    region = False
    rcount = 0
    rlimit = 10 ** 9

    def emit(self, eng, meth, dmakey=None, **kw):
        if self.dry:
            return None
        if self.region:
            self.rcount += 1
            if self.rcount > self.rlimit:
                return None
        op = _Op()
        op.meth = meth
        op.kw = kw
        op.sig = False
        op.dmakey = dmakey
        lst = self.ops[eng]
        if dmakey is not None:
            idx = self.dmacnt.get(dmakey, 0)
            self.dmacnt[dmakey] = idx + 1
            op.dmaidx = idx
            prod, pseq = "dma:" + dmakey, idx
        else:
            prod, pseq = eng, len(lst)
        deps = {}
        acc = []
        for key, v in kw.items():
            if not isinstance(v, _AP):
                continue
            sp = str(v.space)
            if sp not in ("SB", "PSUM"):
                continue
            isw = key in ("out", "accum_out", "ap")
            box = self._box(v)
            acc.append((sp, box, isw))
            for r in self.recs[sp]:
                if r[0] < box[1] and box[0] < r[1] and r[2] < box[3] and box[2] < r[3]:
                    if isw or r[4] or (sp == "PSUM" and r[5] != prod):
                        rp, rs = r[5], r[6]
                        if rp == prod:
                            if not rp.startswith("dma:"):
                                if eng == "pe" or not SAME_ENGINE_SYNC:
                                    continue
                                if not r[4]:
                                    continue
                            if rs == pseq:
                                continue
                        if rp.startswith("dma:"):
                            rs = self.dmacnt[rp[4:]] - 1 - (1 if rp == prod else 0)
                        if deps.get(rp, -1) < rs:
                            deps[rp] = rs
        kn = self.known[eng]
        waits = []
        for rp, rs in deps.items():
            if kn.get(rp, -1) >= rs:
                continue
            kn[rp] = rs
            waits.append((rp, rs))
            if not rp.startswith("dma:"):
                self.ops[rp][rs].sig = True
        op.waits = waits
        for sp, box, isw in acc:
            l2 = self.recs[sp]
            if isw:
                l2[:] = [r for r in l2 if not (box[0] <= r[0] and r[1] <= box[1] and box[2] <= r[2] and r[3] <= box[3])]
            else:
                l2[:] = [r for r in l2 if not (r[5] == prod and not r[4] and box[0] <= r[0] and r[1] <= box[1]
                                               and box[2] <= r[2] and r[3] <= box[3])]
            l2.append((box[0], box[1], box[2], box[3], isw, prod, pseq))
        lst.append(op)
        return op

    def finalize(self, nc, stack):
        CH = 16000
        sems = {}
        for e in COMPUTE:
            n = 0
            for op in self.ops[e]:
                if op.sig:
                    op.sigidx = n
                    n += 1
            sems[e] = [stack.enter_context(nc.semaphore(f"s_{e}_{i}")) for i in range(n // CH + 1)]
        for key in self.dmacnt:
            sems["dma:" + key] = stack.enter_context(nc.semaphore("d_" + key))
        block = stack.enter_context(nc.Block())
        ops = self.ops
        dmacnt = self.dmacnt

        def body(ename, final=False):
            def f(eng):
                for op in ops[ename]:
                    for rp, rs in op.waits:
                        if rp.startswith("dma:"):
                            eng.wait_ge(sems[rp], 16 * (rs + 1))
                        else:
                            si = ops[rp][rs].sigidx
                            eng.wait_ge(sems[rp][si // CH], si % CH + 1)
                    ins = getattr(eng, op.meth)(**op.kw)
                    if op.dmakey is not None:
                        ins.then_inc(sems["dma:" + op.dmakey], 16)
                    elif op.sig:
                        ins.then_inc(sems[ename][op.sigidx // CH], 1)
                if final:
                    for key, cnt in dmacnt.items():
                        eng.wait_ge(sems["dma:" + key], 16 * cnt)
            return f

        block.tensor(body("pe"))
        block.scalar(body("act"))
        block.vector(body("dve"))
        block.gpsimd(body("pool"))
        block.sync(body("sp", final=True))


class Eng:
    def __init__(self, sched, name):
        self.s, self.n = sched, name

    def __getattr__(self, meth):
        def f(**kw):
            return self.s.emit(self.n, meth, **kw)
        return f


class Builder:
    def __init__(self, opts):
        self.opts = opts
        self.nc = bass.Bass("TRN2", target_bir_lowering=False)
        self.stack = ExitStack()
        self.S = Sched()
        self.PE, self.ACT, self.DVE, self.POOL = (Eng(self.S, e) for e in COMPUTE)
        self.ins = {}
        self.in_shapes = {}
        self.outs = {}
        self.arena_off = 0

    def din(self, name, shape):
        need = self.opts.get("need")
        if need is not None and name not in need:
            shape = [1] * len(shape)
        self.in_shapes[name] = tuple(shape)
        t = self.nc.dram_tensor(name, list(shape), F32, kind="ExternalInput").ap()
        self.ins[name] = t
        return t

    def dout(self, name, shape):
        t = self.nc.dram_tensor(name, list(shape), F32, kind="ExternalOutput").ap()
        self.outs[name] = t
        return t

    def dma(self, key, out, in_):
        return self.S.emit("sp", "dma_start", dmakey=key, out=out, in_=in_)

    def setup_mem(self):
        nc = self.nc
        self.ARENA_BYTES = 212480
        self.arena = self.stack.enter_context(nc.sbuf_tensor("arena", [128, self.ARENA_BYTES // 4], F32))
        self.S.base["arena"] = 0
        self.banks = []
        for i in range(8):
            b = self.stack.enter_context(nc.psum_tensor(f"bank{i}", [128, 512], F32))
            self.S.base[f"bank{i}"] = i * 2048
            self.banks.append(b)
        self.bank_rr = {}
        self.POOLS_DENSE = {"mm": (0, 1, 2, 3, 4), "aux": (5, 6, 7)}
        self.POOLS_SSD = {"mm": (0, 1, 2, 6), "aux": (5,), "ssdC": (3, 4), "ssdE": (7,), "ssd": (7, 6)}
        self.pools = self.POOLS_DENSE

    def alloc(self, shape, dtype):
        n = 1
        for s in shape:
            n *= s
        nbytes = (n * ISZ[dtype] + 31) // 32 * 32
        off = self.arena_off
        assert off + nbytes <= self.ARENA_BYTES, ("SBUF arena overflow", off, nbytes)
        self.arena_off = off + nbytes
        v = self.arena[:, off // 4:(off + nbytes) // 4]
        if dtype != F32:
            v = v.bitcast(dtype)
        v = v[:, 0:n]
        if len(shape) == 2:
            v = v.rearrange("p (a b) -> p a b", a=shape[0])
        elif len(shape) == 3:
            v = v.rearrange("p (a b c) -> p a b c", a=shape[0], b=shape[1])
        elif len(shape) == 4:
            v = v.rearrange("p (a b c d) -> p a b c d", a=shape[0], b=shape[1], c=shape[2])
        return v

    def mark(self):
        return self.arena_off

    def release(self, m):
        self.arena_off = m

    def bank(self, pool):
        pools = self.pools
        ids = pools[pool]
        i = self.bank_rr.get(pool, 0)
        self.bank_rr[pool] = i + 1
        return self.banks[ids[i % len(ids)]]


class WStream:
    NST, NWB, CAP = 2, 3, 2112

    def __init__(self, B):
        self.B = B
        self.st = [B.alloc([self.CAP], F32) for _ in range(self.NST)]
        self.wb = [B.alloc([self.CAP], BF16) for _ in range(self.NWB)]
        self.plan = []
        self.planning = True
        self.nl = 0
        self.nu = 0

    def _views(self, buf, pieces):
        out = []
        off = 0
        for src, kc in pieces:
            n = src.shape[1]
            out.append(buf[:, off:off + kc * n].rearrange("p (k n) -> p k n", k=kc))
            off += kc * n
        assert off <= self.CAP, off
        return out

    def _load(self, i):
        name, pieces = self.plan[i]
        sv = self._views(self.st[i % self.NST], pieces)
        for (src, kc), v in zip(pieces, sv):
            self.B.dma(f"w{i % self.NST}", out=v, in_=src.rearrange("(k p) n -> p k n", p=128))
        tot = sum(kc * src.shape[1] for src, kc in pieces)
        half = (tot // 2 + 31) // 32 * 32
        self.B.POOL.tensor_copy(out=self.wb[i % self.NWB][:, 0:half], in_=self.st[i % self.NST][:, 0:half])
        self.B.ACT.copy(out=self.wb[i % self.NWB][:, half:tot], in_=self.st[i % self.NST][:, half:tot])

    def get(self, name, pieces):
        if self.planning:
            self.plan.append((name, pieces))
            return self._views(self.wb[0], pieces)
        i = self.nu
        assert self.plan[i][0] == name, (self.plan[i][0], name)
        while self.nl < min(len(self.plan), i + 3):
            self._load(self.nl)
            self.nl += 1
        self.nu += 1
        return self._views(self.wb[i % self.NWB], pieces)


def tiles_of(blk):
    t = [(0, 512), (512, 512)]
    if blk == NBLK - 1:
        t.append((1024, NS))
    return t


def build(opts):
    B = Builder(opts)
    nc = B.nc
    PE, ACT, DVE, POOL = B.PE, B.ACT, B.DVE, B.POOL
    nlayers = opts.get("nlayers", DEPTH)
    x_p = B.din("x_p", [SEQ, D])
    x_s = B.din("x_s", [NS, D])
    consts = B.din("consts", [128, 128 + 512 + 4096 + 1024 + 128])
    NCOLROWS = opts["ncolrows"]
    smallp = B.din("smallp", [NCOLROWS, 128])
    g_cols = opts["colidx"]
    y_p = B.dout("y_p", [SEQ, D])
    y_s = B.dout("y_s", [NS, D])
    p_p = B.din("p_p", [DEPTH, SEQ, PLE])
    p_s = B.din("p_s", [DEPTH, NS, PLE])
    wg = B.din("ffn_w_gate", [DEPTH, D, DFF])
    wu = B.din("ffn_w_up", [DEPTH, D, DFF])
    wd = B.din("ffn_w_down", [DEPTH, DFF, D])
    wpp = B.din("ple_w_proj", [DEPTH, PLE, D])
    wpg = B.din("ple_w_gate", [DEPTH, D, D])
    w1 = B.din("cv_w_pw1", [2, D, 2 * D])
    w2 = B.din("cv_w_pw2", [2, D, D])
    conv_s = B.din("conv_s", [2, NS, CW - 1, D])
    win = B.din("ssm_w_in", [2, D, SIN])
    wout = B.din("ssm_w_out", [2, DIN, D])
    ssm_s = B.din("ssm_s", [2, NS, NH * HD, DST])
    ssmc_s = B.din("ssmc_s", [2, NS, SCW - 1, SCD])
    ns_p = B.dout("ns_p", [2, NH * HD, DST])
    nsc_p = B.dout("nsc_p", [2, SCW - 1, SCD])
    ns_s = B.dout("ns_s", [2, NS, NH * HD, DST])
    nsc_s = B.dout("nsc_s", [2, NS, SCW - 1, SCD])
    nc_p = B.dout("nc_p", [2, CW - 1, D])
    nc_s = B.dout("nc_s", [2, NS, CW - 1, D])

    B.setup_mem()
    T = TP + NS
    h = B.alloc([8, T], F32)
    u = B.alloc([8, T], BF16)
    ident = B.alloc([128], F32)
    onesb = B.alloc([128], BF16)
    ncolt = (NCOLROWS + 127) // 128
    cols = B.alloc([ncolt * 128], F32)
    identb = B.alloc([128], BF16)
    vprefix = [B.alloc([8, CW - 1], BF16) for _ in range(2)]
    xin = [B.alloc([D], F32) for _ in range(2)]
    rstd = B.alloc([512], F32)
    sqb = B.alloc([8, 512], BF16)
    ws = WStream(B)
    scr4 = B.alloc([4, 512], F32)
    sgb = [scr4[:, 0, :], scr4[:, 1, :]]
    tmpb = [scr4[:, 2, :], scr4[:, 3, :]]
    scr_bf = B.arena[:, (B.arena_off - 4 * 512 * 4) // 4:B.arena_off // 4].bitcast(BF16).rearrange("p (a b) -> p a b", a=8)
    STs = [B.alloc([NG, 256], F32) for _ in range(2)]
    xprefix = [B.alloc([32, SCW], BF16) for _ in range(2)]
    cnt = [0]

    B.dma("c0", out=ident, in_=consts[:, 0:128])
    DVE.memset(ap=onesb, constant=1.0)
    DVE.tensor_copy(out=identb, in_=ident)
    for t in range(ncolt):
        r0 = t * 128
        nr = min(128, NCOLROWS - r0)
        st = xin[t % 2]
        B.dma(f"xin{t % 2}", out=st[0:nr, 0:128], in_=smallp[r0:r0 + nr, :])
        bk = B.bank("aux")
        PE.transpose(out=bk[:, 0:nr], in_=st[0:nr, 0:128], identity=ident[0:nr, 0:nr])
        DVE.tensor_copy(out=cols[:, r0:r0 + nr], in_=bk[:, 0:nr])

    def col(name, i=0):
        c = g_cols[name] + i
        return cols[:, c:c + 1]

    def load_x(blk):
        for tc in range(8):
            st = xin[tc % 2]
            B.dma(f"xin{tc % 2}", out=st, in_=x_p[blk * TP + tc * 128: blk * TP + (tc + 1) * 128, :])
            for half in range(2):
                bk = B.bank("aux")
                for q in range(4):
                    m = half * 4 + q
                    PE.transpose(out=bk[:, q * 128:(q + 1) * 128], in_=st[:, m * 128:(m + 1) * 128], identity=ident)
                ACT.copy(out=h[:, half * 4:half * 4 + 4, tc * 128:(tc + 1) * 128],
                         in_=bk[:, :].rearrange("p (q t) -> p q t", q=4))
        if blk == NBLK - 1:
            st = xin[0]
            B.dma("xin0", out=st[0:NS, :], in_=x_s)
            for half in range(2):
                bk = B.bank("aux")
                for q in range(4):
                    m = half * 4 + q
                    PE.transpose(out=bk[:, q * NS:(q + 1) * NS], in_=st[0:NS, m * 128:(m + 1) * 128],
                                 identity=ident[0:NS, 0:NS])
                ACT.copy(out=h[:, half * 4:half * 4 + 4, TP:TP + NS],
                         in_=bk[:, 0:4 * NS].rearrange("p (q t) -> p q t", q=4))

    def rms_stats(src, nch, c0, n, dim):
        ACT.activation(out=sqb[:, 0:nch, 0:n], in_=src[:, 0:nch, c0:c0 + n], func=AF.Square)
        bk = B.bank("aux")
        for m in range(nch):
            PE.matmul(out=bk[:, 0:n], lhsT=onesb, rhs=sqb[:, m, 0:n], start=(m == 0), stop=(m == nch - 1))
        ACT.activation(out=rstd[:, 0:n], in_=bk[:, 0:n], func=AF.Ln, scale=1.0 / dim, bias=epsc)
        ACT.activation(out=rstd[:, 0:n], in_=rstd[:, 0:n], func=AF.Exp, scale=-0.5)

    epsc = B.alloc([1], F32)
    DVE.memset(ap=epsc, constant=EPS)

    def rmsnorm_to(dst, gname, gi, blk):
        for (c0, n) in tiles_of(blk):
            rms_stats(h, 8, c0, n, D)
            for m in range(8):
                DVE.scalar_tensor_tensor(out=dst[:, m, c0:c0 + n], in0=h[:, m, c0:c0 + n], scalar=col(gname, gi * 8 + m),
                                         in1=rstd[:, 0:n], op0=ALU.mult, op1=ALU.mult)

    def final_out(blk):
        mk = B.mark()
        yv = B.alloc([8, 512], F32)
        ost = [B.alloc([D], F32) for _ in range(2)]
        cnt = 0
        for (c0, n) in tiles_of(blk):
            rms_stats(h, 8, c0, n, D)
            for m in range(8):
                DVE.scalar_tensor_tensor(out=yv[:, m, 0:n], in0=h[:, m, c0:c0 + n], scalar=col("g_final", m),
                                         in1=rstd[:, 0:n], op0=ALU.mult, op1=ALU.mult)
            ntc = (n + 127) // 128
            for tc in range(ntc):
                w = min(128, n - tc * 128)
                o = ost[cnt % 2]
                for half in range(2):
                    bk = B.bank("aux")
                    for q in range(4):
                        m = half * 4 + q
                        PE.transpose(out=bk[0:w, q * 128:(q + 1) * 128], in_=yv[:, m, tc * 128:tc * 128 + w],
                                     identity=ident)
                    ACT.copy(out=o[0:w, half * 512:(half + 1) * 512], in_=bk[0:w, :])
                if c0 < TP:
                    r0 = blk * TP + c0 + tc * 128
                    B.dma(f"ost{cnt % 2}", out=y_p[r0:r0 + w, :], in_=o[0:w, :])
                else:
                    B.dma(f"ost{cnt % 2}", out=y_s[:, :], in_=o[0:w, :])
                cnt += 1
        B.release(mk)

    def ffn(i, blk):
        tiles = tiles_of(blk)
        rmsnorm_to(u, "g_ffn", i, blk)
        mk = B.mark()
        hid = B.alloc([11, T], BF16)
        for half in range(2):
            f0 = half * 11
            for fa in range(0, 11, 1):
                nf = 1
                cs = slice((f0 + fa) * 128, (f0 + fa + nf) * 128)
                wgv, wuv = ws.get(f"ffn_gu{i}_{half}_{fa}", [(wg[i][:, cs], 8), (wu[i][:, cs], 8)])
                for q in range(nf):
                    f = fa + q
                    for (c0, n) in tiles:
                        bg = B.bank("mm")
                        bu = B.bank("mm")
                        for k in range(8):
                            PE.matmul(out=bg[:, 0:n], lhsT=wgv[:, k, q * 128:(q + 1) * 128], rhs=u[:, k, c0:c0 + n],
                                      start=(k == 0), stop=(k == 7))
                        for k in range(8):
                            PE.matmul(out=bu[:, 0:n], lhsT=wuv[:, k, q * 128:(q + 1) * 128], rhs=u[:, k, c0:c0 + n],
                                      start=(k == 0), stop=(k == 7))
                        sg = sgb[cnt[0] % 2]
                        cnt[0] += 1
                        ACT.activation(out=sg[:, 0:n], in_=bg[:, 0:n], func=AF.Silu)
                        DVE.tensor_tensor(out=hid[:, f, c0:c0 + n], in0=bu[:, 0:n], in1=sg[:, 0:n], op=ALU.mult)
            for mo2 in range(8):
                (wdv,) = ws.get(f"ffn_d{i}_{half}_{mo2}",
                                [(wd[i][f0 * 128:(f0 + 11) * 128, mo2 * 128:(mo2 + 1) * 128], 11)])
                for q in range(1):
                    mo = mo2 + q
                    for (c0, n) in tiles:
                        bk = B.bank("mm")
                        for k in range(11):
                            PE.matmul(out=bk[:, 0:n], lhsT=wdv[:, k, q * 128:(q + 1) * 128], rhs=hid[:, k, c0:c0 + n],
                                      start=(k == 0), stop=(k == 10))
                        DVE.tensor_tensor(out=h[:, mo, c0:c0 + n], in0=bk[:, 0:n], in1=h[:, mo, c0:c0 + n], op=ALU.add)
        B.release(mk)

    def ple(i, blk):
        tiles = tiles_of(blk)
        mk = B.mark()
        pT = B.alloc([2, T], BF16)
        for tc in range(8):
            st = xin[tc % 2]
            r0 = blk * TP + tc * 128
            B.dma(f"xin{tc % 2}", out=st[:, 0:PLE], in_=p_p[i, r0:r0 + 128, :])
            bk = B.bank("aux")
            for q in range(2):
                PE.transpose(out=bk[:, q * 128:(q + 1) * 128], in_=st[:, q * 128:(q + 1) * 128], identity=ident)
            ACT.copy(out=pT[:, 0:2, tc * 128:(tc + 1) * 128], in_=bk[:, 0:256].rearrange("p (q t) -> p q t", q=2))
        if blk == NBLK - 1:
            st = xin[0]
            B.dma("xin0", out=st[0:NS, 0:PLE], in_=p_s[i])
            bk = B.bank("aux")
            for q in range(2):
                PE.transpose(out=bk[:, q * NS:(q + 1) * NS], in_=st[0:NS, q * 128:(q + 1) * 128],
                             identity=ident[0:NS, 0:NS])
            ACT.copy(out=pT[:, 0:2, TP:TP + NS], in_=bk[:, 0:2 * NS].rearrange("p (q t) -> p q t", q=2))
        rmsnorm_to(u, "g_ple", i, blk)
        for mo2 in range(8):
            cs = slice(mo2 * 128, (mo2 + 1) * 128)
            wgv, wpv = ws.get(f"ple{i}_{mo2}", [(wpg[i][:, cs], 8), (wpp[i][:, cs], 2)])
            for q in range(1):
                mo = mo2 + q
                for (c0, n) in tiles:
                    bg = B.bank("mm")
                    bp = B.bank("mm")
                    for k in range(8):
                        PE.matmul(out=bg[:, 0:n], lhsT=wgv[:, k, q * 128:(q + 1) * 128], rhs=u[:, k, c0:c0 + n],
                                  start=(k == 0), stop=(k == 7))
                    for k in range(2):
                        PE.matmul(out=bp[:, 0:n], lhsT=wpv[:, k, q * 128:(q + 1) * 128], rhs=pT[:, k, c0:c0 + n],
                                  start=(k == 0), stop=(k == 1))
                    sg = sgb[cnt[0] % 2]
                    tm = tmpb[cnt[0] % 2]
                    cnt[0] += 1
                    ACT.activation(out=sg[:, 0:n], in_=bg[:, 0:n], func=AF.Sigmoid)
                    DVE.tensor_tensor(out=tm[:, 0:n], in0=bp[:, 0:n], in1=sg[:, 0:n], op=ALU.mult)
                    DVE.tensor_tensor(out=h[:, mo, c0:c0 + n], in0=tm[:, 0:n], in1=h[:, mo, c0:c0 + n], op=ALU.add)
        B.release(mk)

    def bc8(ap2, n):
        return ap2[:, 0:n].unsqueeze(1).to_broadcast([128, 8, n])

    def conformer(i, blk):
        j = i // 2
        tiles = tiles_of(blk)
        last = blk == NBLK - 1
        rmsnorm_to(u, "g_mix", i, blk)
        mk = B.mark()
        v = B.alloc([8, CW - 1 + T], BF16)
        c32 = B.alloc([8, T], F32)
        diag = [B.alloc([CW, 128], BF16)]
        cb = scr_bf
        mean = B.alloc([512], F32)
        msq = B.alloc([512], F32)
        v32 = B.alloc([8, CW - 1], F32)
        vs32 = B.alloc([8, NS], F32)
        if blk == 0:
            DVE.memset(ap=v[:, :, 0:CW - 1], constant=0.0)
        else:
            DVE.tensor_copy(out=v[:, :, 0:CW - 1], in_=vprefix[j])
        for mo in range(8):
            wa, wgt = ws.get(f"cv1_{i}_{mo}", [(w1[j][:, mo * 128:(mo + 1) * 128], 8),
                                              (w1[j][:, D + mo * 128:D + (mo + 1) * 128], 8)])
            for (c0, n) in tiles:
                ba = B.bank("mm")
                bg = B.bank("mm")
                for k in range(8):
                    PE.matmul(out=ba[:, 0:n], lhsT=wa[:, k, :], rhs=u[:, k, c0:c0 + n], start=(k == 0), stop=(k == 7))
                for k in range(8):
                    PE.matmul(out=bg[:, 0:n], lhsT=wgt[:, k, :], rhs=u[:, k, c0:c0 + n], start=(k == 0), stop=(k == 7))
                sg = sgb[cnt[0] % 2]
                cnt[0] += 1
                ACT.activation(out=sg[:, 0:n], in_=bg[:, 0:n], func=AF.Sigmoid, bias=col("cv_b_pw1", j * 16 + 8 + mo))
                ba_col = col("cv_b_pw1", j * 16 + mo)
                DVE.scalar_tensor_tensor(out=v[:, mo, CW - 1 + c0:CW - 1 + c0 + n], in0=ba[:, 0:n], scalar=ba_col,
                                         in1=sg[:, 0:n], op0=ALU.add, op1=ALU.mult)
                if last and c0 == 512:
                    DVE.scalar_tensor_tensor(out=v32[:, mo, :], in0=ba[:, 512 - (CW - 1):512], scalar=ba_col,
                                             in1=sg[:, 512 - (CW - 1):512], op0=ALU.add, op1=ALU.mult)
                if last and c0 == TP:
                    DVE.scalar_tensor_tensor(out=vs32[:, mo, :], in0=ba[:, 0:NS], scalar=ba_col,
                                             in1=sg[:, 0:NS], op0=ALU.add, op1=ALU.mult)
        if not last:
            DVE.tensor_copy(out=vprefix[j], in_=v[:, :, TP:TP + CW - 1])
        stt = None
        if last:
            stt = B.alloc([8, NS, CW - 1], BF16)
            for s4 in range(4):
                st = xin[s4 % 2]
                B.dma(f"xin{s4 % 2}", out=st[0:120, :], in_=conv_s[j, s4 * 4:(s4 + 1) * 4].rearrange("s k c -> (s k) c"))
                for half in range(2):
                    bk = B.bank("aux")
                    for q in range(4):
                        PE.transpose(out=bk[:, q * 120:(q + 1) * 120], in_=st[0:120, (half * 4 + q) * 128:(half * 4 + q + 1) * 128],
                                     identity=ident[0:120, 0:120])
                    ACT.copy(out=stt[:, half * 4:half * 4 + 4, s4 * 4:(s4 + 1) * 4, :].rearrange("p m s k -> p m (s k)"),
                             in_=bk[:, 0:480].rearrange("p (q t) -> p q t", q=4))
            B.dma("ncs", out=nc_s[j, :, 0:CW - 2, :], in_=conv_s[j, :, 1:CW - 1, :])
            for (src, wdt, dst) in ((v32, CW - 1, nc_p[j]), (vs32, NS, nc_s[j, :, CW - 2, :])):
                o = xin[0] if wdt == NS else xin[1]
                for half in range(2):
                    bk = B.bank("aux")
                    for q in range(4):
                        PE.transpose(out=bk[0:wdt, q * 128:(q + 1) * 128], in_=src[:, half * 4 + q, :], identity=ident)
                    ACT.copy(out=o[0:wdt, half * 512:(half + 1) * 512], in_=bk[0:wdt, :])
                B.dma("xin0" if wdt == NS else "xin1", out=dst, in_=o[0:wdt, :])
        for mo in range(8):
            dg = diag[0]
            for k in range(CW):
                DVE.tensor_scalar(out=dg[:, k, :], in0=identb, scalar1=col("cv_w_dw", (j * CW + k) * 8 + mo), scalar2=None,
                                  op0=ALU.mult)
            for (c0, n) in tiles:
                bk = B.bank("mm")
                if c0 < TP:
                    for k in range(CW):
                        PE.matmul(out=bk[:, 0:n], lhsT=dg[:, k, :], rhs=v[:, mo, c0 + k:c0 + k + n],
                                  start=(k == 0), stop=(k == CW - 1))
                else:
                    for k in range(CW - 1):
                        PE.matmul(out=bk[:, 0:NS], lhsT=dg[:, k, :], rhs=stt[:, mo, :, k], start=(k == 0), stop=False)
                    PE.matmul(out=bk[:, 0:NS], lhsT=dg[:, CW - 1, :], rhs=v[:, mo, CW - 1 + TP:CW - 1 + TP + NS],
                              start=False, stop=True)
                ACT.activation(out=c32[:, mo, c0:c0 + n], in_=bk[:, 0:n], func=AF.Identity, bias=col("cv_b_dw", j * 8 + mo))
        for (c0, n) in tiles:
            ACT.activation(out=sqb[:, :, 0:n], in_=c32[:, :, c0:c0 + n], func=AF.Square)
            DVE.tensor_copy(out=cb[:, :, 0:n], in_=c32[:, :, c0:c0 + n])
            bs = B.bank("aux")
            bq = B.bank("aux")
            for m in range(8):
                PE.matmul(out=bs[:, 0:n], lhsT=onesb, rhs=cb[:, m, 0:n], start=(m == 0), stop=(m == 7))
            for m in range(8):
                PE.matmul(out=bq[:, 0:n], lhsT=onesb, rhs=sqb[:, m, 0:n], start=(m == 0), stop=(m == 7))
            DVE.tensor_scalar(out=mean[:, 0:n], in0=bs[:, 0:n], scalar1=1.0 / D, scalar2=None, op0=ALU.mult)
            DVE.tensor_tensor(out=msq[:, 0:n], in0=mean[:, 0:n], in1=mean[:, 0:n], op=ALU.mult)
            DVE.scalar_tensor_tensor(out=msq[:, 0:n], in0=bq[:, 0:n], scalar=1.0 / D, in1=msq[:, 0:n],
                                     op0=ALU.mult, op1=ALU.subtract)
            ACT.activation(out=rstd[:, 0:n], in_=msq[:, 0:n], func=AF.Ln, bias=epsc)
            ACT.activation(out=rstd[:, 0:n], in_=rstd[:, 0:n], func=AF.Exp, scale=-0.5)
            DVE.tensor_tensor(out=c32[:, :, c0:c0 + n], in0=c32[:, :, c0:c0 + n], in1=bc8(mean, n), op=ALU.subtract)
            DVE.tensor_tensor(out=c32[:, :, c0:c0 + n], in0=c32[:, :, c0:c0 + n], in1=bc8(rstd, n), op=ALU.mult)
            for m in range(8):
                ACT.activation(out=u[:, m, c0:c0 + n], in_=c32[:, m, c0:c0 + n], func=AF.Silu,
                               scale=col("cv_ln_g", j * 8 + m), bias=col("cv_ln_b", j * 8 + m))
        for mo in range(8):
            (w2v,) = ws.get(f"cv2_{i}_{mo}", [(w2[j][:, mo * 128:(mo + 1) * 128], 8)])
            for (c0, n) in tiles:
                bk = B.bank("mm")
                for k in range(8):
                    PE.matmul(out=bk[:, 0:n], lhsT=w2v[:, k, :], rhs=u[:, k, c0:c0 + n], start=(k == 0), stop=(k == 7))
                DVE.scalar_tensor_tensor(out=h[:, mo, c0:c0 + n], in0=bk[:, 0:n], scalar=col("cv_b_pw2", j * 8 + mo),
                                         in1=h[:, mo, c0:c0 + n], op0=ALU.add, op1=ALU.add)
        B.release(mk)

    def bcast(ap2, shape, axis):
        return ap2.unsqueeze(axis).to_broadcast(shape)

    def mamba(i, blk):
        j = i // 2
        tiles = tiles_of(blk)
        last = blk == NBLK - 1
        Tb = TP + (NS if last else 0)
        rmsnorm_to(u, "g_mix", i, blk)
        mk = B.mark()
        B.S.region = True
        B.S.rlimit = opts.get("rlimit", 10 ** 9)
        B.pools = B.POOLS_SSD
        mask01 = B.alloc([128], F32)
        selhp = B.alloc([256], F32)
        ones4 = B.alloc([128], F32)
        dtT = B.alloc([T], F32)
        acT = B.alloc([T], F32)
        tmpT = B.alloc([T], F32)
        sz = B.alloc([2, T], F32)
        xpre = B.alloc([4, SCW + T], BF16)
        xs = B.alloc([2, 512], F32)
        y32 = B.alloc([2, 512], F32)
        B32 = B.alloc([512], F32)
        BT = B.alloc([512], BF16)
        CT = B.alloc([512], BF16)
        C32s = B.alloc([NS], F32)
        diag4 = B.alloc([4, SCW, 128], BF16)
        nac = B.alloc([8, 4], F32)
        dtm = B.alloc([8, 4], F32)
        dd = B.alloc([8, 4], F32)
        dl = B.alloc([4], F32)
        aneg = B.alloc([1], F32)
        ctmp = []
        for _ in range(2):
            ctmp.append((B.alloc([256], BF16), B.alloc([256], BF16), B.alloc([128], BF16), B.alloc([4, 128], F32),
                         B.alloc([4, 128], F32), B.alloc([4, 128], BF16), B.alloc([4, 128], BF16), B.alloc([4, 128], F32)))
        ccnt = [0]
        prevT = B.alloc([256], BF16)
        ygb = B.alloc([2, 512], BF16)
        xl32 = B.alloc([32, SCW - 1], F32)
        xs32s = B.alloc([32, NS], F32)
        stg = ctmp[1][7][:, 2:4, :]
        t1 = scr4[:, 0:2, :]
        if last:
            sst = B.alloc([4, NS, SCW - 1], BF16)
            ea = B.alloc([2, NS], F32)
            xdths = B.alloc([2, NS], F32)
            ysm = B.alloc([2, NS], F32)
            BCs = B.alloc([256], F32)
            S0 = [B.alloc([4, 2, 128], F32) for _ in range(2)]
        B.dma("xin0", out=xin[0][:, 0:128], in_=consts[:, 128:256])
        DVE.tensor_scalar(out=mask01, in0=xin[0][:, 0:128], scalar1=0.0, scalar2=None, op0=ALU.is_equal)
        B.dma("c1", out=selhp[0:4, :], in_=consts[0:4, 640:896])
        DVE.memset(ap=ones4, constant=1.0)
        if blk == 0:
            DVE.memset(ap=STs[j], constant=0.0)
        skip = opts.get("skip", ())
        if last and "nscs" not in skip:
            B.dma("nscs", out=nsc_s[j, :, 0:SCW - 2, :], in_=ssmc_s[j, :, 1:SCW - 1, :])
        s0cnt = 0
        for g in range(NG):
            colz = g * 256
            colx = DIN + g * 256
            colB = 2 * DIN + g * 128
            colC = 2 * DIN + 1024 + g * 128
            coldt = DIN + SCD + 4 * g
            chunk_ids = [2 * g, 2 * g + 1, 16 + g, 24 + g]
            STg = STs[j][:, g, :].rearrange("p (a b) -> p a b", a=4)
            if blk == 0:
                DVE.memset(ap=xpre[:, :, 0:SCW], constant=0.0)
            else:
                DVE.tensor_copy(out=xpre[:, :, 0:SCW], in_=xprefix[j][:, g * 4:(g + 1) * 4, :])
            (wz,) = ws.get(f"mz{i}_{g}", [(win[j][:, colz:colz + 256], 8)])
            for q in range(2):
                for (c0, n) in tiles:
                    bk = B.bank("mm")
                    for k in range(8):
                        PE.matmul(out=bk[:, 0:n], lhsT=wz[:, k, q * 128:(q + 1) * 128], rhs=u[:, k, c0:c0 + n],
                                  start=(k == 0), stop=(k == 7))
                    ACT.activation(out=sz[:, q, c0:c0 + n], in_=bk[:, 0:n], func=AF.Silu)

            def evac_pre(bk, qi, c0, n):
                ACT.copy(out=xpre[:, qi, SCW + c0:SCW + c0 + n], in_=bk[:, 0:n])
                if last and c0 == 512:
                    DVE.tensor_copy(out=xl32[:, chunk_ids[qi], :], in_=bk[:, 512 - (SCW - 1):512])
                if last and c0 == TP:
                    DVE.tensor_copy(out=xs32s[:, chunk_ids[qi], :], in_=bk[:, 0:NS])

            (wx,) = ws.get(f"mx{i}_{g}", [(win[j][:, colx:colx + 256], 8)])
            for q in range(2):
                for (c0, n) in tiles:
                    bk = B.bank("mm")
                    for k in range(8):
                        PE.matmul(out=bk[:, 0:n], lhsT=wx[:, k, q * 128:(q + 1) * 128], rhs=u[:, k, c0:c0 + n],
                                  start=(k == 0), stop=(k == 7))
                    evac_pre(bk, q, c0, n)
            wB, wC, wdt = ws.get(f"mbc{i}_{g}", [(win[j][:, colB:colB + 128], 8), (win[j][:, colC:colC + 128], 8),
                                                 (win[j][:, coldt:coldt + 4], 8)])
            for qi, wv in ((2, wB), (3, wC)):
                for (c0, n) in tiles:
                    bk = B.bank("mm")
                    for k in range(8):
                        PE.matmul(out=bk[:, 0:n], lhsT=wv[:, k, :], rhs=u[:, k, c0:c0 + n], start=(k == 0), stop=(k == 7))
                    evac_pre(bk, qi, c0, n)
            dtb = col("ssm_dt_bias_g", j * 8 + g)[0:4, :]
            for (c0, n) in (tiles if "dt" not in skip else ()):
                bk = B.bank("mm")
                for k in range(8):
                    PE.matmul(out=bk[0:4, 0:n], lhsT=wdt[:, k, :], rhs=u[:, k, c0:c0 + n], start=(k == 0), stop=(k == 7))
                ACT.activation(out=tmpT[0:4, c0:c0 + n], in_=bk[0:4, 0:n], func=AF.Exp, bias=dtb)
                ACT.activation(out=dtT[0:4, c0:c0 + n], in_=tmpT[0:4, c0:c0 + n], func=AF.Ln, bias=ones4[0:4, 0:1])
            if not last:
                DVE.tensor_copy(out=xprefix[j][:, g * 4:(g + 1) * 4, :], in_=xpre[:, :, TP:TP + SCW])
            mcut = opts.get("mcut", 9)
            if mcut <= -3:
                continue
            ACT.activation(out=aneg[0:4, :], in_=col("ssm_a_log_g", j * 8 + g)[0:4, :], func=AF.Exp)
            DVE.tensor_scalar(out=aneg[0:4, :], in0=aneg[0:4, :], scalar1=-1.0, scalar2=None, op0=ALU.mult)
            DVE.tensor_scalar(out=tmpT[0:4, 0:Tb], in0=dtT[0:4, 0:Tb], scalar1=aneg[0:4, 0:1], scalar2=None, op0=ALU.mult)
            for c in range(8):
                cs = slice(c * 128, (c + 1) * 128)
                DVE.tensor_tensor_scan(out=acT[0:4, cs], data0=ones4[0:4, 0:128], data1=tmpT[0:4, cs], initial=0.0,
                                       op0=ALU.mult, op1=ALU.add)
            for c in range(8):
                cs = slice(c * 128, (c + 1) * 128)
                bk = B.bank("aux")
                PE.transpose(out=bk[:, 0:4], in_=acT[0:4, cs], identity=ident[0:4, 0:4])
                PE.transpose(out=bk[:, 4:8], in_=dtT[0:4, cs], identity=ident[0:4, 0:4])
                DVE.tensor_scalar(out=dl[0:4, 0:4], in0=ident[0:4, 0:4], scalar1=acT[0:4, c * 128 + 127:c * 128 + 128],
                                  scalar2=None, op0=ALU.mult)
                PE.matmul(out=bk[:, 8:12], lhsT=ones4[0:4, 0:128], rhs=dl[0:4, 0:4], start=True, stop=True)
                DVE.tensor_scalar(out=nac[:, c, :], in0=bk[:, 0:4], scalar1=-1.0, scalar2=None, op0=ALU.mult)
                DVE.tensor_tensor(out=dd[:, c, :], in0=bk[:, 8:12], in1=nac[:, c, :], op=ALU.add)
                ACT.activation(out=dd[:, c, :], in_=dd[:, c, :], func=AF.Exp)
                DVE.tensor_copy(out=dtm[:, c, :], in_=bk[:, 4:8])
                DVE.tensor_tensor(out=dd[:, c, :], in0=dd[:, c, :], in1=dtm[:, c, :], op=ALU.mult)
            for qi, q32 in enumerate(chunk_ids):
                for k in range(SCW):
                    DVE.tensor_scalar(out=diag4[:, qi, k, :], in0=identb, scalar1=col("ssm_w_conv", (j * SCW + k) * 32 + q32),
                                      scalar2=None, op0=ALU.mult)
            if last:
                st = xin[1]
                for qi, q32 in enumerate(chunk_ids):
                    B.dma("xin1", out=st[0:NS * (SCW - 1), qi * 128:(qi + 1) * 128],
                          in_=ssmc_s[j, :, :, q32 * 128:(q32 + 1) * 128].rearrange("s k c -> (s k) c"))
                bk = B.bank("aux")
                nr = NS * (SCW - 1)
                for qi in range(4):
                    PE.transpose(out=bk[:, qi * nr:(qi + 1) * nr], in_=st[0:nr, qi * 128:(qi + 1) * 128],
                                 identity=ident[0:nr, 0:nr])
                ACT.copy(out=sst.rearrange("p q s k -> p q (s k)"), in_=bk[:, 0:4 * nr].rearrange("p (q t) -> p q t", q=4))
            (wo,) = ws.get(f"mo{i}_{g}", [(wout[j][g * 256:(g + 1) * 256, :], 2)])
            ACT.copy(out=prevT, in_=STs[j][:, g, :])
            for (c0, n) in tiles:
                for qi, q32 in enumerate(chunk_ids):
                    bias = col("ssm_b_conv", j * 32 + q32)
                    bk = B.bank("mm")
                    if c0 < TP:
                        for k in range(SCW):
                            PE.matmul(out=bk[:, 0:n], lhsT=diag4[:, qi, k, :], rhs=xpre[:, qi, c0 + k + 1:c0 + k + 1 + n],
                                      start=(k == 0), stop=(k == SCW - 1))
                    else:
                        for k in range(SCW - 1):
                            PE.matmul(out=bk[:, 0:NS], lhsT=diag4[:, qi, k, :], rhs=sst[:, qi, :, k], start=(k == 0), stop=False)
                        PE.matmul(out=bk[:, 0:NS], lhsT=diag4[:, qi, SCW - 1, :],
                                  rhs=xpre[:, qi, SCW + TP:SCW + TP + NS], start=False, stop=True)
                    if qi < 2:
                        ACT.activation(out=xs[:, qi, 0:n], in_=bk[:, 0:n], func=AF.Silu, bias=bias)
                    elif qi == 2:
                        ACT.activation(out=B32[:, 0:n], in_=bk[:, 0:n], func=AF.Silu, bias=bias)
                        DVE.tensor_copy(out=BT[:, 0:n], in_=B32[:, 0:n])
                    else:
                        ACT.activation(out=CT[:, 0:n], in_=bk[:, 0:n], func=AF.Silu, bias=bias)
                        if c0 == TP:
                            ACT.activation(out=C32s[:, 0:NS], in_=bk[:, 0:NS], func=AF.Silu, bias=bias)
                def stage_a(cl):
                    c = c0 // 128 + cl
                    cs = slice(c * 128, (c + 1) * 128)
                    ls = slice(cl * 128, (cl + 1) * 128)
                    pp = ccnt[0] % 2
                    ccnt[0] += 1
                    xdt, dxdt, Btm, Ebuf, LT, Mb, Cs, R4 = ctmp[pp]
                    POOL.tensor_tensor(out=R4[0:4], in0=bcast(acT[0:4, cs], [4, 4, 128], 1),
                                       in1=bcast(ident[0:4, 0:4], [4, 4, 128], 2), op=ALU.mult)
                    bkT = B.bank("aux")
                    for q in range(2):
                        PE.transpose(out=bkT[:, q * 128:(q + 1) * 128], in_=xs[:, q, ls], identity=ident)
                    PE.transpose(out=bkT[:, 256:384], in_=B32[:, ls], identity=ident)
                    xT4 = bkT[:, 0:256].rearrange("p (a b) -> p a b", a=4)
                    DVE.tensor_tensor(out=xdt.rearrange("p (a b) -> p a b", a=4), in0=xT4,
                                      in1=bcast(dtm[:, c, :], [128, 4, 64], 2), op=ALU.mult)
                    DVE.tensor_tensor(out=dxdt.rearrange("p (a b) -> p a b", a=4), in0=xT4,
                                      in1=bcast(dd[:, c, :], [128, 4, 64], 2), op=ALU.mult)
                    ACT.copy(out=Btm, in_=bkT[:, 256:384])
                    bkC = B.bank("ssdC")
                    PE.matmul(out=bkC[:, 0:128], lhsT=BT[:, ls], rhs=CT[:, ls], start=True, stop=True)
                    bkE = B.bank("ssdE")
                    PE.matmul(out=bkE[:, 0:512], lhsT=ones4[0:4, 0:128], rhs=R4[0:4].rearrange("p a b -> p (a b)"),
                              start=True, stop=True)
                    ACT.activation(out=Ebuf.rearrange("p a b -> p (a b)"), in_=bkE[:, 0:512], func=AF.Exp)
                    return (ls, ctmp[pp], bkC, bkE, c)

                def stage_a2(info):
                    ls, (xdt, dxdt, Btm, Ebuf, LT, Mb, Cs, R4), bkC, bkE, c = info
                    for hh in range(4):
                        DVE.tensor_scalar(out=LT[:, hh, :], in0=bkE[:, hh * 128:(hh + 1) * 128], scalar1=nac[:, c, hh:hh + 1],
                                          scalar2=0.0, op0=ALU.add, op1=ALU.min)
                    ACT.activation(out=LT.rearrange("p a b -> p (a b)"), in_=LT.rearrange("p a b -> p (a b)"), func=AF.Exp)

                def stage_b(info):
                    ls, (xdt, dxdt, Btm, Ebuf, LT, Mb, Cs, R4), bkC, bkE, c = info
                    DVE.tensor_tensor(out=bkC[:, 0:128], in0=bkC[:, 0:128], in1=mask01, op=ALU.mult)
                    DVE.tensor_tensor(out=Mb, in0=LT, in1=bcast(bkC[:, 0:128], [128, 4, 128], 1), op=ALU.mult)
                    POOL.tensor_tensor(out=Cs, in0=Ebuf, in1=bcast(CT[:, ls], [128, 4, 128], 1), op=ALU.mult)
                    for m2 in range(2):
                        bkY = B.bank("mm")
                        for h2 in range(2):
                            hh = m2 * 2 + h2
                            PE.matmul(out=bkY[h2 * 64:(h2 + 1) * 64, 0:128], lhsT=xdt[:, hh * 64:(hh + 1) * 64], rhs=Mb[:, hh, :],
                                      start=True, stop=False)
                            PE.matmul(out=bkY[h2 * 64:(h2 + 1) * 64, 0:128], lhsT=prevT[:, hh * 64:(hh + 1) * 64], rhs=Cs[:, hh, :],
                                      start=False, stop=True)
                        DVE.scalar_tensor_tensor(out=y32[:, m2, ls], in0=xs[:, m2, ls], scalar=col("ssm_d_rep", j * 16 + 2 * g + m2),
                                                 in1=bkY[:, 0:128], op0=ALU.mult, op1=ALU.add)
                    bkS = B.bank("mm")
                    PE.matmul(out=bkS[:, 0:256], lhsT=Btm, rhs=dxdt, start=True, stop=True)
                    for hh in range(4):
                        DVE.scalar_tensor_tensor(out=STg[:, hh, :], in0=STg[:, hh, :], scalar=Ebuf[:, hh, 127:128],
                                                 in1=bkS[:, hh * 64:(hh + 1) * 64], op0=ALU.mult, op1=ALU.add)
                    ACT.copy(out=prevT, in_=STs[j][:, g, :])

                nch = n // 128 if c0 < TP else 0
                pend = stage_a(0) if nch else None
                if nch:
                    stage_a2(pend)
                for cl in range(nch):
                    nxt = stage_a(cl + 1) if cl + 1 < nch else None
                    stage_b(pend)
                    if nxt is not None:
                        stage_a2(nxt)
                    pend = nxt
                if last and c0 == 512:
                    bk = B.bank("aux")
                    for m2 in range(2):
                        PE.transpose(out=bk[:, m2 * 128:(m2 + 1) * 128], in_=STs[j][:, g, m2 * 128:(m2 + 1) * 128], identity=ident)
                    ACT.copy(out=stg, in_=bk[:, 0:256].rearrange("p (a b) -> p a b", a=2))
                    B.dma("stg", out=ns_p[j, g * 256:(g + 1) * 256, :].rearrange("(m q) n -> q m n", m=2), in_=stg)
                if c0 == TP:
                    sc = slice(TP, TP + NS)
                    R4s = ctmp[0][7]
                    RB = ctmp[0][5]
                    RC = ctmp[0][6]
                    tBs = [R4s[:, 0, :], R4s[:, 2, :]]
                    junk = R4s[:, 1, :]
                    bk = B.bank("aux")
                    for m2 in range(2):
                        PE.matmul(out=bk[:, m2 * 32:m2 * 32 + NS], lhsT=selhp[0:4, m2 * 128:(m2 + 1) * 128], rhs=tmpT[0:4, sc],
                                  start=True, stop=True)
                        PE.matmul(out=bk[:, 64 + m2 * 32:64 + m2 * 32 + NS], lhsT=selhp[0:4, m2 * 128:(m2 + 1) * 128],
                                  rhs=dtT[0:4, sc], start=True, stop=True)
                    for m2 in range(2):
                        ACT.activation(out=ea[:, m2, :], in_=bk[:, m2 * 32:m2 * 32 + NS], func=AF.Exp)
                        DVE.tensor_tensor(out=xdths[:, m2, :], in0=bk[:, 64 + m2 * 32:64 + m2 * 32 + NS], in1=xs[:, m2, 0:NS], op=ALU.mult)
                    bk2 = B.bank("aux")
                    PE.transpose(out=bk2[0:NS, 0:128], in_=B32[:, 0:NS], identity=ident)
                    PE.transpose(out=bk2[0:NS, 128:256], in_=C32s[:, 0:NS], identity=ident)
                    ACT.copy(out=BCs[0:NS, :], in_=bk2[0:NS, 0:256])
                    def s_load(s4):
                        S = S0[s4 % 2]
                        for sl in range(4):
                            B.dma(f"s0_{s4 % 2}", out=S[:, sl],
                                  in_=ssm_s[j, s4 * 4 + sl, g * 256:(g + 1) * 256, :].rearrange("(m q) n -> q m n", m=2))

                    s_load(0)
                    s_load(1)
                    for s4 in range(4):
                        idn = bcast(ident[0:NS, s4 * 4:(s4 + 1) * 4], [NS, 4, 128], 2)
                        DVE.tensor_tensor(out=RB[0:NS], in0=bcast(BCs[0:NS, 0:128], [NS, 4, 128], 1), in1=idn, op=ALU.mult)
                        DVE.tensor_tensor(out=RC[0:NS], in0=bcast(BCs[0:NS, 128:256], [NS, 4, 128], 1), in1=idn, op=ALU.mult)
                        bkB = B.bank("ssd")
                        bkC2 = B.bank("ssd")
                        PE.matmul(out=bkB[:, 0:512], lhsT=onesb[0:NS, 0:128], rhs=RB[0:NS].rearrange("p a b -> p (a b)"),
                                  start=True, stop=True)
                        PE.matmul(out=bkC2[:, 0:512], lhsT=onesb[0:NS, 0:128], rhs=RC[0:NS].rearrange("p a b -> p (a b)"),
                                  start=True, stop=True)
                        S = S0[s4 % 2]
                        key = f"s0_{s4 % 2}"
                        for sl in range(4):
                            sidx = s4 * 4 + sl
                            for m2 in range(2):
                                tB = tBs[m2]
                                DVE.tensor_scalar(out=tB, in0=bkB[:, sl * 128:(sl + 1) * 128], scalar1=xdths[:, m2, sidx:sidx + 1],
                                                  scalar2=None, op0=ALU.mult)
                                DVE.scalar_tensor_tensor(out=S[:, sl, m2, :], in0=S[:, sl, m2, :], scalar=ea[:, m2, sidx:sidx + 1],
                                                         in1=tB, op0=ALU.mult, op1=ALU.add)
                                DVE.scalar_tensor_tensor(out=junk, in0=S[:, sl, m2, :], scalar=1.0, in1=bkC2[:, sl * 128:(sl + 1) * 128],
                                                         op0=ALU.mult, op1=ALU.mult, accum_out=ysm[:, m2, sidx:sidx + 1])
                        for sl in range(4):
                            B.dma(key, out=ns_s[j, s4 * 4 + sl, g * 256:(g + 1) * 256, :].rearrange("(m q) n -> q m n", m=2), in_=S[:, sl])
                        if s4 + 2 < 4:
                            s_load(s4 + 2)
                    for m2 in range(2):
                        DVE.scalar_tensor_tensor(out=y32[:, m2, 0:NS], in0=xs[:, m2, 0:NS], scalar=col("ssm_d_rep", j * 16 + 2 * g + m2),
                                                 in1=ysm[:, m2, :], op0=ALU.mult, op1=ALU.add)
                DVE.tensor_tensor(out=t1[:, :, 0:n], in0=y32[:, :, 0:n], in1=sz[:, :, c0:c0 + n], op=ALU.mult)
                ACT.activation(out=sqb[:, 0:2, 0:n], in_=t1[:, :, 0:n], func=AF.Square)
                bk = B.bank("aux")
                for m in range(2):
                    PE.matmul(out=bk[:, 0:n], lhsT=onesb, rhs=sqb[:, m, 0:n], start=(m == 0), stop=(m == 1))
                ACT.activation(out=rstd[:, 0:n], in_=bk[:, 0:n], func=AF.Ln, scale=1.0 / 256.0, bias=epsc)
                ACT.activation(out=rstd[:, 0:n], in_=rstd[:, 0:n], func=AF.Exp, scale=-0.5)
                for q in range(2):
                    DVE.scalar_tensor_tensor(out=ygb[:, q, 0:n], in0=t1[:, q, 0:n], scalar=col("ssm_norm_g", j * 16 + 2 * g + q),
                                             in1=rstd[:, 0:n], op0=ALU.mult, op1=ALU.mult)
                for mo in range(8):
                    bk = B.bank("mm")
                    for k in range(2):
                        PE.matmul(out=bk[:, 0:n], lhsT=wo[:, k, mo * 128:(mo + 1) * 128], rhs=ygb[:, k, 0:n],
                                  start=(k == 0), stop=(k == 1))
                    DVE.tensor_tensor(out=h[:, mo, c0:c0 + n], in0=bk[:, 0:n], in1=h[:, mo, c0:c0 + n], op=ALU.add)
        if last and opts.get("mcut", 9) >= 1:
            for (src, wdt, dst) in ((xl32, SCW - 1, nsc_p[j]), (xs32s, NS, nsc_s[j, :, SCW - 2, :])):
                for pc in range(4):
                    o = xin[pc % 2]
                    for half in range(2):
                        bk = B.bank("aux")
                        for q in range(4):
                            PE.transpose(out=bk[0:wdt, q * 128:(q + 1) * 128], in_=src[:, pc * 8 + half * 4 + q, :], identity=ident)
                        ACT.copy(out=o[0:wdt, half * 512:(half + 1) * 512], in_=bk[0:wdt, :])
                    B.dma(f"xin{pc % 2}", out=dst[:, pc * 1024:(pc + 1) * 1024], in_=o[0:wdt, :])
        B.S.region = False
        B.pools = B.POOLS_DENSE
        B.release(mk)

    stage = opts.get("stage", "full")

    def program():
        for blk in range(NBLK):
            load_x(blk)
            for i in range(nlayers):
                if stage in ("s3", "full") and i % 2 == 0:
                    conformer(i, blk)
                if stage in ("s4", "full") and i % 2 == 1:
                    mamba(i, blk)
                if stage in ("s2", "s3", "full"):
                    ffn(i, blk)
                if stage in ("s2", "s3", "full"):
                    ple(i, blk)
            final_out(blk)

    B.S.dry = True
    ws.planning = True
    program()
    B.S.dry = False
    ws.planning = False
    B.bank_rr = {}
    cnt[0] = 0
    program()

    B.S.finalize(nc, B.stack)
    B.stack.close()
    return B


def _small_params(inp):
    rows = []
    idx = {}

    def add(name, arr):
        a = np.ascontiguousarray(arr, dtype=np.float32).reshape(-1)
        pad = (-a.size) % 128
        if pad:
            a = np.concatenate([a, np.zeros(pad, np.float32)])
        idx[name] = sum(r.shape[0] for r in rows)
        rows.append(a.reshape(-1, 128))

    for nm in ("g_mix", "g_ffn", "g_ple", "g_final", "cv_b_pw1", "cv_b_dw", "cv_ln_g", "cv_ln_b", "cv_b_pw2", "cv_w_dw"):
        add(nm, inp[nm])
    for nm in ("ssm_b_conv", "ssm_w_conv", "ssm_norm_g"):
        add(nm, inp[nm])
    add("ssm_d_rep", np.repeat(np.asarray(inp["ssm_d"]), HD, axis=1))
    for nm in ("ssm_dt_bias", "ssm_a_log"):
        a = np.zeros((2, NG, 128), np.float32)
        a[:, :, 0:4] = np.asarray(inp[nm]).reshape(2, NG, 4)
        add(nm + "_g", a)
    return np.concatenate(rows, 0), idx


def _consts():
    c = np.zeros((128, 128 + 512 + 4096 + 1024 + 128), np.float32)
    c[:, 0:128] = np.eye(128, dtype=np.float32)
    j = np.arange(128)[:, None]
    i = np.arange(128)[None, :]
    neg = np.where(i < j, -30000.0, 0.0).astype(np.float32)
    c[:, 128:128 + 512] = np.tile(neg, (1, 4))
    sel = np.zeros((128, 2, 128), np.float32)
    for m2 in range(2):
        sel[2 * m2, m2, 0:64] = 1.0
        sel[2 * m2 + 1, m2, 64:128] = 1.0
    c[:, 640:640 + 256] = sel.reshape(128, 256)
    rm = np.ones((128, 1024), np.float32)
    rm[:, ::128] = 0.0
    c[:, 4736:4736 + 1024] = rm
    c[:, 5760:5888] = 1.0
    return c


_CACHE = {}


_NEED = {
    "s1": {"x_p", "x_s", "consts", "smallp"},
    "s2": {"x_p", "x_s", "consts", "smallp", "p_p", "p_s", "ffn_w_gate", "ffn_w_up", "ffn_w_down", "ple_w_proj", "ple_w_gate"},
    "s3": {"x_p", "x_s", "consts", "smallp", "p_p", "p_s", "ffn_w_gate", "ffn_w_up", "ffn_w_down", "ple_w_proj", "ple_w_gate",
           "cv_w_pw1", "cv_w_pw2", "conv_s"},
    "s4": {"x_p", "x_s", "consts", "smallp", "ssm_w_in", "ssm_w_out", "ssm_s", "ssmc_s"},
}


def kernel(**inp):
    opts = dict(inp.pop("_opts", {}) or {})
    if opts.get("stage", "full") in _NEED:
        opts["need"] = _NEED[opts.get("need_as", opts["stage"])]
    smallp, colidx = _small_params(inp)
    opts["ncolrows"] = smallp.shape[0]
    opts["colidx"] = colidx
    B = build(opts)
    global _LASTB
    _LASTB = B
    cst = _consts()
    in_maps = []
    for c in range(NCORES):
        m = {
            "x_p": np.ascontiguousarray(inp["x_prompt"][c]),
            "x_s": np.ascontiguousarray(inp["x_sample"][c * NS:(c + 1) * NS, 0, :]),
            "consts": cst,
            "smallp": smallp,
            "p_p": np.ascontiguousarray(inp["p_prompt"][:, c]),
            "p_s": np.ascontiguousarray(inp["p_sample"][:, c * NS:(c + 1) * NS, 0, :]),
        }
        m["conv_s"] = np.ascontiguousarray(inp["state_conv_mixer"][:, c * NS:(c + 1) * NS])
        m["ssm_s"] = np.ascontiguousarray(inp["state_ssm"][:, c * NS:(c + 1) * NS]).reshape(2, NS, NH * HD, DST)
        m["ssmc_s"] = np.ascontiguousarray(inp["state_ssm_conv"][:, c * NS:(c + 1) * NS])
        for wn in ("ffn_w_gate", "ffn_w_up", "ffn_w_down", "ple_w_proj", "ple_w_gate", "cv_w_pw1", "cv_w_pw2",
                   "ssm_w_in", "ssm_w_out"):
            m[wn] = inp[wn]
        in_maps.append({k: (v if tuple(v.shape) == B.in_shapes[k] else np.zeros(B.in_shapes[k], np.float32))
                        for k, v in m.items() if k in B.ins})
    ncr = opts.get("ncores", NCORES)
    res = run_bass_kernel_spmd(B.nc, in_maps[:ncr], core_ids=list(range(ncr)))
    R = list(res.results) + [res.results[0]] * (NCORES - ncr)
    y_prompt = np.stack([R[c]["y_p"] for c in range(NCORES)], 0)
    y_sample = np.concatenate([R[c]["y_s"] for c in range(NCORES)], 0)[:, None, :]
    nc_p = np.stack([R[c]["nc_p"] for c in range(NCORES)], 1)
    nc_s = np.concatenate([R[c]["nc_s"] for c in range(NCORES)], 1)
    ns_p = np.stack([R[c]["ns_p"] for c in range(NCORES)], 1).reshape(2, NCORES, NH, HD, DST)
    nsc_p = np.stack([R[c]["nsc_p"] for c in range(NCORES)], 1)
    ns_s = np.concatenate([R[c]["ns_s"] for c in range(NCORES)], 1).reshape(2, NCORES * NS, NH, HD, DST)
    nsc_s = np.concatenate([R[c]["nsc_s"] for c in range(NCORES)], 1)
    return (y_prompt, y_sample, nc_p, ns_p, nsc_p, nc_s, ns_s, nsc_s)
```
